# Optimizing a Trainium2 kernel written in Bass

```python
import jax
import jax.numpy as jnp
from jax import lax
import numpy as np

D_MODEL = 1024
BATCH = 32
SEQ = 2048
DEPTH = 1

HEAD_DIM = 64
FOX_HEADS = D_MODEL // (2 * HEAD_DIM)
NSA_HEADS = D_MODEL // (2 * HEAD_DIM)
NSA_KV_HEADS = max(1, NSA_HEADS // 4)
MIX_WIDTH = (FOX_HEADS + NSA_HEADS) * HEAD_DIM
CMP_LEN = 32
CMP_STRIDE = 16
CMP_HIDDEN = 2 * HEAD_DIM
SEL_BLOCK = 64
SEL_TOPK = 16
WINDOW = 512
Q_BLOCK = 128
SEL_CHUNK = 16
N_BRANCH = 3
MEM_LEN = 256
CROSS_HEADS = 4
CROSS_HEAD_DIM = D_MODEL // CROSS_HEADS
MLP_HIDDEN = 4 * D_MODEL
ROPE_THETA = 10000.0
RMS_EPS = 1e-6
NEG_BIG = -1e30
FORCE_SCORE = 1e4

FOX_QKV = FOX_HEADS * HEAD_DIM
NSA_Q = NSA_HEADS * HEAD_DIM
NSA_KV = NSA_KV_HEADS * HEAD_DIM
IN_SPLITS = (FOX_QKV, FOX_QKV, FOX_QKV, FOX_HEADS, NSA_Q, NSA_KV, NSA_KV, NSA_KV, NSA_KV, NSA_KV, NSA_KV, NSA_HEADS * N_BRANCH)
IN_COLS = sum(IN_SPLITS)

kernel_name = 'hybrid_fox_nsa_sandwich_layer'


def _rms_norm(x, g):
    xf = x.astype(jnp.float32)
    y = xf * lax.rsqrt(jnp.mean(xf * xf, axis=-1, keepdims=True) + RMS_EPS)
    return (y * g.astype(jnp.float32)).astype(x.dtype)


def _rope(x, pos):
    half = x.shape[-1] // 2
    inv = ROPE_THETA ** (-jnp.arange(half, dtype=jnp.float32) / half)
    ang = pos.astype(jnp.float32)[:, None] * inv[None, :]
    cos, sin = jnp.cos(ang), jnp.sin(ang)
    xf = x.astype(jnp.float32)
    x1, x2 = xf[..., :half], xf[..., half:]
    return jnp.concatenate([x1 * cos - x2 * sin, x2 * cos + x1 * sin], axis=-1).astype(x.dtype)


def _split_cols(a, sizes):
    out, lo = [], 0
    for s in sizes:
        out.append(a[..., lo:lo + s])
        lo += s
    return out


def _fox_attention(q, k, v, log_f):
    T = q.shape[2]
    scale = q.shape[-1] ** -0.5
    c = jnp.cumsum(log_f, axis=-1)
    outs = []
    for i in range(T // Q_BLOCK):
        lo, hi = i * Q_BLOCK, (i + 1) * Q_BLOCK
        s = jnp.einsum('bhqd,bhkd->bhqk', q[:, :, lo:hi], k[:, :, :hi]).astype(jnp.float32) * scale
        s = s + c[:, :, lo:hi, None] - c[:, :, None, :hi]
        causal = jnp.arange(lo, hi)[:, None] >= jnp.arange(hi)[None, :]
        s = jnp.where(causal, s, -jnp.inf)
        p = jax.nn.softmax(s, axis=-1).astype(v.dtype)
        outs.append(jnp.einsum('bhqk,bhkd->bhqd', p, v[:, :, :hi]))
    return jnp.concatenate(outs, axis=2)


def _nsa_attention(q, k_cmp, v_cmp, k_slc, v_slc, k_win, v_win, gate_logits,
                   w_ck1, w_ck2, w_cv1, w_cv2, pe_k, pe_v):
    B, T, H, dh = q.shape
    G = k_cmp.shape[1]
    R = H // G
    scale = dh ** -0.5
    pos = jnp.arange(T)
    qg = q.reshape(B, T, G, R, dh).transpose(0, 2, 3, 1, 4)

    n_cmp = (T - CMP_LEN) // CMP_STRIDE + 1
    starts = jnp.arange(n_cmp) * CMP_STRIDE
    blk_idx = starts[:, None] + jnp.arange(CMP_LEN)[None, :]

    def compress(a, pe, w1, w2):
        blk = a[:, :, blk_idx] + pe
        flat = blk.reshape(B, G, n_cmp, CMP_LEN * dh)
        return jax.nn.silu(flat @ w1) @ w2

    kc = compress(k_cmp, pe_k, w_ck1, w_ck2)
    vc = compress(v_cmp, pe_v, w_cv1, w_cv2)
    s_c = jnp.einsum('bgrtd,bgnd->bgrtn', qg, kc).astype(jnp.float32) * scale
    valid_c = (starts + CMP_LEN - 1)[None, :] <= pos[:, None]
    s_c = jnp.where(valid_c, s_c, NEG_BIG)
    p_c = jax.nn.softmax(s_c, axis=-1) * jnp.any(valid_c, axis=-1)[:, None].astype(jnp.float32)
    o_cmp = jnp.einsum('bgrtn,bgnd->bgrtd', p_c.astype(vc.dtype), vc)

    n_sel = T // SEL_BLOCK
    sel_lo = jnp.arange(n_sel) * SEL_BLOCK
    overlap = ((starts[:, None] < sel_lo[None, :] + SEL_BLOCK)
               & (starts[:, None] + CMP_LEN > sel_lo[None, :])).astype(jnp.float32)
    imp = jnp.einsum('bgrtn,nj->bgtj', p_c, overlap)
    cur = pos // SEL_BLOCK
    jb = jnp.arange(n_sel)
    is_cur = jb[None, :] == cur[:, None]
    is_fixed = (jb[None, :] == 0) | (jb[None, :] == cur[:, None] - 1)
    imp = jnp.where(is_cur, 2.0 * FORCE_SCORE, jnp.where(is_fixed, FORCE_SCORE, imp))
    imp = jnp.where(jb[None, :] <= cur[:, None], imp, -1.0)
    n_top = min(SEL_TOPK, n_sel)
    _, top_idx = lax.top_k(imp, n_top)

    q_rot = _rope(qg, pos)
    ks_blk = _rope(k_slc, pos).reshape(B, G, n_sel, SEL_BLOCK, dh)
    vs_blk = v_slc.reshape(B, G, n_sel, SEL_BLOCK, dh)
    n_ch = T // SEL_CHUNK
    q_ch = q_rot.reshape(B, G, R, n_ch, SEL_CHUNK, dh).transpose(3, 0, 1, 2, 4, 5)
    i_ch = top_idx.reshape(B, G, n_ch, SEL_CHUNK, n_top).transpose(2, 0, 1, 3, 4)
    t_ch = pos.reshape(n_ch, SEL_CHUNK)
    bi = jnp.arange(B)[:, None, None, None]
    gi = jnp.arange(G)[None, :, None, None]
    n_keys = n_top * SEL_BLOCK

    def sel_step(args):
        qc, ic, tc = args
        kg = ks_blk[bi, gi, ic].reshape(B, G, SEL_CHUNK, n_keys, dh)
        vg = vs_blk[bi, gi, ic].reshape(B, G, SEL_CHUNK, n_keys, dh)
        kpos = (ic[..., None] * SEL_BLOCK + jnp.arange(SEL_BLOCK)).reshape(B, G, SEL_CHUNK, n_keys)
        mask = kpos <= tc[:, None]
        s = jnp.einsum('bgrcd,bgckd->bgrck', qc, kg).astype(jnp.float32) * scale
        s = jnp.where(mask[:, :, None], s, -jnp.inf)
        p = jax.nn.softmax(s, axis=-1).astype(vg.dtype)
        return jnp.einsum('bgrck,bgckd->bgrcd', p, vg)

    o_slc = lax.map(sel_step, (q_ch, i_ch, t_ch))
    o_slc = o_slc.transpose(1, 2, 3, 0, 4, 5).reshape(B, G, R, T, dh)

    pad = ((0, 0), (0, 0), (WINDOW, 0), (0, 0))
    kw = jnp.pad(_rope(k_win, pos), pad)
    vw = jnp.pad(v_win, pad)
    n_qb = T // Q_BLOCK
    q_blk = q_rot.reshape(B, G, R, n_qb, Q_BLOCK, dh).transpose(3, 0, 1, 2, 4, 5)

    def win_step(args):
        qb, i = args
        lo = i * Q_BLOCK
        kb = lax.dynamic_slice_in_dim(kw, lo, Q_BLOCK + WINDOW, axis=2)
        vb = lax.dynamic_slice_in_dim(vw, lo, Q_BLOCK + WINDOW, axis=2)
        qpos = lo + jnp.arange(Q_BLOCK)
        kpos = lo - WINDOW + jnp.arange(Q_BLOCK + WINDOW)
        mask = ((kpos[None, :] <= qpos[:, None]) & (kpos[None, :] > qpos[:, None] - WINDOW)
                & (kpos[None, :] >= 0))
        s = jnp.einsum('bgrqd,bgkd->bgrqk', qb, kb).astype(jnp.float32) * scale
        s = jnp.where(mask, s, -jnp.inf)
        p = jax.nn.softmax(s, axis=-1).astype(vb.dtype)
        return jnp.einsum('bgrqk,bgkd->bgrqd', p, vb)

    o_win = lax.map(win_step, (q_blk, jnp.arange(n_qb)))
    o_win = o_win.transpose(1, 2, 3, 0, 4, 5).reshape(B, G, R, T, dh)

    gts = jax.nn.sigmoid(gate_logits.astype(jnp.float32)).astype(q.dtype)
    gts = gts.reshape(B, T, G, R, N_BRANCH).transpose(0, 2, 3, 1, 4)[..., None]
    o = gts[..., 0, :] * o_cmp + gts[..., 1, :] * o_slc + gts[..., 2, :] * o_win
    return o.transpose(0, 3, 1, 2, 4).reshape(B, T, H * dh)


def _hybrid_mixer(n, w_in, b_forget, w_ck1, w_ck2, w_cv1, w_cv2, pe_k, pe_v, w_out):
    B, T, _ = n.shape
    proj = n @ w_in
    fq, fk, fv, ff, nq, kc, vc, ks, vs, kw, vw, ng = _split_cols(proj, IN_SPLITS)

    def heads(a, h):
        return a.reshape(B, T, h, HEAD_DIM).transpose(0, 2, 1, 3)

    log_f = jax.nn.log_sigmoid((ff + b_forget).astype(jnp.float32)).transpose(0, 2, 1)
    o_fox = _fox_attention(heads(fq, FOX_HEADS), heads(fk, FOX_HEADS), heads(fv, FOX_HEADS), log_f)
    o_fox = o_fox.transpose(0, 2, 1, 3).reshape(B, T, FOX_QKV)
    o_nsa = _nsa_attention(nq.reshape(B, T, NSA_HEADS, HEAD_DIM),
                           heads(kc, NSA_KV_HEADS), heads(vc, NSA_KV_HEADS),
                           heads(ks, NSA_KV_HEADS), heads(vs, NSA_KV_HEADS),
                           heads(kw, NSA_KV_HEADS), heads(vw, NSA_KV_HEADS),
                           ng.reshape(B, T, NSA_HEADS, N_BRANCH),
                           w_ck1, w_ck2, w_cv1, w_cv2, pe_k, pe_v)
    return jnp.concatenate([o_fox, o_nsa], axis=-1) @ w_out


def _memory_cross_attention(n, m, w_q, w_kv, w_o):
    B, T, D = n.shape
    M = m.shape[1]
    q = (n @ w_q).reshape(B, T, CROSS_HEADS, CROSS_HEAD_DIM)
    kv = (m @ w_kv).reshape(B, M, 2, CROSS_HEADS, CROSS_HEAD_DIM)
    k, v = kv[:, :, 0], kv[:, :, 1]
    s = jnp.einsum('bthd,bmhd->bhtm', q, k).astype(jnp.float32) * (CROSS_HEAD_DIM ** -0.5)
    p = jax.nn.softmax(s, axis=-1).astype(v.dtype)
    o = jnp.einsum('bhtm,bmhd->bthd', p, v).reshape(B, T, D)
    return o @ w_o


def _sq_relu_mlp(n, w_up, w_down):
    return jnp.square(jax.nn.relu(n @ w_up)) @ w_down


def setup_inputs(seed: int = 0) -> dict:
    key = jax.random.key(seed)
    ks = jax.random.split(key, 23)
    f32 = jnp.float32
    L = DEPTH

    def dense(k, shape, fan_in):
        return jax.random.normal(k, shape, f32) * fan_in ** -0.5

    def gain(k, dim):
        return 1.0 + 0.05 * jax.random.normal(k, (L, dim), f32)

    return {
        'x': jax.random.normal(ks[0], (BATCH, SEQ, D_MODEL), f32),
        'mem': jax.random.normal(ks[1], (BATCH, MEM_LEN, D_MODEL), f32),
        'g_mix_pre': gain(ks[2], D_MODEL),
        'w_in': dense(ks[3], (L, D_MODEL, IN_COLS), D_MODEL),
        'b_forget': jax.random.uniform(ks[4], (L, FOX_HEADS), f32, 1.0, 5.0),
        'w_ck1': dense(ks[5], (L, CMP_LEN * HEAD_DIM, CMP_HIDDEN), CMP_LEN * HEAD_DIM),
        'w_ck2': dense(ks[6], (L, CMP_HIDDEN, HEAD_DIM), CMP_HIDDEN),
        'w_cv1': dense(ks[7], (L, CMP_LEN * HEAD_DIM, CMP_HIDDEN), CMP_LEN * HEAD_DIM),
        'w_cv2': dense(ks[8], (L, CMP_HIDDEN, HEAD_DIM), CMP_HIDDEN),
        'pe_k': 0.1 * jax.random.normal(ks[9], (L, CMP_LEN, HEAD_DIM), f32),
        'pe_v': 0.1 * jax.random.normal(ks[10], (L, CMP_LEN, HEAD_DIM), f32),
        'w_mix_out': dense(ks[11], (L, MIX_WIDTH, D_MODEL), MIX_WIDTH),
        'g_mix_post': gain(ks[12], D_MODEL),
        'g_x_pre': gain(ks[13], D_MODEL),
        'g_mem': gain(ks[14], D_MODEL),
        'w_xq': dense(ks[15], (L, D_MODEL, D_MODEL), D_MODEL),
        'w_xkv': dense(ks[16], (L, D_MODEL, 2 * D_MODEL), D_MODEL),
        'w_xo': dense(ks[17], (L, D_MODEL, D_MODEL), D_MODEL),
        'g_x_post': gain(ks[18], D_MODEL),
        'g_mlp_pre': gain(ks[19], D_MODEL),
        'w_up': dense(ks[20], (L, D_MODEL, MLP_HIDDEN), D_MODEL),
        'w_down': dense(ks[21], (L, MLP_HIDDEN, D_MODEL), MLP_HIDDEN),
        'g_mlp_post': gain(ks[22], D_MODEL),
    }


def reference(x, mem, g_mix_pre, w_in, b_forget, w_ck1, w_ck2, w_cv1, w_cv2, pe_k, pe_v,
              w_mix_out, g_mix_post, g_x_pre, g_mem, w_xq, w_xkv, w_xo, g_x_post,
              g_mlp_pre, w_up, w_down, g_mlp_post):
    h = x
    for l in range(DEPTH):
        n = _rms_norm(h, g_mix_pre[l])
        mix = _hybrid_mixer(n, w_in[l], b_forget[l], w_ck1[l], w_ck2[l], w_cv1[l], w_cv2[l],
                            pe_k[l], pe_v[l], w_mix_out[l])
        h = h + _rms_norm(mix, g_mix_post[l])
        n = _rms_norm(h, g_x_pre[l])
        m = _rms_norm(mem, g_mem[l])
        h = h + _rms_norm(_memory_cross_attention(n, m, w_xq[l], w_xkv[l], w_xo[l]), g_x_post[l])
        n = _rms_norm(h, g_mlp_pre[l])
        h = h + _rms_norm(_sq_relu_mlp(n, w_up[l], w_down[l]), g_mlp_post[l])
    return h
```

```python
import contextlib
import numpy as np
import concourse.bass as bass
import concourse.mybir as mybir
from concourse.bass_utils import run_bass_kernel_spmd

F32 = mybir.dt.float32
BF16 = mybir.dt.bfloat16
AF = mybir.ActivationFunctionType
ALU = mybir.AluOpType

ENGS = ("pe", "act", "dve", "pool", "sp")
T = 2048
D = 1024
NT = 16
EPS = 1e-6
NEGB = -240000.0
NCOLS = 64 + 1536 + 2048
import os as _os
DSTOP = int(_os.environ.get("DSTOP", "99"))
FINAL_ENG = _os.environ.get("FINAL_ENG", "dve")


class Op:
    __slots__ = ("eng", "fn", "reads", "writes", "dsem", "waits", "inc", "epoch")

    def __init__(self, eng, fn, reads, writes, dsem, epoch):
        self.eng = eng
        self.fn = fn
        self.reads = tuple(reads)
        self.writes = tuple(writes)
        self.dsem = dsem
        self.waits = {}
        self.inc = None
        self.epoch = epoch


class Prog:
    def __init__(self, nc):
        self.nc = nc
        self.ops = []
        self.epoch = 0
        self.allkeys = set()
        self.bar_keys = ()

    def op(self, eng, fn, reads=(), writes=(), dsem=None):
        assert eng in ENGS
        reads = tuple(reads) + self.bar_keys
        self.allkeys.update(reads)
        self.allkeys.update(writes)
        self.ops.append(Op(eng, fn, reads, writes, dsem, self.epoch))

    def barrier(self, fns):
        keys = tuple(sorted(self.allkeys, key=str))
        n = len([o for o in self.ops if o.fn is not None])
        newbar = tuple("BAR_%s_%d" % (e, n) for e in ("pe", "act", "dve", "pool"))
        for e, k in zip(("pe", "act", "dve", "pool"), newbar):
            self.ops.append(Op(e, fns[e], keys, (k,), None, self.epoch))
        self.allkeys = set(newbar)
        self.bar_keys = newbar

    def finish(self):
        nc = self.nc
        ops = self.ops
        last_w = {}
        readers = {}
        deps_of = []
        needed = set()
        for i, op in enumerate(ops):
            deps = set()
            is_dma = op.dsem is not None
            for k in op.reads:
                j = last_w.get(k)
                if j is not None:
                    deps.add(j)
                if isinstance(k, str) and k.startswith("ps"):
                    for j in readers.get(k, ()):
                        if ops[j].eng != op.eng:
                            deps.add(j)
            for k in op.writes:
                j = last_w.get(k)
                if j is not None:
                    oj = ops[j]
                    if is_dma or oj.dsem is not None or oj.eng != op.eng or op.eng != "pe":
                        deps.add(j)
                for j in readers.get(k, ()):
                    oj = ops[j]
                    if is_dma or oj.dsem is not None or oj.eng != op.eng or op.eng != "pe":
                        deps.add(j)
            deps.discard(i)
            for k in op.reads:
                readers.setdefault(k, []).append(i)
            for k in op.writes:
                last_w[k] = i
                readers[k] = []
            deps_of.append(deps)
            needed |= deps
        cnt = {}
        token = {}
        sem_names = set()
        for i, op in enumerate(ops):
            if op.dsem is not None:
                cnt[op.dsem] = cnt.get(op.dsem, 0) + 16
                token[i] = (op.dsem, cnt[op.dsem])
                op.inc = (op.dsem, 16)
                sem_names.add(op.dsem)
            elif i in needed:
                assert op.fn is not None
                s = "c_%s_%d" % (op.eng, op.epoch)
                cnt[s] = cnt.get(s, 0) + 1
                token[i] = (s, cnt[s])
                op.inc = (s, 1)
                sem_names.add(s)
        waited = {e: {} for e in ENGS}
        nwaits = 0
        for i, op in enumerate(ops):
            w = {}
            for j in deps_of[i]:
                s, v = token[j]
                if v > w.get(s, 0):
                    w[s] = v
            wd = waited[op.eng]
            for s, v in list(w.items()):
                if wd.get(s, 0) >= v:
                    del w[s]
                else:
                    wd[s] = v
            op.waits = w
            nwaits += len(w)
        self.stats = dict(n_ops=len(ops), n_waits=nwaits, n_sems=len(sem_names),
                          maxcnt=max(cnt.values()) if cnt else 0)
        sems = {}
        with contextlib.ExitStack() as st:
            for s in sorted(sem_names):
                sems[s] = st.enter_context(nc.semaphore(s))
            block = st.enter_context(nc.Block())
            per = {e: [o for o in ops if o.eng == e] for e in ENGS}

            def run(eng, lst):
                for o in lst:
                    for s, v in o.waits.items():
                        eng.wait_ge(sems[s], v)
                    if o.fn is None:
                        continue
                    ins = o.fn(eng)
                    if o.inc is not None:
                        ins.then_inc(sems[o.inc[0]], o.inc[1])

            @block.tensor
            def _(e):
                run(e, per["pe"])

            @block.scalar
            def _(e):
                run(e, per["act"])

            @block.vector
            def _(e):
                run(e, per["dve"])

            @block.gpsimd
            def _(e):
                run(e, per["pool"])

            @block.sync
            def _(e):
                run(e, per["sp"])


def _bf(a):
    import ml_dtypes
    return np.asarray(a, np.float32).astype(ml_dtypes.bfloat16)


def host_consts():
    c = {}
    p = np.arange(128)
    c["ident_f"] = np.eye(128, dtype=np.float32)
    c["tri"] = (p[:, None] <= p[None, :]).astype(np.float32)
    c["atri"] = (p[:, None] > p[None, :]).astype(np.float32)
    half = 32
    inv = (10000.0 ** (-np.arange(half, dtype=np.float32) / half)).astype(np.float32)
    pos = np.arange(T, dtype=np.float32)
    ang = (pos[:, None] * inv[None, :]).astype(np.float32)
    cos = np.cos(ang).astype(np.float32).T
    sin = np.sin(ang).astype(np.float32).T
    cs = np.zeros((2, 128, T), np.float32)
    for base in (0, 64):
        cs[0, base:base + 32] = cos
        cs[0, base + 32:base + 64] = cos
        cs[1, base:base + 32] = -sin
        cs[1, base + 32:base + 64] = sin
    c["cossin"] = cs
    c["blkind"] = (np.arange(T)[None, :] // 64 == np.arange(32)[:, None]).astype(np.float32)
    n = np.arange(128)
    valid = ((16 * n[:, None] + 31) <= np.arange(T)[None, :]) & (n[:, None] < 127)
    c["validT"] = valid.astype(np.float32)
    starts = np.arange(127) * 16
    sel_lo = np.arange(32) * 64
    ov = ((starts[:, None] < sel_lo[None, :] + 64) & (starts[:, None] + 32 > sel_lo[None, :])).astype(np.float32)
    ovz = np.zeros((128, 40), np.float32)
    ovz[:127, :32] = ov
    ovz[:127, 32] = 1.0
    c["ovz"] = ovz
    t = np.arange(T)
    cur = t // 64
    j = np.arange(32)
    keep = ((j[None, :] < cur[:, None] - 1) & (j[None, :] != 0)).astype(np.float32)
    forced = np.zeros((T, 32), np.float32)
    forced[(j[None, :] == 0) | (j[None, :] == cur[:, None] - 1)] = 1e4
    forced[j[None, :] == cur[:, None]] = 2e4
    forced[j[None, :] > cur[:, None]] = -1.0
    c["keep"] = keep.reshape(NT, 128, 32).transpose(1, 0, 2).copy()
    c["forced"] = forced.reshape(NT, 128, 32).transpose(1, 0, 2).copy()
    sel = np.zeros((128, 24, 64), np.float32)
    for r in range(24):
        sel[r, r, :] = 1.0
        sel[32 + r, r, :] = 1.0
    c["sel"] = sel
    return c


def w_in_index():
    idx = []
    idx += list(range(1536, 1544)) + [-1] * 24 + list(range(2824, 2848)) + [-1] * 8
    for hp in range(4):
        idx += list(range(128 * hp, 128 * hp + 128))
        idx += list(range(512 + 128 * hp, 512 + 128 * hp + 128))
        idx += list(range(1024 + 128 * hp, 1024 + 128 * hp + 128))

    def sw(l):
        return l[32:] + l[:32]

    for g in range(2):
        qa, qb = [], []
        for r in range(4):
            h = 4 * g + r
            cols = list(range(1544 + 64 * h, 1544 + 64 * h + 64))
            qa += cols
            qb += sw(cols)
        idx += qa + qb
        kc = list(range(2056 + 64 * g, 2056 + 64 * g + 64))
        vc = list(range(2184 + 64 * g, 2184 + 64 * g + 64))
        ks = list(range(2312 + 64 * g, 2312 + 64 * g + 64))
        vs = list(range(2440 + 64 * g, 2440 + 64 * g + 64))
        kw = list(range(2568 + 64 * g, 2568 + 64 * g + 64))
        vw = list(range(2696 + 64 * g, 2696 + 64 * g + 64))
        idx += kc + vc + ks + kw + sw(ks) + sw(kw) + vs + vw
    assert len(idx) == NCOLS
    return np.array(idx)


class SB:
    def __init__(self, nc, limit):
        self.nc = nc
        self.limit = limit
        self.top = 16512
        self.n = 0
        self.cache = {}

    def alloc(self, name, shape, dt):
        nbytes = int(np.prod(shape[1:])) * (4 if dt == F32 else 2)
        nbytes = (nbytes + 31) // 32 * 32
        off = self.top
        self.top += nbytes
        assert self.top <= self.limit, (name, self.top, self.limit)
        key = (name, off, tuple(shape), str(dt))
        if key not in self.cache:
            self.n += 1
            self.cache[key] = self.nc.alloc_sbuf_tensor_at("%s_%d" % (name, self.n), list(shape), dt, offset=off)
        return self.cache[key]

    def mark(self):
        return self.top

    def reset(self, m):
        self.top = m


def build(nseq, upto="E", dbg=False, only_br=None):
    nc = bass.Bass("TRN2", target_bir_lowering=False)
    P = Prog(nc)

    def din(name, shape, dt=F32):
        return nc.dram_tensor(name, list(shape), dt, kind="ExternalInput").ap()

    x_d = din("x", [nseq, T, D])
    mem_d = din("mem", [nseq, 256, D])
    w_d = {
        "in": din("w_in_ext", [D, NCOLS]),
        "out": din("w_out", [D, D]),
        "xq": din("w_xq", [D, D]),
        "xkv": din("w_xkv", [D, 2 * D]),
        "xo": din("w_xo", [D, D]),
        "up": din("w_up", [D, 4 * D]),
        "down": din("w_down", [4 * D, D]),
        "c1": din("w_c1", [4096, 128]),
    }
    wc2_d = din("w_c2", [128, 128])
    peT_d = din("peT", [128, 32])
    gcols_d = din("gcols", [128, 32])
    gbc_d = din("gbc", [3, 128, D])
    gpre_d = din("gpre", [2, 128, D])
    bf_d = din("b_forget", [8, 1])
    cst = host_consts()
    cd = {k: din("c_" + k, v.shape) for k, v in cst.items()}
    y_d = nc.dram_tensor("y", [nseq, T, D], F32, kind="ExternalOutput").ap()
    dbg_d = {}
    if dbg:
        dbg_d["mixT"] = nc.dram_tensor("dbg_mixT", [128, 8, T], BF16, kind="ExternalOutput").ap()
        dbg_d["nT"] = nc.dram_tensor("dbg_nT", [128, 8, T], BF16, kind="ExternalOutput").ap()
        dbg_d["cpos"] = nc.dram_tensor("dbg_cpos", [8, T], F32, kind="ExternalOutput").ap()
        dbg_d["mb"] = nc.dram_tensor("dbg_mb", [2, 4, 32, 512], BF16, kind="ExternalOutput").ap()
    wbf = {k: nc.dram_tensor("wbf_" + k, list(v.shape), BF16, kind="Internal").ap() for k, v in w_d.items()}

    ps = [nc.alloc_psum_tensor("ps%d" % i, [128, 512], F32) for i in range(8)]
    psk = ["ps%d" % i for i in range(8)]
    rr = {}

    def bank(pool):
        lst = {"mm": (0, 1, 2, 3), "acc": (4, 5), "aux": (6, 7), "mm4": (0, 1, 2, 3, 4, 5, 6, 7)}[pool]
        i = rr.get(pool, 0)
        rr[pool] = i + 1
        return lst[i % len(lst)]

    sb = SB(nc, 229376)
    ident_bf = sb.alloc("ident_bf", [128, 128], BF16)
    ident_f = sb.alloc("ident_f", [128, 128], F32)
    tri = sb.alloc("tri", [128, 128], BF16)
    atri = sb.alloc("atri", [128, 128], BF16)
    ones_bf = sb.alloc("ones_bf", [128, 128], BF16)
    gcols = sb.alloc("gcols", [128, 32], F32)
    negb = sb.alloc("negb", [8, 1], F32)
    bfs = sb.alloc("bfs", [8, 1], F32)
    hb = sb.alloc("hb", [128, 2], F32)
    nhb = sb.alloc("nhb", [128, 2], F32)
    scr = sb.alloc("scr", [128, 8], F32)
    eps_t = sb.alloc("eps_t", [128, 1], F32)
    tiny_t = sb.alloc("tiny_t", [128, 1], F32)
    wslots = [sb.alloc("wslot%d" % i, [128, 8, 512], BF16) for i in range(4)]
    wsk = ["wslot%d" % i for i in range(4)]
    wrr = [0]
    mixT = sb.alloc("mixT", [128, 8, T], BF16)
    xbuf_off = []
    xbuf = []
    for i in range(2):
        xbuf_off.append(sb.top)
        xbuf.append(sb.alloc("xbuf%d" % i, [128, D], F32))
    xn = [sb.alloc("xn%d" % i, [128, D], BF16) for i in range(2)]
    junk = sb.alloc("junk", [128, D], BF16)
    ssb = [sb.alloc("ss%d" % i, [128, 2], F32) for i in range(2)]
    P_END = sb.mark()

    def cload(dst_ap, src_ap, key, eng="sp"):
        P.op(eng, lambda e: e.dma_start(out=dst_ap, in_=src_ap), writes=[key], dsem="d_" + key)

    def bar():
        P.barrier({
            "pe": lambda e: e.matmul(ps[7][0:1, 0:1], lhsT=ones_bf[0:1, 0:1], rhs=ones_bf[0:1, 0:1], start=True, stop=True),
            "act": lambda e: e.activation(out=scr[0:1, 0:1], in_=scr[0:1, 1:2], func=AF.Copy),
            "dve": lambda e: e.memset(scr[0:1, 2:3], 0.0),
            "pool": lambda e: e.memset(scr[0:1, 3:4], 0.0),
        })

    P.op("dve", lambda e: e.memset(scr[:], 0.0), writes=["scr"])
    P.op("pool", lambda e: e.memset(ones_bf[:], 1.0), writes=["ones_bf"])
    P.op("pool", lambda e: e.memset(eps_t[:], EPS), writes=["eps_t"])
    P.op("pool", lambda e: e.memset(tiny_t[:], 1e-18), writes=["tiny_t"])
    cload(ident_f[:], cd["ident_f"], "ident_f")
    cload(ident_bf[:], cd["ident_f"], "ident_bf", "pool")
    cload(tri[:], cd["tri"], "tri", "pool")
    cload(atri[:], cd["atri"], "atri", "pool")
    cload(gcols[:], gcols_d, "gcols")
    cload(bfs[:], bf_d, "bfs")
    P.op("dve", lambda e: e.tensor_scalar_mul(out=negb[:], in0=bfs[:], scalar1=-1.0), reads=["bfs"], writes=["negb"])
    for k in ("c1", "in", "out", "xq", "xkv", "xo", "up", "down"):
        src = w_d[k]
        dst = wbf[k]
        rows, cols = src.shape
        if cols > 1024:
            nsp = (cols + 1023) // 1024
            step = cols // nsp
            assert step * nsp == cols
            for i in range(nsp):
                def f(e, s=src[:, i * step:(i + 1) * step], d=dst[:, i * step:(i + 1) * step]):
                    return e.dma_start(out=d, in_=s)
                P.op("pool", f, writes=["wbf_%s_%d" % (k, i)], dsem="d_wbf_%s_%d" % (k, i))
        else:
            def f(e, s=src, d=dst):
                return e.dma_start(out=d, in_=s)
            P.op("pool", f, writes=["wbf_%s_0" % k], dsem="d_wbf_%s_0" % k)

    def wkeys(k, c0=None, ncol=None):
        cols = w_d[k].shape[1]
        n = (cols + 1023) // 1024 if cols > 1024 else 1
        if c0 is None or n == 1:
            return ["wbf_%s_%d" % (k, i) for i in range(n)]
        step = cols // n
        return ["wbf_%s_%d" % (k, i) for i in range(n) if i * step < c0 + ncol and (i + 1) * step > c0]

    def wview(k):
        return wbf[k].rearrange("(kc p) c -> p kc c", p=128)

    def wload(k, kc0, nkc, c0, ncol, nslots=1):
        assert nkc == 8 and ncol in (512, 1024, 384, 64)
        ns = 2 if ncol == 1024 else 1
        i0 = wrr[0] % 4
        if ns == 2 and i0 % 2 == 1:
            wrr[0] += 1
            i0 = wrr[0] % 4
        wrr[0] += ns
        keys = [wsk[i0 + i] for i in range(ns)]
        src = wview(k)[:, kc0:kc0 + nkc, c0:c0 + ncol]
        aps = []
        for i in range(ns):
            w = min(512, ncol)
            dstt = wslots[i0 + i][:, :, 0:w]
            srci = src[:, :, i * 512:i * 512 + w]

            def f(e, d=dstt, s=srci):
                return e.dma_start(out=d, in_=s)
            P.op("sp", f, reads=wkeys(k, c0 + i * 512, w), writes=[keys[i]], dsem="d_" + keys[i])
            aps.append(dstt)
        return aps, keys

    LA = 3

    def pipeline(steps):
        n = len(steps)
        for i in range(n + LA):
            if i < n:
                steps[i][0]()
            if i >= LA:
                steps[i - LA][1]()

    def mm(out, lhsT, rhs, start, stop, reads, writes):
        P.op("pe", lambda e: e.matmul(out, lhsT=lhsT, rhs=rhs, start=start, stop=stop), reads, writes)

    def act(out, in_, func, reads, writes, bias=None, scale=None, accum=None):
        kw = {}
        if bias is not None:
            kw["bias"] = bias
        if scale is not None:
            kw["scale"] = scale
        if accum is not None:
            kw["accum_out"] = accum
        P.op("act", lambda e: e.activation(out=out, in_=in_, func=func, **kw), reads, writes)

    def tt(eng, out, in0, in1, op, reads, writes):
        P.op(eng, lambda e: e.tensor_tensor(out=out, in0=in0, in1=in1, op=op), reads, writes)

    def ts(eng, out, in0, s1, op0, reads, writes, s2=None, op1=None):
        if op1 is None:
            P.op(eng, lambda e: e.tensor_scalar(out=out, in0=in0, scalar1=s1, scalar2=None, op0=op0), reads, writes)
        else:
            P.op(eng, lambda e: e.tensor_scalar(out=out, in0=in0, scalar1=s1, scalar2=s2, op0=op0, op1=op1), reads, writes)

    def stt(eng, out, in0, scalar, in1, op0, op1, reads, writes, accum=None):
        if accum is None:
            P.op(eng, lambda e: e.scalar_tensor_tensor(out=out, in0=in0, scalar=scalar, in1=in1, op0=op0, op1=op1), reads, writes)
        else:
            P.op(eng, lambda e: e.scalar_tensor_tensor(out=out, in0=in0, scalar=scalar, in1=in1, op0=op0, op1=op1, accum_out=accum), reads, writes)

    def cp(eng, out, in_, reads, writes):
        P.op(eng, lambda e: e.tensor_copy(out=out, in_=in_), reads, writes)

    def memset(eng, ap, val, writes):
        P.op(eng, lambda e: e.memset(ap, val), writes=writes)

    def dma(out, in_, reads, writes, dsem, eng="sp"):
        P.op(eng, lambda e: e.dma_start(out=out, in_=in_), reads, writes, dsem=dsem)

    def norm_T(src, srck, gi, dst, dstk, xn, xnk, ssb, ssk, junk, junkk, gfree=None):
        act(junk, src, AF.Square, [srck], [ssk], accum=ssb[:, 0:1])
        act(ssb[:, 1:2], ssb[:, 0:1], AF.Ln, [ssk, "eps_t"], [ssk + "r"], bias=eps_t[:, 0:1], scale=1.0 / D)
        act(ssb[:, 1:2], ssb[:, 1:2], AF.Exp, [ssk + "r"], [ssk + "r"], scale=-0.5)
        if gfree is None:
            ts("dve", xn, src, ssb[:, 1:2], ALU.mult, [srck, ssk + "r"], [xnk])
        else:
            stt("dve", xn, src, ssb[:, 1:2], gfree[0], ALU.mult, ALU.mult, [srck, ssk + "r", gfree[1]], [xnk])
        b = bank("aux")
        pv = ps[b][:].bitcast(BF16).rearrange("p (k t) -> p k t", k=8)
        for kc in range(8):
            def f(e, o=pv[:, kc, :], i=xn[:, 128 * kc:128 * kc + 128]):
                return e.transpose(out=o, in_=i, identity=ident_bf[:])
            P.op("pe", f, [xnk, "ident_bf"], [psk[b]])
        if gfree is None:
            g_bc = gcols[:, 8 * gi:8 * gi + 8].unsqueeze(2).to_broadcast([128, 8, 128])
            tt("dve", dst, pv, g_bc, ALU.mult, [psk[b], "gcols"], [dstk])
        else:
            act(dst, pv, AF.Copy, [psk[b]], [dstk])

    ykeys = []
    for b in range(nseq):
        P.epoch = b
        sb.reset(P_END)
        if b > 0:
            bar()
        nT = sb.alloc("nT", [128, 8, T], BF16)
        GT = sb.alloc("GT", [56, T], BF16)
        cposTok = sb.alloc("cposTok", [128, NT, 8], F32)
        Wc1 = sb.alloc("Wc1", [128, 32, 128], BF16)
        Wc2 = sb.alloc("Wc2", [128, 128], BF16)
        peT = sb.alloc("peT", [128, 32], BF16)
        validT = sb.alloc("validT", [128, T], BF16)
        keep = sb.alloc("keep", [128, NT, 32], F32)
        forced = sb.alloc("forced", [128, NT, 32], F32)
        ovz = sb.alloc("ovz", [128, 40], BF16)
        sel = sb.alloc("sel", [128, 24, 64], BF16)
        AD_END = sb.mark()

        for t_ in range(NT):
            xb = xbuf[t_ % 2]
            xk = "xbuf%d" % (t_ % 2)
            dma(xb[:], x_d[b, 128 * t_:128 * t_ + 128, :], [], [xk], "d_" + xk)
            norm_T(xb[:], xk, 0, nT[:, :, 128 * t_:128 * t_ + 128], "nT%d" % t_,
                   xn[t_ % 2][:], "xn%d" % (t_ % 2), ssb[t_ % 2], "ss%d" % (t_ % 2), junk[:], "junk")
        nTk = lambda c: ["nT%d" % (4 * c + i) for i in range(4)]
        if dbg and b == 0:
            dma(dbg_d["nT"], nT[:], ["nT%d" % i for i in range(NT)], ["dbg_nT"], "d_dbg_nT")

        dma(Wc1[0:64, :, :], wbf["c1"][0:2048, :].rearrange("(l d) h -> d l h", d=64), wkeys("c1"), ["Wc1a"], "d_Wc1a")
        dma(Wc1[64:128, :, :], wbf["c1"][2048:4096, :].rearrange("(l d) h -> d l h", d=64), wkeys("c1"), ["Wc1b"], "d_Wc1b")
        cload(Wc2[:], wc2_d, "Wc2", "pool")
        cload(peT[:], peT_d, "peT", "pool")
        cload(validT[:], cd["validT"], "validT", "pool")
        cload(keep[:], cd["keep"], "keep")
        cload(forced[:], cd["forced"], "forced")
        cload(ovz[:], cd["ovz"], "ovz", "pool")
        cload(sel[:], cd["sel"], "sel", "pool")
        for s in range(2 if b == 0 else 0):
            bk = bank("aux")
            rows = slice(64 * s, 64 * s + 64)
            for l in range(32):
                mm(ps[bk][:, 0:1], Wc1[rows, l, :], peT[rows, l:l + 1], l == 0, l == 31,
                   ["Wc1a", "Wc1b", "peT"], [psk[bk]])
            cp("dve", hb[:, s:s + 1], ps[bk][:, 0:1], [psk[bk]], ["hb%d" % s])
            ts("dve", nhb[:, s:s + 1], ps[bk][:, 0:1], -1.0, ALU.mult, [psk[bk]], ["nhb%d" % s])

        sb.reset(AD_END)
        sm = sb.alloc("sm", [64, T], F32)
        cw = sm
        cpos = sb.alloc("cpos", [8, T], F32)
        r1 = sm
        hmlT = sb.alloc("hmlT", [72, T], BF16)
        Gh = sb.alloc("Gh", [64, T], BF16)
        (wsm,), (wsmk,) = wload("in", 0, 8, 0, 64)
        for c in range(4):
            bk = bank("mm")
            for kc in range(8):
                mm(ps[bk][0:64, :], wsm[:, kc, 0:64], nT[:, kc, 512 * c:512 * c + 512], kc == 0, kc == 7,
                   [wsmk] + nTk(c), [psk[bk]])
            act(sm[:, 512 * c:512 * c + 512], ps[bk][0:64, :], AF.Copy, [psk[bk]], ["sm%d" % c])
        smk = ["sm%d" % c for c in range(4)]
        act(cw[0:8, :], sm[0:8, :], AF.Exp, smk + ["negb"], ["cw_f"], bias=negb[:, 0:1], scale=-1.0)
        act(cw[0:8, :], cw[0:8, :], AF.Ln, ["cw_f"], ["cw_f"], bias=1.0)
        P.op("dve", lambda e: e.tensor_tensor_scan(out=cpos[:], data0=cw[0:8, :], data1=cw[0:8, :], initial=0.0,
                                                    op0=ALU.add, op1=ALU.max), ["cw_f"], ["cpos"])
        ts("dve", hmlT[0:8, :], cpos[:], -8.0, ALU.mult, ["cpos"], ["hml0"])
        stt("dve", r1[0:8, :], cpos[:], -8.0, hmlT[0:8, :], ALU.mult, ALU.subtract, ["cpos", "hml0", "cw_f"], ["r1"])
        cp("dve", Gh[0:8, :], r1[0:8, :], ["r1"], ["midtmp"])
        cp("dve", hmlT[32:40, :], Gh[0:8, :], ["midtmp"], ["hml1"])
        tt("dve", r1[0:8, :], r1[0:8, :], Gh[0:8, :], ALU.subtract, ["r1", "midtmp"], ["r1"])
        cp("dve", hmlT[64:72, :], r1[0:8, :], ["r1"], ["hml2"])
        def emit_cposTok():
            bk = bank("aux")
            pvw = ps[bk][:, 0:128].rearrange("p (t h) -> p t h", h=8)
            for t_ in range(NT):
                def f(e, o=pvw[:, t_, :], i=cpos[0:8, 128 * t_:128 * t_ + 128]):
                    return e.transpose(out=o, in_=i, identity=ident_f[0:8, 0:8])
                P.op("pe", f, ["cpos", "ident_f"], [psk[bk]])
            cp("dve", cposTok[:], pvw, [psk[bk]], ["cposTok"])
        act(cw[32:56, :], sm[32:56, :], AF.Exp, smk, ["cw_g"], scale=-1.0)
        act(cw[32:56, :], cw[32:56, :], AF.Ln, ["cw_g"], ["cw_g"], bias=1.0)
        memset("pool", GT[:], 0.0, ["GT"])
        cp("dve", Gh[32:56, :], cw[32:56, :], ["cw_g"], ["Gh"])
        tt("dve", GT[32:56, :], cw[32:56, :], Gh[32:56, :], ALU.subtract, ["cw_g", "Gh", "GT"], ["GT"])
        cp("dve", GT[0:24, :], Gh[32:56, :], ["Gh", "GT"], ["GT"])
        if dbg and b == 0:
            dma(dbg_d["cpos"], cpos[:], ["cpos"], ["dbg_cpos"], "d_dbg_cpos")

        QTh = [sb.alloc("QTh%d" % i, [96, T], BF16) for i in range(2)]
        KTh = [sb.alloc("KTh%d" % i, [96, T], BF16) for i in range(2)]
        for i in range(2):
            memset("pool", KTh[i][64:96, :], 1.0, ["KTh_ones%d" % i])
            memset("pool", QTh[i][64:96, :], -1.0, ["QTh_neg%d" % i])
        Vaug = sb.alloc("Vaug", [128, NT, 2, 128], BF16)
        PT = [sb.alloc("PT%d" % i, [128, 512], BF16) for i in range(4)]
        rz = [sb.alloc("rz%d" % i, [64, 512], F32) for i in range(2)]
        ocp = [sb.alloc("ocp%d" % i, [64, 512], F32) for i in range(2)]
        memset("pool", Vaug[:, :, :, 64:128], 1.0, ["Vaug_ones"])
        ptr = [0]
        rzr = [0]
        for hp in range(4 if upto >= "C" else 0):
            (wc,), (wck,) = wload("in", 0, 8, 64 + 384 * hp, 384)
            for c in range(4):
                for which, dst, dk in ((0, QTh, "QT"), (1, KTh, "KT")):
                    bk = bank("mm")
                    for kc in range(8):
                        mm(ps[bk][:], wc[:, kc, 128 * which:128 * which + 128], nT[:, kc, 512 * c:512 * c + 512],
                           kc == 0, kc == 7, [wck] + nTk(c), [psk[bk]])
                    act(dst[0][0:64, 512 * c:512 * c + 512], ps[bk][0:64, :], AF.Copy, [psk[bk]], ["%s%d_0" % (dk, c)])
                    act(dst[1][0:64, 512 * c:512 * c + 512], ps[bk][64:128, :], AF.Copy, [psk[bk]], ["%s%d_1" % (dk, c)])
                bk = bank("mm")
                pv4 = ps[bk][:].rearrange("p (a e d) -> p a e d", a=4, e=2)
                for a in range(4):
                    t_ = 4 * c + a
                    for kc in range(8):
                        mm(ps[bk][:, 128 * a:128 * a + 128], nT[:, kc, 128 * t_:128 * t_ + 128], wc[:, kc, 256:384],
                           kc == 0, kc == 7, [wck, "nT%d" % t_], [psk[bk]])
                cp("dve", Vaug[:, 4 * c:4 * c + 4, :, 0:64], pv4, [psk[bk]], ["Vaug%d" % c])
            for e_ in range(2):
                h = 2 * hp + e_
                for r in range(3):
                    dma(QTh[e_][64 + r:65 + r, :], hmlT[32 * r + h:32 * r + h + 1, :], ["hml%d" % r, "QTh_neg%d" % e_], ["C3_%d" % e_], "d_C3_%d" % e_)
                    dma(KTh[e_][67 + r:68 + r, :], hmlT[32 * r + h:32 * r + h + 1, :], ["hml%d" % r, "KTh_ones%d" % e_], ["C3k_%d" % e_], "d_C3k_%d" % e_)
            steps = []
            units = {}
            for I in range(4):
                njs = 4 * I + 4
                for j in range(njs):
                    for e_ in range(2):
                        unit = units.setdefault((e_, I), {})
                        def front(st={}, e_=e_, I=I, j=j, hp=hp):
                            h = 2 * hp + e_
                            pb = 64 * e_
                            c0 = max(0, 128 * (j - 4 * I))
                            sbk = bank("mm")
                            q0 = 512 * I + c0
                            mm(ps[sbk][:, c0:512], KTh[e_][0:70, 128 * j:128 * j + 128], QTh[e_][0:70, q0:512 * I + 512],
                               True, True, ["KT%d_%d" % (j // 4, e_), "QT%d_%d" % (I, e_), "KTh_ones%d" % e_, "QTh_neg%d" % e_,
                                            "C3_%d" % e_, "C3k_%d" % e_], [psk[sbk]])
                            pt = PT[ptr[0] % 4]
                            ptk = "PT%d" % (ptr[0] % 4)
                            ptr[0] += 1
                            act(pt[:, c0:512], ps[sbk][:, c0:512], AF.Exp, [psk[sbk]], [ptk], scale=0.125)
                            if j >= 4 * I:
                                tt("pool", pt[:, c0:c0 + 128], pt[:, c0:c0 + 128], tri[:], ALU.mult, [ptk, "tri"], [ptk])
                            st["pt"], st["ptk"], st["c0"] = pt, ptk, c0

                        def back(st=front.__defaults__[0], unit=unit, e_=e_, I=I, j=j, njs=njs, hp=hp):
                            pb = 64 * e_
                            if "ab" not in unit:
                                unit["ab"] = bank("acc")
                            ab = unit["ab"]
                            pt, ptk, c0 = st["pt"], st["ptk"], st["c0"]
                            mm(ps[ab][:, c0:512], Vaug[:, j, e_, :], pt[:, c0:512], j == 0, j == njs - 1,
                               ["Vaug%d" % (j // 4), "Vaug_ones", ptk], [psk[ab]])
                            if j == njs - 1:
                                rzz = rz[rzr[0] % 2]
                                rzk = "rz%d" % (rzr[0] % 2)
                                rzr[0] += 1
                                oc = ocp[(rzr[0] - 1) % 2]
                                ock = "ocp%d" % ((rzr[0] - 1) % 2)
                                cp("dve", oc[0:64, :], ps[ab][0:64, :], [psk[ab]], [ock])
                                cp("dve", rzz[0:64, :], ps[ab][64:128, :], [psk[ab]], [rzk])
                                P.op("dve", lambda e, o=rzz[0:64, :]: e.reciprocal(out=o, in_=o), [rzk], [rzk])
                                tt("dve", mixT[pb:pb + 64, hp, 512 * I:512 * I + 512], oc[0:64, :], rzz[0:64, :], ALU.mult,
                                   [ock, rzk], ["mixT%d_%d_%d" % (hp, I, e_)])
                        steps.append((front, back))
            pipeline(steps)

        if upto >= "D":
            bar()
            sb.reset(AD_END)
            Qp0 = sb.alloc("Qp", [64, 4, 512], BF16)
            QrA0 = sb.alloc("QrA", [96, 4, 512], BF16)
            if "Qp1" not in sb.cache:
                sb.cache["Qp1"] = nc.alloc_sbuf_tensor_at("Qp1", [64, 4, 512], BF16, offset=xbuf_off[0])
                sb.cache["QrA1"] = nc.alloc_sbuf_tensor_at("QrA1", [96, 4, 512], BF16, offset=xbuf_off[1])
            Qp1, QrA1 = sb.cache["Qp1"], sb.cache["QrA1"]
            Qpb = [Qp0, Qp1]
            QrAb = [QrA0, QrA1]
            mb4 = sb.alloc("mb4", [128, 4, 32], BF16)
            KVc = sb.alloc("KVc", [128, T], BF16)
            KsAug = sb.alloc("KsAug", [96, T], BF16)
            KwT = sb.alloc("KwT", [64, T], BF16)
            VsAug = sb.alloc("VsAug", [128, NT, 128], BF16)
            VwAug = sb.alloc("VwAug", [128, NT, 128], BF16)
            hid = sb.alloc("hid", [128, 2, 128], BF16)
            kccT = sb.alloc("kccT", [64, 128], BF16)
            VcAug = sb.alloc("VcAug", [128, 128], BF16)
            U = [sb.alloc("U%d" % i, [128, 512], BF16) for i in range(4)]
            PT = [sb.alloc("PTd%d" % i, [128, 512], BF16) for i in range(4)]
            ropec = [[sb.alloc("rope%d_%d" % (i, k), [128, 512], F32) for k in range(2)] for i in range(2)]
            t12 = [[sb.alloc("t12_%d_%d" % (i, k), [128, 512], F32) for k in range(2)] for i in range(1)]
            accS = sb.alloc("accS", [64, 4, 512], F32)
            rz = [sb.alloc("rzd%d" % i, [64, 512], F32) for i in range(2)]
            fg = [sb.alloc("fg%d" % i, [64, 512], F32) for i in range(2)]
            tmpc = [sb.alloc("tmpc%d" % i, [64, 512], F32) for i in range(2)]
            silu_x = sb.alloc("silu_x", [128, 128], F32)
            silu_e = sb.alloc("silu_e", [128, 128], F32)
            zt = [sb.alloc("zt%d" % i, [128, 4], F32) for i in range(2)]
            impacc = [sb.alloc("impacc%d" % i, [128, 32], F32) for i in range(2)]
            imp2 = [sb.alloc("imp2_%d" % i, [128, 32], F32) for i in range(2)]
            imp3 = [sb.alloc("imp3_%d" % i, [128, 32], F32) for i in range(2)]
            top = [sb.alloc("top%d" % i, [128, 16], F32) for i in range(2)]
            mb = [sb.alloc("mb%d" % i, [128, 32], BF16) for i in range(2)]
            cload(KsAug[64:96, :], cd["blkind"], "KsAug_ind", "pool")
            memset("pool", VsAug[:, :, 64:128], 1.0, ["VsAug_ones"])
            memset("pool", VwAug[:, :, 64:128], 1.0, ["VwAug_ones"])
            memset("pool", VcAug[:, 0:64], 0.0, ["VcAug_zero"])
            memset("pool", VcAug[:, 64:128], 1.0, ["VcAug_ones"])
            ptr = [0]
            cnt2 = [0]
            ropei = [0]
            t12i = [0]

            def load_rope(c):
                i = ropei[0] % 2
                ropei[0] += 1
                dma(ropec[i][0][:], cd["cossin"][0, :, 512 * c:512 * c + 512], [], ["ropeC%d" % i], "d_ropeC%d" % i)
                dma(ropec[i][1][:], cd["cossin"][1, :, 512 * c:512 * c + 512], [], ["ropeS%d" % i], "d_ropeS%d" % i)
                return ropec[i][0], ropec[i][1], "ropeC%d" % i, "ropeS%d" % i

            def proj_fm(w, wk, c0, c, nrows=128):
                bk = bank("mm")
                for kc in range(8):
                    mm(ps[bk][0:nrows, :], w[:, kc, c0:c0 + nrows], nT[:, kc, 512 * c:512 * c + 512], kc == 0, kc == 7,
                       [wk] + nTk(c), [psk[bk]])
                return bk

            def rope_pair(bA, bB, cosT, sinT, ck, sk_):
                i = 0
                t1, t2 = t12[i]
                tt("dve", t1[:], ps[bA][:], cosT[:], ALU.mult, [psk[bA], ck], ["t1_%d" % i])
                tt("dve", t2[:], ps[bB][:], sinT[:], ALU.mult, [psk[bB], sk_], ["t2_%d" % i])
                return t1, t2, ["t1_%d" % i, "t2_%d" % i]

            for g in range(2):
                base_g = 64 + 1536 + 1024 * g
                (wq,), (wqk,) = wload("in", 0, 8, base_g, 512)
                (wkv,), (wkvk,) = wload("in", 0, 8, base_g + 512, 512)
                for c in range(4):
                    cosT, sinT, ck, sk_ = load_rope(c)
                    bk = proj_fm(wkv, wkvk, 0, c)
                    act(KVc[:, 512 * c:512 * c + 512], ps[bk][:], AF.Copy, [psk[bk]], ["KVc%d" % c])
                    bA = proj_fm(wkv, wkvk, 128, c)
                    bB = proj_fm(wkv, wkvk, 256, c)
                    t1, t2, tk = rope_pair(bA, bB, cosT, sinT, ck, sk_)
                    tt("pool", KsAug[0:64, 512 * c:512 * c + 512], t1[0:64, :], t2[0:64, :], ALU.add, tk, ["KsAug%d" % c])
                    tt("pool", KwT[0:64, 512 * c:512 * c + 512], t1[64:128, :], t2[64:128, :], ALU.add, tk, ["KwT%d" % c])
                    bk = bank("mm")
                    pv4 = ps[bk][:].rearrange("p (a d) -> p a d", a=4)
                    for a in range(4):
                        t_ = 4 * c + a
                        for kc in range(8):
                            mm(ps[bk][:, 128 * a:128 * a + 128], nT[:, kc, 128 * t_:128 * t_ + 128], wkv[:, kc, 384:512],
                               kc == 0, kc == 7, [wkvk, "nT%d" % t_], [psk[bk]])
                    cp("dve", VsAug[:, 4 * c:4 * c + 4, 0:64], pv4[:, :, 0:64], [psk[bk]], ["VsAug%d" % c])
                    cp("dve", VwAug[:, 4 * c:4 * c + 4, 0:64], pv4[:, :, 64:128], [psk[bk]], ["VwAug%d" % c])
                KVck = ["KVc%d" % c for c in range(4)]
                if DSTOP <= 1:
                    continue
                for s in range(2):
                    rows = slice(64 * s, 64 * s + 64)
                    bk = bank("mm")
                    for l in range(32):
                        mm(ps[bk][:, 0:127], Wc1[rows, l, :], KVc[rows, l:l + 16 * 126 + 1:16], l == 0, l == 31,
                           ["Wc1a", "Wc1b"] + KVck, [psk[bk]])
                    act(silu_e[:, 0:127], ps[bk][:, 0:127], AF.Exp, [psk[bk], "nhb%d" % s], ["silu_e"],
                        bias=nhb[:, s:s + 1], scale=-1.0)
                    ts("dve", silu_x[:, 0:127], ps[bk][:, 0:127], hb[:, s:s + 1], ALU.add, [psk[bk], "hb%d" % s], ["silu_x"])
                    ts("dve", silu_e[:, 0:127], silu_e[:, 0:127], 1.0, ALU.add, ["silu_e"], ["silu_e"])
                    P.op("dve", lambda e: e.reciprocal(out=silu_e[:, 0:127], in_=silu_e[:, 0:127]), ["silu_e"], ["silu_e"])
                    tt("dve", hid[:, s, 0:127], silu_x[:, 0:127], silu_e[:, 0:127], ALU.mult, ["silu_x", "silu_e"], ["hid%d" % s])
                bk = bank("mm")
                mm(ps[bk][0:64, 0:127], Wc2[:, 0:64], hid[:, 0, 0:127], True, True, ["Wc2", "hid0"], [psk[bk]])
                act(kccT[0:64, 0:127], ps[bk][0:64, 0:127], AF.Copy, [psk[bk]], ["kccT"])
                bk = bank("mm")
                mm(ps[bk][0:127, 0:64], hid[:, 1, 0:127], Wc2[:, 64:128], True, True, ["Wc2", "hid1"], [psk[bk]])
                cp("dve", VcAug[0:127, 0:64], ps[bk][0:127, 0:64], [psk[bk], "VcAug_zero"], ["VcAug"])

                if DSTOP <= 2:
                    continue
                def qproj(I):
                    par = I % 2
                    Qp, QrA = Qpb[par], QrAb[par]
                    cosT, sinT, ck, sk_ = load_rope(I)
                    for mt in range(2):
                        bA = proj_fm(wq, wqk, 128 * mt, I)
                        bB = proj_fm(wq, wqk, 256 + 128 * mt, I)
                        act(Qp[0:64, 2 * mt, :], ps[bA][0:64, :], AF.Copy, [psk[bA]], ["Qp%d_%d" % (2 * mt, par)])
                        act(Qp[0:64, 2 * mt + 1, :], ps[bA][64:128, :], AF.Copy, [psk[bA]], ["Qp%d_%d" % (2 * mt + 1, par)])
                        t1, t2, tk = rope_pair(bA, bB, cosT, sinT, ck, sk_)
                        tt("dve", QrA[0:64, 2 * mt, :], t1[0:64, :], t2[0:64, :], ALU.add, tk, ["Qr%d_%d" % (2 * mt, par)])
                        tt("dve", QrA[0:64, 2 * mt + 1, :], t1[64:128, :], t2[64:128, :], ALU.add, tk, ["Qr%d_%d" % (2 * mt + 1, par)])

                if DSTOP <= 2:
                    continue
                qproj(0)
                for I in range(4):
                    par = I % 2
                    Qp, QrA = Qpb[par], QrAb[par]

                    def combine(ab, r, br, first, last, I=I, g=g):
                        h = 4 * g + r
                        i = cnt2[0] % 2
                        cnt2[0] += 1
                        if only_br is not None and br != only_br:
                            return
                        act(rz[i][:], ps[ab][64:128, :], AF.Ln, [psk[ab], "tiny_t"], ["rzd%d" % i], bias=tiny_t[0:64, 0:1])
                        dst = mixT[64 * (h % 2):64 * (h % 2) + 64, 4 + h // 2, 512 * I:512 * I + 512]
                        dk = "mixT%d_%d_%d" % (4 + h // 2, I, h % 2)
                        if only_br is not None:
                            act(rz[i][:], rz[i][:], AF.Exp, ["rzd%d" % i], ["rzd%d" % i], scale=-1.0)
                            tt("dve", dst, ps[ab][0:64, :], rz[i][:], ALU.mult, [psk[ab], "rzd%d" % i], [dk])
                            return
                        mm(ps[6][0:64, :], sel[0:56, 3 * h + br, :], GT[0:56, 512 * I:512 * I + 512], True, True,
                           ["sel", "GT"], [psk[6]])
                        tt("dve", rz[i][:], ps[6][0:64, :], rz[i][:], ALU.add, [psk[6], "rzd%d" % i], ["rzd%d" % i])
                        act(fg[i][:], rz[i][:], AF.Exp, ["rzd%d" % i], ["fg%d" % i], scale=-1.0)
                        fgi, fgk = fg[i], "fg%d" % i
                        if first:
                            tt("dve", accS[:, r, :], ps[ab][0:64, :], fgi[:], ALU.mult, [psk[ab], fgk], ["accS%d" % r])
                        elif not last:
                            tt("dve", tmpc[i][:], ps[ab][0:64, :], fgi[:], ALU.mult, [psk[ab], fgk], ["tmpc%d" % i])
                            tt("pool", accS[:, r, :], accS[:, r, :], tmpc[i][:], ALU.add, ["accS%d" % r, "tmpc%d" % i], ["accS%d" % r])
                        else:
                            tt("dve", tmpc[i][:], ps[ab][0:64, :], fgi[:], ALU.mult, [psk[ab], fgk], ["tmpc%d" % i])
                            tt("dve", dst, accS[:, r, :], tmpc[i][:], ALU.add, ["accS%d" % r, "tmpc%d" % i], [dk])

                    if DSTOP <= 3:
                        continue
                    impb = {}

                    def emit_imp(half, I=I):
                        ib = 7
                        impb[half] = ib
                        iv4 = ps[ib][:].rearrange("p (a r c) -> p a r c", a=2, r=4)
                        for r in range(4):
                            for a2 in range(2):
                                a = 2 * half + a2
                                mm(iv4[:, a2, r, 0:33], U[r][0:127, 128 * a:128 * a + 128], ovz[0:127, 0:33], True, True,
                                   ["U%d" % r, "ovz"], [psk[ib]])

                    def emit_topk(a, I=I):
                        half, a2 = a // 2, a % 2
                        ib = impb[half]
                        iv = ps[ib][:].rearrange("p (a r c) -> p a r c", a=2, r=4)[:, a2, :, :]
                        t_ = 4 * I + a
                        i = a % 2
                        ts("dve", zt[i][:], iv[:, :, 32], 1e-30, ALU.max, [psk[ib]], ["zt%d" % i])
                        P.op("dve", lambda e, o=zt[i][:]: e.reciprocal(out=o, in_=o), ["zt%d" % i], ["zt%d" % i])
                        ts("dve", impacc[i][:], iv[:, 0, 0:32], zt[i][:, 0:1], ALU.mult, [psk[ib], "zt%d" % i], ["impacc%d" % i])
                        for r in range(1, 4):
                            stt("dve", impacc[i][:], iv[:, r, 0:32], zt[i][:, r:r + 1], impacc[i][:], ALU.mult, ALU.add,
                                [psk[ib], "zt%d" % i, "impacc%d" % i], ["impacc%d" % i])
                        tt("pool", imp2[i][:], impacc[i][:], keep[:, t_, :], ALU.mult, ["impacc%d" % i, "keep"], ["imp2_%d" % i])
                        tt("pool", imp2[i][:], imp2[i][:], forced[:, t_, :], ALU.add, ["imp2_%d" % i, "forced"], ["imp2_%d" % i])
                        P.op("dve", lambda e, o=top[i][:, 0:8], x_=imp2[i][:]: e.max(out=o, in_=x_), ["imp2_%d" % i], ["topa%d" % i])
                        P.op("dve", lambda e, o=imp3[i][:], a_=top[i][:, 0:8], x_=imp2[i][:]:
                             e.match_replace(out=o, in_to_replace=a_, in_values=x_, imm_value=-1e30),
                             ["imp2_%d" % i, "topa%d" % i], ["imp3_%d" % i])
                        P.op("dve", lambda e, o=top[i][:, 8:16], x_=imp3[i][:]: e.max(out=o, in_=x_), ["imp3_%d" % i], ["topb%d" % i])
                        ts("dve", mb4[:, a, :], imp2[i][:], top[i][:, 15:16], ALU.is_lt, ["imp2_%d" % i, "topb%d" % i], ["mb4_%d" % a],
                           s2=NEGB, op1=ALU.mult)

                    steps = []
                    for r in range(4):
                        def front(st={}, r=r, I=I, par=par, Qp=Qp):
                            sbk = bank("mm")
                            mm(ps[sbk][0:127, :], kccT[0:64, 0:127], Qp[0:64, r, :], True, True, ["kccT", "Qp%d_%d" % (r, par)], [psk[sbk]])
                            act(U[r][0:127, :], ps[sbk][0:127, :], AF.Exp, [psk[sbk]], ["U%d" % r], scale=0.125)
                            tt("pool", U[r][0:127, :], U[r][0:127, :], validT[0:127, 512 * I:512 * I + 512], ALU.mult,
                               ["U%d" % r, "validT"], ["U%d" % r])

                        def back(r=r, I=I):
                            ab = bank("acc")
                            mm(ps[ab][:, :], VcAug[0:127, :], U[r][0:127, :], True, True, ["VcAug", "VcAug_ones", "U%d" % r], [psk[ab]])
                            combine(ab, r, 0, True, False)
                            if r == 3 and I >= 2 and DSTOP > 4:
                                emit_imp(0)
                        steps.append((front, back))
                    if DSTOP > 6:
                      for r in range(4):
                        unit = {}
                        jlo = max(0, 4 * I - 4)
                        first_j = 4 * I - 1 if I > 0 else 0
                        js = [first_j] + [j for j in range(jlo, 4 * I + 4) if j != first_j]
                        for n_, j in enumerate(js):
                            def front(st={}, r=r, I=I, j=j, par=par, QrA=QrA):
                                qlo = max(j, 4 * I)
                                qhi = min(j + 4, 4 * I + 3)
                                ca = 128 * (qlo - 4 * I)
                                cb = 128 * (qhi - 4 * I + 1)
                                sbk = bank("mm")
                                mm(ps[sbk][:, ca:cb], KwT[0:64, 128 * j:128 * j + 128], QrA[0:64, r, ca:cb], True, True,
                                   ["KwT%d" % (j // 4), "Qr%d_%d" % (r, par)], [psk[sbk]])
                                pt = PT[ptr[0] % 4]
                                ptk = "PTd%d" % (ptr[0] % 4)
                                ptr[0] += 1
                                act(pt[:, ca:cb], ps[sbk][:, ca:cb], AF.Exp, [psk[sbk]], [ptk], scale=0.125)
                                if qlo == j:
                                    tt("pool", pt[:, ca:ca + 128], pt[:, ca:ca + 128], tri[:], ALU.mult, [ptk, "tri"], [ptk])
                                if qhi == j + 4:
                                    tt("pool", pt[:, cb - 128:cb], pt[:, cb - 128:cb], atri[:], ALU.mult, [ptk, "atri"], [ptk])
                                st["pt"], st["ptk"], st["ca"], st["cb"] = pt, ptk, ca, cb

                            def back(st=front.__defaults__[0], unit=unit, r=r, j=j, n_=n_, nj=len(js), I=I):
                                if "ab" not in unit:
                                    unit["ab"] = bank("acc")
                                ab = unit["ab"]
                                pt, ptk, ca, cb = st["pt"], st["ptk"], st["ca"], st["cb"]
                                mm(ps[ab][:, ca:cb], VwAug[:, j, :], pt[:, ca:cb], n_ == 0, n_ == nj - 1,
                                   ["VwAug%d" % (j // 4), "VwAug_ones", ptk], [psk[ab]])
                                if n_ == nj - 1:
                                    combine(ab, r, 2, False, False)
                                    if I >= 2 and DSTOP > 4:
                                        if r == 2:
                                            emit_imp(1)
                                        for a in {0: (0,), 1: (1,), 2: (2, 3), 3: ()}[r]:
                                            emit_topk(a)
                            steps.append((front, back))
                    pipeline(steps)
                    if I < 3:
                        qproj(I + 1)
                    if I >= 2:
                        mbv = ps[6][:].bitcast(BF16)
                        for a in range(4):
                            P.op("pe", lambda e, o=mbv[0:32, 128 * a:128 * a + 128], x_=mb4[:, a, :]:
                                 e.transpose(out=o, in_=x_, identity=ident_bf[:]), ["mb4_%d" % a, "ident_bf"], [psk[6]])
                        for r in range(4):
                            cp("dve", QrA[64:96, r, :], mbv[0:32, 0:512], [psk[6]], ["QrM%d_%d" % (r, par)])
                        if dbg and b == 0:
                            dma(dbg_d["mb"][g, I], QrA[64:96, 0, :], ["QrM0_%d" % par], ["dbg_mb%d%d" % (g, I)], "d_dbg_mb")
                    else:
                        for r in range(4):
                            memset("pool", QrA[64:96, r, :], 0.0, ["QrM%d_%d" % (r, par)])
                    if DSTOP <= 5:
                        continue
                    steps = []
                    for r in range(4):
                        unit = {}
                        njs = 4 * I + 4
                        for j in range(njs):
                            def front(st={}, r=r, I=I, j=j, par=par, QrA=QrA):
                                c0 = max(0, 128 * (j - 4 * I))
                                sbk = bank("mm")
                                mm(ps[sbk][:, c0:512], KsAug[0:96, 128 * j:128 * j + 128], QrA[0:96, r, c0:512], True, True,
                                   ["KsAug%d" % (j // 4), "KsAug_ind", "Qr%d_%d" % (r, par), "QrM%d_%d" % (r, par)], [psk[sbk]])
                                pt = PT[ptr[0] % 4]
                                ptk = "PTd%d" % (ptr[0] % 4)
                                ptr[0] += 1
                                act(pt[:, c0:512], ps[sbk][:, c0:512], AF.Exp, [psk[sbk]], [ptk], scale=0.125)
                                if j >= 4 * I:
                                    tt("pool", pt[:, c0:c0 + 128], pt[:, c0:c0 + 128], tri[:], ALU.mult, [ptk, "tri"], [ptk])
                                st["pt"], st["ptk"], st["c0"] = pt, ptk, c0

                            def back(st=front.__defaults__[0], unit=unit, r=r, j=j, njs=njs):
                                if "ab" not in unit:
                                    unit["ab"] = bank("acc")
                                ab = unit["ab"]
                                pt, ptk, c0 = st["pt"], st["ptk"], st["c0"]
                                mm(ps[ab][:, c0:512], VsAug[:, j, :], pt[:, c0:512], j == 0, j == njs - 1,
                                   ["VsAug%d" % (j // 4), "VsAug_ones", ptk], [psk[ab]])
                                if j == njs - 1:
                                    combine(ab, r, 1, False, True)
                            steps.append((front, back))
                    pipeline(steps)

        if upto >= "E":
            bar()
            sb.reset(P_END)
            KmT = sb.alloc("KmT", [128, 8, 256], BF16)
            Vm = sb.alloc("Vm", [128, 2, D], BF16)
            gbc = sb.alloc("gbc", [128, 3, D], F32)
            mT = sb.alloc("mT", [128, 8, 256], BF16)
            hbuf = sb.alloc("hbuf", [128, 4, D], F32)
            nT2 = sb.alloc("nT2", [128, 8, 512], BF16)
            hT = sb.alloc("hT", [128, 32, 512], BF16)
            PTe = [sb.alloc("PTe%d" % i, [128, 512], BF16) for i in range(4)]
            rze = [sb.alloc("rze%d" % i, [128, 512], F32) for i in range(2)]
            tmpe = [sb.alloc("tmpe%d" % i, [128, D], F32) for i in range(2)]
            relu_t = [sb.alloc("relu%d" % i, [128, 512], F32) for i in range(2)]
            ssq = [sb.alloc("ssq%d" % i, [128, 4], F32) for i in range(2)]
            gpre = sb.alloc("gpre", [128, 2, D], F32)
            npi = [0]
            nti = [0]

            def norm_T2(src, srck, gi, dst, dstk):
                i = nti[0] % 2
                nti[0] += 1
                gf = None
                norm_T(src, srck, gi, dst, dstk, xn[i][:], "xn%d" % i, ssb[i], "ss%d" % i, junk[:], "junk", gfree=gf)

            def norm_post(b0, b1, gi, resid, residk, out, outk, srcs=None):
                i = npi[0] % 2
                npi[0] += 1
                sq = ssq[i]
                sk = "ssq%d" % i
                if srcs is None:
                    s0, s0k, s1, s1k = ps[b0][:], psk[b0], ps[b1][:], psk[b1]
                else:
                    s0, s0k, s1, s1k = srcs
                act(junk[:, 0:512], s0, AF.Square, [s0k], [sk + "a"], accum=sq[:, 0:1])
                act(junk[:, 512:1024], s1, AF.Square, [s1k], [sk + "b"], accum=sq[:, 1:2])
                tt("dve", sq[:, 2:3], sq[:, 0:1], sq[:, 1:2], ALU.add, [sk + "a", sk + "b"], [sk + "c"])
                act(sq[:, 3:4], sq[:, 2:3], AF.Ln, [sk + "c", "eps_t"], [sk + "r"], bias=eps_t[:, 0:1], scale=1.0 / D)
                act(sq[:, 3:4], sq[:, 3:4], AF.Exp, [sk + "r"], [sk + "r"], scale=-0.5)
                tk = "tmpe%d" % i
                stt("dve", tmpe[i][:, 0:512], s0, sq[:, 3:4], gbc[:, gi, 0:512], ALU.mult, ALU.mult,
                    [s0k, sk + "r", "gbc%d" % gi], [tk + "a"])
                stt("dve", tmpe[i][:, 512:1024], s1, sq[:, 3:4], gbc[:, gi, 512:1024], ALU.mult, ALU.mult,
                    [s1k, sk + "r", "gbc%d" % gi], [tk + "b"])
                tt("dve", out, tmpe[i][:], resid, ALU.add, [tk + "a", tk + "b", residk], [outk])

            for i in range(3):
                dma(gbc[:, i, :], gbc_d[i], [], ["gbc%d" % i], "d_gbc%d" % i)
            for i in range(2):
                dma(gpre[:, i, :], gpre_d[i], [], ["gpre%d" % i], "d_gpre%d" % i)
            for m_ in range(2):
                xb = xbuf[m_ % 2]
                xk = "xbuf%d" % (m_ % 2)
                dma(xb[:], mem_d[b, 128 * m_:128 * m_ + 128, :], [], [xk], "d_" + xk)
                norm_T2(xb[:], xk, 2, mT[:, :, 128 * m_:128 * m_ + 128], "mT%d" % m_)
            mTk = ["mT0", "mT1"]
            for half in range(2):
                (w,), (wk,) = wload("xkv", 0, 8, 512 * half, 512)
                for m4 in range(4):
                    bk = bank("mm")
                    for kc in range(8):
                        mm(ps[bk][:, 0:256], w[:, kc, 128 * m4:128 * m4 + 128], mT[:, kc, :], kc == 0, kc == 7, [wk] + mTk, [psk[bk]])
                    act(KmT[:, 4 * half + m4, :], ps[bk][:, 0:256], AF.Copy, [psk[bk]], ["KmT%d" % (4 * half + m4)])
            for half in range(2):
                (w,), (wk,) = wload("xkv", 0, 8, 1024 + 512 * half, 512)
                for kb in range(2):
                    bk = bank("mm")
                    for kc in range(8):
                        mm(ps[bk][:], mT[:, kc, 128 * kb:128 * kb + 128], w[:, kc, :], kc == 0, kc == 7, [wk, "mT%d" % kb], [psk[bk]])
                    act(Vm[:, kb, 512 * half:512 * half + 512], ps[bk][:], AF.Copy, [psk[bk]], ["Vm%d_%d" % (kb, half)])
            Vmk = ["Vm%d_%d" % (kb, half) for kb in range(2) for half in range(2)]

            pte = [0]
            for ci in range(4):
                mixk = [k for k in P.allkeys if isinstance(k, str) and k.startswith("mixT") and k.split("_")[1] == str(ci)]
                (w0,), (w0k,) = wload("out", 0, 8, 0, 512)
                (w1,), (w1k,) = wload("out", 0, 8, 512, 512)
                for a in range(4):
                    t_ = 4 * ci + a
                    xb = xbuf[a % 2]
                    xk = "xbuf%d" % (a % 2)
                    dma(xb[:], x_d[b, 128 * t_:128 * t_ + 128, :], [], [xk], "d_" + xk)
                    bks = []
                    for w, wk in ((w0, w0k), (w1, w1k)):
                        bk = bank("mm4")
                        for kc in range(8):
                            mm(ps[bk][:], mixT[:, kc, 128 * t_:128 * t_ + 128], w[:, kc, :], kc == 0, kc == 7,
                               [wk] + [k for k in mixk if k.startswith("mixT%d_" % kc)], [psk[bk]])
                        bks.append(bk)
                    norm_post(bks[0], bks[1], 0, xb[:], xk, hbuf[:, a, :], "h%d" % a)
                for a in range(4):
                    norm_T2(hbuf[:, a, :], "h%d" % a, 1, nT2[:, :, 128 * a:128 * a + 128], "nT2_%d" % a)
                nT2k = ["nT2_%d" % a for a in range(4)]
                for half in range(2):
                    (w,), (wk,) = wload("xq", 0, 8, 512 * half, 512)
                    for m4 in range(4):
                        mt = 4 * half + m4
                        bk = bank("mm")
                        for kc in range(8):
                            mm(ps[bk][:], w[:, kc, 128 * m4:128 * m4 + 128], nT2[:, kc, :], kc == 0, kc == 7, [wk] + nT2k, [psk[bk]])
                        act(hT[:, mt, :], ps[bk][:], AF.Copy, [psk[bk]], ["hT%d" % mt])
                for hh in range(4):
                    pts = []
                    for kb in range(2):
                        sbk = bank("mm")
                        for dc in range(2):
                            mm(ps[sbk][:], KmT[:, 2 * hh + dc, 128 * kb:128 * kb + 128], hT[:, 2 * hh + dc, :], dc == 0, dc == 1,
                               ["KmT%d" % (2 * hh + dc), "hT%d" % (2 * hh + dc)], [psk[sbk]])
                        pt = PTe[pte[0] % 4]
                        ptk = "PTe%d" % (pte[0] % 4)
                        pte[0] += 1
                        act(pt[:], ps[sbk][:], AF.Exp, [psk[sbk]], [ptk], scale=1.0 / 16.0)
                        pts.append((pt, ptk))
                    obs = [bank("acc"), bank("acc")]
                    zb = 6 + (hh % 2)
                    for mo in range(2):
                        for kb in range(2):
                            mm(ps[obs[mo]][:], Vm[:, kb, 256 * hh + 128 * mo:256 * hh + 128 * mo + 128], pts[kb][0][:], kb == 0, kb == 1,
                               Vmk + [pts[kb][1]], [psk[obs[mo]]])
                    for kb in range(2):
                        mm(ps[zb][:], ones_bf[:, :], pts[kb][0][:], kb == 0, kb == 1, ["ones_bf", pts[kb][1]], [psk[zb]])
                    rzi = rze[hh % 2]
                    rzk = "rze%d" % (hh % 2)
                    act(rzi[:], ps[zb][:], AF.Ln, [psk[zb]], [rzk])
                    act(rzi[:], rzi[:], AF.Exp, [rzk], [rzk], scale=-1.0)
                    for mo in range(2):
                        tt("dve", hT[:, 8 + 2 * hh + mo, :], ps[obs[mo]][:], rzi[:], ALU.mult, [psk[obs[mo]], rzk], ["hT%d" % (8 + 2 * hh + mo)])
                (w0,), (w0k,) = wload("xo", 0, 8, 0, 512)
                (w1,), (w1k,) = wload("xo", 0, 8, 512, 512)
                for a in range(4):
                    bks = []
                    for w, wk in ((w0, w0k), (w1, w1k)):
                        bk = bank("mm4")
                        for kc in range(8):
                            mm(ps[bk][:], hT[:, 8 + kc, 128 * a:128 * a + 128], w[:, kc, :], kc == 0, kc == 7,
                               [wk, "hT%d" % (8 + kc)], [psk[bk]])
                        bks.append(bk)
                    norm_post(bks[0], bks[1], 1, hbuf[:, a, :], "h%d" % a, hbuf[:, a, :], "h%d" % a)
                for a in range(4):
                    norm_T2(hbuf[:, a, :], "h%d" % a, 3, nT2[:, :, 128 * a:128 * a + 128], "nT2_%d" % a)
                for hc in range(8):
                    (w,), (wk,) = wload("up", 0, 8, 512 * hc, 512)
                    for m4 in range(4):
                        m_ = 4 * hc + m4
                        bk = bank("mm")
                        for kc in range(8):
                            mm(ps[bk][:], w[:, kc, 128 * m4:128 * m4 + 128], nT2[:, kc, :], kc == 0, kc == 7, [wk] + nT2k, [psk[bk]])
                        ri = m_ % 2
                        act(relu_t[ri][:], ps[bk][:], AF.Relu, [psk[bk]], ["relu%d" % ri])
                        tt("pool", hT[:, m_, :], relu_t[ri][:], relu_t[ri][:], ALU.mult, ["relu%d" % ri], ["hT%d" % m_])
                for hkg in range(4):
                    (w0, w1), (w0k, w1k) = wload("down", 8 * hkg, 8, 0, 1024)
                    for a in range(4):
                        for half, (w, wk) in enumerate(((w0, w0k), (w1, w1k))):
                            bk = 2 * a + half
                            for k8 in range(8):
                                hk = 8 * hkg + k8
                                mm(ps[bk][:], hT[:, hk, 128 * a:128 * a + 128], w[:, k8, :], hkg == 0 and k8 == 0, hkg == 3 and k8 == 7,
                                   [wk, "hT%d" % hk], [psk[bk]])
                for a in range(4):
                    t_ = 4 * ci + a
                    norm_post(2 * a, 2 * a + 1, 2, hbuf[:, a, :], "h%d" % a, hbuf[:, a, :], "h%d" % a)
                    yk = "y%d_%d" % (b, t_)
                    dma(y_d[b, 128 * t_:128 * t_ + 128, :], hbuf[:, a, :], ["h%d" % a], [yk], "d_y%d" % a)
                    ykeys.append(yk)

    P.op("sp", None, reads=ykeys)
    if dbg:
        mk = [k for k in P.allkeys if str(k).startswith("mixT")]
        dma(dbg_d["mixT"], mixT[:], mk, ["dbg_mixT"], "d_dbg_mixT")
        P.op("sp", None, reads=["dbg_mixT", "dbg_nT", "dbg_cpos"])
    P.finish()
    return nc, P


def prep_shared(inputs):
    f = lambda k: np.ascontiguousarray(np.asarray(inputs[k], np.float32)[0])
    w_in = f("w_in")
    idx = w_in_index()
    w_ext = np.zeros((D, NCOLS), np.float32)
    m = idx >= 0
    w_ext[:, m] = w_in[:, idx[m]]
    sh = {
        "w_in_ext": w_ext,
        "w_out": f("w_mix_out"), "w_xq": f("w_xq"), "w_xkv": f("w_xkv"), "w_xo": f("w_xo"),
        "w_up": f("w_up"), "w_down": f("w_down"),
        "w_c1": np.ascontiguousarray(np.concatenate([f("w_ck1"), f("w_cv1")], 0)),
        "w_c2": np.ascontiguousarray(np.concatenate([f("w_ck2"), f("w_cv2")], 1)),
        "peT": np.ascontiguousarray(np.concatenate([f("pe_k").T, f("pe_v").T], 0)),
        "b_forget": np.ascontiguousarray(f("b_forget").reshape(8, 1)),
    }
    gc = np.zeros((128, 32), np.float32)
    for i, k in enumerate(("g_mix_pre", "g_x_pre", "g_mem", "g_mlp_pre")):
        gc[:, 8 * i:8 * i + 8] = f(k).reshape(8, 128).T
    sh["gcols"] = gc
    sh["gbc"] = np.ascontiguousarray(np.stack([np.broadcast_to(f(k)[None, :], (128, D))
                                               for k in ("g_mix_post", "g_x_post", "g_mlp_post")], 0))
    sh["gpre"] = np.ascontiguousarray(np.stack([np.broadcast_to(f(k)[None, :], (128, D))
                                                for k in ("g_x_pre", "g_mlp_pre")], 0))
    for k, v in host_consts().items():
        sh["c_" + k] = np.ascontiguousarray(v)
    return sh


def kernel(**inputs):
    ncores = 8
    x = np.asarray(inputs["x"], np.float32)
    mem = np.asarray(inputs["mem"], np.float32)
    nseq = x.shape[0] // ncores
    sh = prep_shared(inputs)
    nc, P = build(nseq)
    in_maps = []
    for c in range(ncores):
        m = dict(sh)
        m["x"] = np.ascontiguousarray(x[c * nseq:(c + 1) * nseq])
        m["mem"] = np.ascontiguousarray(mem[c * nseq:(c + 1) * nseq])
        in_maps.append(m)
    res = run_bass_kernel_spmd(nc, in_maps, core_ids=list(range(ncores)))
    return np.concatenate([np.asarray(r["y"], np.float32) for r in res.results], axis=0)
```

```python
import contextlib
import numpy as np
import concourse.bass as bass
import concourse.mybir as mybir
from concourse.bass_utils import run_bass_kernel_spmd

F32 = mybir.dt.float32
BF16 = mybir.dt.bfloat16
AF = mybir.ActivationFunctionType
ALU = mybir.AluOpType

ENGS = ("pe", "act", "dve", "pool", "sp")
T = 2048
D = 1024
NT = 16
EPS = 1e-6
NEGB = -240000.0
NCOLS = 64 + 1536 + 2048
import os as _os
DSTOP = int(_os.environ.get("DSTOP", "99"))
FINAL_ENG = _os.environ.get("FINAL_ENG", "dve")


class Op:
    __slots__ = ("eng", "fn", "reads", "writes", "dsem", "waits", "inc", "epoch")

    def __init__(self, eng, fn, reads, writes, dsem, epoch):
        self.eng = eng
        self.fn = fn
        self.reads = tuple(reads)
        self.writes = tuple(writes)
        self.dsem = dsem
        self.waits = {}
        self.inc = None
        self.epoch = epoch


class Prog:
    def __init__(self, nc):
        self.nc = nc
        self.ops = []
        self.epoch = 0
        self.allkeys = set()
        self.bar_keys = ()

    def op(self, eng, fn, reads=(), writes=(), dsem=None):
        assert eng in ENGS
        reads = tuple(reads) + self.bar_keys
        self.allkeys.update(reads)
        self.allkeys.update(writes)
        self.ops.append(Op(eng, fn, reads, writes, dsem, self.epoch))

    def barrier(self, fns):
        keys = tuple(sorted(self.allkeys, key=str))
        n = len([o for o in self.ops if o.fn is not None])
        newbar = tuple("BAR_%s_%d" % (e, n) for e in ("pe", "act", "dve", "pool"))
        for e, k in zip(("pe", "act", "dve", "pool"), newbar):
            self.ops.append(Op(e, fns[e], keys, (k,), None, self.epoch))
        self.allkeys = set(newbar)
        self.bar_keys = newbar

    def finish(self):
        nc = self.nc
        ops = self.ops
        last_w = {}
        readers = {}
        deps_of = []
        needed = set()
        for i, op in enumerate(ops):
            deps = set()
            is_dma = op.dsem is not None
            for k in op.reads:
                j = last_w.get(k)
                if j is not None:
                    deps.add(j)
                if isinstance(k, str) and k.startswith("ps"):
                    for j in readers.get(k, ()):
                        if ops[j].eng != op.eng:
                            deps.add(j)
            for k in op.writes:
                j = last_w.get(k)
                if j is not None:
                    oj = ops[j]
                    if is_dma or oj.dsem is not None or oj.eng != op.eng or op.eng != "pe":
                        deps.add(j)
                for j in readers.get(k, ()):
                    oj = ops[j]
                    if is_dma or oj.dsem is not None or oj.eng != op.eng or op.eng != "pe":
                        deps.add(j)
            deps.discard(i)
            for k in op.reads:
                readers.setdefault(k, []).append(i)
            for k in op.writes:
                last_w[k] = i
                readers[k] = []
            deps_of.append(deps)
            needed |= deps
        cnt = {}
        token = {}
        sem_names = set()
        for i, op in enumerate(ops):
            if op.dsem is not None:
                cnt[op.dsem] = cnt.get(op.dsem, 0) + 16
                token[i] = (op.dsem, cnt[op.dsem])
                op.inc = (op.dsem, 16)
                sem_names.add(op.dsem)
            elif i in needed:
                assert op.fn is not None
                s = "c_%s_%d" % (op.eng, op.epoch)
                cnt[s] = cnt.get(s, 0) + 1
                token[i] = (s, cnt[s])
                op.inc = (s, 1)
                sem_names.add(s)
        waited = {e: {} for e in ENGS}
        nwaits = 0
        for i, op in enumerate(ops):
            w = {}
            for j in deps_of[i]:
                s, v = token[j]
                if v > w.get(s, 0):
                    w[s] = v
            wd = waited[op.eng]
            for s, v in list(w.items()):
                if wd.get(s, 0) >= v:
                    del w[s]
                else:
                    wd[s] = v
            op.waits = w
            nwaits += len(w)
        self.stats = dict(n_ops=len(ops), n_waits=nwaits, n_sems=len(sem_names),
                          maxcnt=max(cnt.values()) if cnt else 0)
        sems = {}
        with contextlib.ExitStack() as st:
            for s in sorted(sem_names):
                sems[s] = st.enter_context(nc.semaphore(s))
            block = st.enter_context(nc.Block())
            per = {e: [o for o in ops if o.eng == e] for e in ENGS}

            def run(eng, lst):
                for o in lst:
                    for s, v in o.waits.items():
                        eng.wait_ge(sems[s], v)
                    if o.fn is None:
                        continue
                    ins = o.fn(eng)
                    if o.inc is not None:
                        ins.then_inc(sems[o.inc[0]], o.inc[1])

            @block.tensor
            def _(e):
                run(e, per["pe"])

            @block.scalar
            def _(e):
                run(e, per["act"])

            @block.vector
            def _(e):
                run(e, per["dve"])

            @block.gpsimd
            def _(e):
                run(e, per["pool"])

            @block.sync
            def _(e):
                run(e, per["sp"])


def _bf(a):
    import ml_dtypes
    return np.asarray(a, np.float32).astype(ml_dtypes.bfloat16)


def host_consts():
    c = {}
    p = np.arange(128)
    c["ident_f"] = np.eye(128, dtype=np.float32)
    c["tri"] = (p[:, None] <= p[None, :]).astype(np.float32)
    c["atri"] = (p[:, None] > p[None, :]).astype(np.float32)
    half = 32
    inv = (10000.0 ** (-np.arange(half, dtype=np.float32) / half)).astype(np.float32)
    pos = np.arange(T, dtype=np.float32)
    ang = (pos[:, None] * inv[None, :]).astype(np.float32)
    cos = np.cos(ang).astype(np.float32).T
    sin = np.sin(ang).astype(np.float32).T
    cs = np.zeros((2, 128, T), np.float32)
    for base in (0, 64):
        cs[0, base:base + 32] = cos
        cs[0, base + 32:base + 64] = cos
        cs[1, base:base + 32] = -sin
        cs[1, base + 32:base + 64] = sin
    c["cossin"] = cs
    c["blkind"] = (np.arange(T)[None, :] // 64 == np.arange(32)[:, None]).astype(np.float32)
    n = np.arange(128)
    valid = ((16 * n[:, None] + 31) <= np.arange(T)[None, :]) & (n[:, None] < 127)
    c["validT"] = valid.astype(np.float32)
    starts = np.arange(127) * 16
    sel_lo = np.arange(32) * 64
    ov = ((starts[:, None] < sel_lo[None, :] + 64) & (starts[:, None] + 32 > sel_lo[None, :])).astype(np.float32)
    ovz = np.zeros((128, 40), np.float32)
    ovz[:127, :32] = ov
    ovz[:127, 32] = 1.0
    c["ovz"] = ovz
    t = np.arange(T)
    cur = t // 64
    j = np.arange(32)
    keep = ((j[None, :] < cur[:, None] - 1) & (j[None, :] != 0)).astype(np.float32)
    forced = np.zeros((T, 32), np.float32)
    forced[(j[None, :] == 0) | (j[None, :] == cur[:, None] - 1)] = 1e4
    forced[j[None, :] == cur[:, None]] = 2e4
    forced[j[None, :] > cur[:, None]] = -1.0
    c["keep"] = keep.reshape(NT, 128, 32).transpose(1, 0, 2).copy()
    c["forced"] = forced.reshape(NT, 128, 32).transpose(1, 0, 2).copy()
    sel = np.zeros((128, 24, 64), np.float32)
    for r in range(24):
        sel[r, r, :] = 1.0
        sel[32 + r, r, :] = 1.0
    c["sel"] = sel
    return c


def w_in_index():
    idx = []
    idx += list(range(1536, 1544)) + [-1] * 24 + list(range(2824, 2848)) + [-1] * 8
    for hp in range(4):
        idx += list(range(128 * hp, 128 * hp + 128))
        idx += list(range(512 + 128 * hp, 512 + 128 * hp + 128))
        idx += list(range(1024 + 128 * hp, 1024 + 128 * hp + 128))

    def sw(l):
        return l[32:] + l[:32]

    for g in range(2):
        qa, qb = [], []
        for r in range(4):
            h = 4 * g + r
            cols = list(range(1544 + 64 * h, 1544 + 64 * h + 64))
            qa += cols
            qb += sw(cols)
        idx += qa + qb
        kc = list(range(2056 + 64 * g, 2056 + 64 * g + 64))
        vc = list(range(2184 + 64 * g, 2184 + 64 * g + 64))
        ks = list(range(2312 + 64 * g, 2312 + 64 * g + 64))
        vs = list(range(2440 + 64 * g, 2440 + 64 * g + 64))
        kw = list(range(2568 + 64 * g, 2568 + 64 * g + 64))
        vw = list(range(2696 + 64 * g, 2696 + 64 * g + 64))
        idx += kc + vc + ks + kw + sw(ks) + sw(kw) + vs + vw
    assert len(idx) == NCOLS
    return np.array(idx)


class SB:
    def __init__(self, nc, limit):
        self.nc = nc
        self.limit = limit
        self.top = 16512
        self.n = 0
        self.cache = {}

    def alloc(self, name, shape, dt):
        nbytes = int(np.prod(shape[1:])) * (4 if dt == F32 else 2)
        nbytes = (nbytes + 31) // 32 * 32
        off = self.top
        self.top += nbytes
        assert self.top <= self.limit, (name, self.top, self.limit)
        key = (name, off, tuple(shape), str(dt))
        if key not in self.cache:
            self.n += 1
            self.cache[key] = self.nc.alloc_sbuf_tensor_at("%s_%d" % (name, self.n), list(shape), dt, offset=off)
        return self.cache[key]

    def mark(self):
        return self.top

    def reset(self, m):
        self.top = m


def build(nseq, upto="E", dbg=False, only_br=None):
    nc = bass.Bass("TRN2", target_bir_lowering=False)
    P = Prog(nc)

    def din(name, shape, dt=F32):
        return nc.dram_tensor(name, list(shape), dt, kind="ExternalInput").ap()

    x_d = din("x", [nseq, T, D])
    mem_d = din("mem", [nseq, 256, D])
    w_d = {
        "in": din("w_in_ext", [D, NCOLS]),
        "out": din("w_out", [D, D]),
        "xq": din("w_xq", [D, D]),
        "xkv": din("w_xkv", [D, 2 * D]),
        "xo": din("w_xo", [D, D]),
        "up": din("w_up", [D, 4 * D]),
        "down": din("w_down", [4 * D, D]),
        "c1": din("w_c1", [4096, 128]),
    }
    wc2_d = din("w_c2", [128, 128])
    peT_d = din("peT", [128, 32])
    gcols_d = din("gcols", [128, 32])
    gbc_d = din("gbc", [3, 128, D])
    gpre_d = din("gpre", [2, 128, D])
    bf_d = din("b_forget", [8, 1])
    cst = host_consts()
    cd = {k: din("c_" + k, v.shape) for k, v in cst.items()}
    y_d = nc.dram_tensor("y", [nseq, T, D], F32, kind="ExternalOutput").ap()
    dbg_d = {}
    if dbg:
        dbg_d["mixT"] = nc.dram_tensor("dbg_mixT", [128, 8, T], BF16, kind="ExternalOutput").ap()
        dbg_d["nT"] = nc.dram_tensor("dbg_nT", [128, 8, T], BF16, kind="ExternalOutput").ap()
        dbg_d["cpos"] = nc.dram_tensor("dbg_cpos", [8, T], F32, kind="ExternalOutput").ap()
        dbg_d["mb"] = nc.dram_tensor("dbg_mb", [2, 4, 32, 512], BF16, kind="ExternalOutput").ap()
    wbf = {k: nc.dram_tensor("wbf_" + k, list(v.shape), BF16, kind="Internal").ap() for k, v in w_d.items()}

    ps = [nc.alloc_psum_tensor("ps%d" % i, [128, 512], F32) for i in range(8)]
    psk = ["ps%d" % i for i in range(8)]
    rr = {}

    def bank(pool):
        lst = {"mm": (0, 1, 2, 3), "acc": (4, 5), "aux": (6, 7), "mm4": (0, 1, 2, 3, 4, 5, 6, 7)}[pool]
        i = rr.get(pool, 0)
        rr[pool] = i + 1
        return lst[i % len(lst)]

    sb = SB(nc, 229376)
    ident_bf = sb.alloc("ident_bf", [128, 128], BF16)
    ident_f = sb.alloc("ident_f", [128, 128], F32)
    tri = sb.alloc("tri", [128, 128], BF16)
    atri = sb.alloc("atri", [128, 128], BF16)
    ones_bf = sb.alloc("ones_bf", [128, 128], BF16)
    gcols = sb.alloc("gcols", [128, 32], F32)
    negb = sb.alloc("negb", [8, 1], F32)
    bfs = sb.alloc("bfs", [8, 1], F32)
    hb = sb.alloc("hb", [128, 2], F32)
    nhb = sb.alloc("nhb", [128, 2], F32)
    scr = sb.alloc("scr", [128, 8], F32)
    eps_t = sb.alloc("eps_t", [128, 1], F32)
    tiny_t = sb.alloc("tiny_t", [128, 1], F32)
    wslots = [sb.alloc("wslot%d" % i, [128, 8, 512], BF16) for i in range(4)]
    wsk = ["wslot%d" % i for i in range(4)]
    wrr = [0]
    mixT = sb.alloc("mixT", [128, 8, T], BF16)
    xbuf_off = []
    xbuf = []
    for i in range(2):
        xbuf_off.append(sb.top)
        xbuf.append(sb.alloc("xbuf%d" % i, [128, D], F32))
    xn = [sb.alloc("xn%d" % i, [128, D], BF16) for i in range(2)]
    junk = sb.alloc("junk", [128, D], BF16)
    ssb = [sb.alloc("ss%d" % i, [128, 2], F32) for i in range(2)]
    P_END = sb.mark()

    def cload(dst_ap, src_ap, key, eng="sp"):
        P.op(eng, lambda e: e.dma_start(out=dst_ap, in_=src_ap), writes=[key], dsem="d_" + key)

    def bar():
        P.barrier({
            "pe": lambda e: e.matmul(ps[7][0:1, 0:1], lhsT=ones_bf[0:1, 0:1], rhs=ones_bf[0:1, 0:1], start=True, stop=True),
            "act": lambda e: e.activation(out=scr[0:1, 0:1], in_=scr[0:1, 1:2], func=AF.Copy),
            "dve": lambda e: e.memset(scr[0:1, 2:3], 0.0),
            "pool": lambda e: e.memset(scr[0:1, 3:4], 0.0),
        })

    P.op("dve", lambda e: e.memset(scr[:], 0.0), writes=["scr"])
    P.op("pool", lambda e: e.memset(ones_bf[:], 1.0), writes=["ones_bf"])
    P.op("pool", lambda e: e.memset(eps_t[:], EPS), writes=["eps_t"])
    P.op("pool", lambda e: e.memset(tiny_t[:], 1e-18), writes=["tiny_t"])
    cload(ident_f[:], cd["ident_f"], "ident_f")
    cload(ident_bf[:], cd["ident_f"], "ident_bf", "pool")
    cload(tri[:], cd["tri"], "tri", "pool")
    cload(atri[:], cd["atri"], "atri", "pool")
    cload(gcols[:], gcols_d, "gcols")
    cload(bfs[:], bf_d, "bfs")
    P.op("dve", lambda e: e.tensor_scalar_mul(out=negb[:], in0=bfs[:], scalar1=-1.0), reads=["bfs"], writes=["negb"])
    for k in ("c1", "in", "out", "xq", "xkv", "xo", "up", "down"):
        src = w_d[k]
        dst = wbf[k]
        rows, cols = src.shape
        if cols > 1024:
            nsp = (cols + 1023) // 1024
            step = cols // nsp
            assert step * nsp == cols
            for i in range(nsp):
                def f(e, s=src[:, i * step:(i + 1) * step], d=dst[:, i * step:(i + 1) * step]):
                    return e.dma_start(out=d, in_=s)
                P.op("pool", f, writes=["wbf_%s_%d" % (k, i)], dsem="d_wbf_%s_%d" % (k, i))
        else:
            def f(e, s=src, d=dst):
                return e.dma_start(out=d, in_=s)
            P.op("pool", f, writes=["wbf_%s_0" % k], dsem="d_wbf_%s_0" % k)

    def wkeys(k, c0=None, ncol=None):
        cols = w_d[k].shape[1]
        n = (cols + 1023) // 1024 if cols > 1024 else 1
        if c0 is None or n == 1:
            return ["wbf_%s_%d" % (k, i) for i in range(n)]
        step = cols // n
        return ["wbf_%s_%d" % (k, i) for i in range(n) if i * step < c0 + ncol and (i + 1) * step > c0]

    def wview(k):
        return wbf[k].rearrange("(kc p) c -> p kc c", p=128)

    def wload(k, kc0, nkc, c0, ncol, nslots=1):
        assert nkc == 8 and ncol in (512, 1024, 384, 64)
        ns = 2 if ncol == 1024 else 1
        i0 = wrr[0] % 4
        if ns == 2 and i0 % 2 == 1:
            wrr[0] += 1
            i0 = wrr[0] % 4
        wrr[0] += ns
        keys = [wsk[i0 + i] for i in range(ns)]
        src = wview(k)[:, kc0:kc0 + nkc, c0:c0 + ncol]
        aps = []
        for i in range(ns):
            w = min(512, ncol)
            dstt = wslots[i0 + i][:, :, 0:w]
            srci = src[:, :, i * 512:i * 512 + w]

            def f(e, d=dstt, s=srci):
                return e.dma_start(out=d, in_=s)
            P.op("sp", f, reads=wkeys(k, c0 + i * 512, w), writes=[keys[i]], dsem="d_" + keys[i])
            aps.append(dstt)
        return aps, keys

    LA = 3

    def pipeline(steps):
        n = len(steps)
        for i in range(n + LA):
            if i < n:
                steps[i][0]()
            if i >= LA:
                steps[i - LA][1]()

    def mm(out, lhsT, rhs, start, stop, reads, writes):
        P.op("pe", lambda e: e.matmul(out, lhsT=lhsT, rhs=rhs, start=start, stop=stop), reads, writes)

    def act(out, in_, func, reads, writes, bias=None, scale=None, accum=None):
        kw = {}
        if bias is not None:
            kw["bias"] = bias
        if scale is not None:
            kw["scale"] = scale
        if accum is not None:
            kw["accum_out"] = accum
        P.op("act", lambda e: e.activation(out=out, in_=in_, func=func, **kw), reads, writes)

    def tt(eng, out, in0, in1, op, reads, writes):
        P.op(eng, lambda e: e.tensor_tensor(out=out, in0=in0, in1=in1, op=op), reads, writes)

    def ts(eng, out, in0, s1, op0, reads, writes, s2=None, op1=None):
        if op1 is None:
            P.op(eng, lambda e: e.tensor_scalar(out=out, in0=in0, scalar1=s1, scalar2=None, op0=op0), reads, writes)
        else:
            P.op(eng, lambda e: e.tensor_scalar(out=out, in0=in0, scalar1=s1, scalar2=s2, op0=op0, op1=op1), reads, writes)

    def stt(eng, out, in0, scalar, in1, op0, op1, reads, writes, accum=None):
        if accum is None:
            P.op(eng, lambda e: e.scalar_tensor_tensor(out=out, in0=in0, scalar=scalar, in1=in1, op0=op0, op1=op1), reads, writes)
        else:
            P.op(eng, lambda e: e.scalar_tensor_tensor(out=out, in0=in0, scalar=scalar, in1=in1, op0=op0, op1=op1, accum_out=accum), reads, writes)

    def cp(eng, out, in_, reads, writes):
        P.op(eng, lambda e: e.tensor_copy(out=out, in_=in_), reads, writes)

    def memset(eng, ap, val, writes):
        P.op(eng, lambda e: e.memset(ap, val), writes=writes)

    def dma(out, in_, reads, writes, dsem, eng="sp"):
        P.op(eng, lambda e: e.dma_start(out=out, in_=in_), reads, writes, dsem=dsem)

    def norm_T(src, srck, gi, dst, dstk, xn, xnk, ssb, ssk, junk, junkk, gfree=None):
        act(junk, src, AF.Square, [srck], [ssk], accum=ssb[:, 0:1])
        act(ssb[:, 1:2], ssb[:, 0:1], AF.Ln, [ssk, "eps_t"], [ssk + "r"], bias=eps_t[:, 0:1], scale=1.0 / D)
        act(ssb[:, 1:2], ssb[:, 1:2], AF.Exp, [ssk + "r"], [ssk + "r"], scale=-0.5)
        if gfree is None:
            ts("dve", xn, src, ssb[:, 1:2], ALU.mult, [srck, ssk + "r"], [xnk])
        else:
            stt("dve", xn, src, ssb[:, 1:2], gfree[0], ALU.mult, ALU.mult, [srck, ssk + "r", gfree[1]], [xnk])
        b = bank("aux")
        pv = ps[b][:].bitcast(BF16).rearrange("p (k t) -> p k t", k=8)
        for kc in range(8):
            def f(e, o=pv[:, kc, :], i=xn[:, 128 * kc:128 * kc + 128]):
                return e.transpose(out=o, in_=i, identity=ident_bf[:])
            P.op("pe", f, [xnk, "ident_bf"], [psk[b]])
        if gfree is None:
            g_bc = gcols[:, 8 * gi:8 * gi + 8].unsqueeze(2).to_broadcast([128, 8, 128])
            tt("dve", dst, pv, g_bc, ALU.mult, [psk[b], "gcols"], [dstk])
        else:
            act(dst, pv, AF.Copy, [psk[b]], [dstk])

    ykeys = []
    for b in range(nseq):
        P.epoch = b
        sb.reset(P_END)
        if b > 0:
            bar()
        nT = sb.alloc("nT", [128, 8, T], BF16)
        GT = sb.alloc("GT", [56, T], BF16)
        cposTok = sb.alloc("cposTok", [128, NT, 8], F32)
        Wc1 = sb.alloc("Wc1", [128, 32, 128], BF16)
        Wc2 = sb.alloc("Wc2", [128, 128], BF16)
        peT = sb.alloc("peT", [128, 32], BF16)
        validT = sb.alloc("validT", [128, T], BF16)
        keep = sb.alloc("keep", [128, NT, 32], F32)
        forced = sb.alloc("forced", [128, NT, 32], F32)
        ovz = sb.alloc("ovz", [128, 40], BF16)
        sel = sb.alloc("sel", [128, 24, 64], BF16)
        AD_END = sb.mark()

        for t_ in range(NT):
            xb = xbuf[t_ % 2]
            xk = "xbuf%d" % (t_ % 2)
            dma(xb[:], x_d[b, 128 * t_:128 * t_ + 128, :], [], [xk], "d_" + xk)
            norm_T(xb[:], xk, 0, nT[:, :, 128 * t_:128 * t_ + 128], "nT%d" % t_,
                   xn[t_ % 2][:], "xn%d" % (t_ % 2), ssb[t_ % 2], "ss%d" % (t_ % 2), junk[:], "junk")
        nTk = lambda c: ["nT%d" % (4 * c + i) for i in range(4)]
        if dbg and b == 0:
            dma(dbg_d["nT"], nT[:], ["nT%d" % i for i in range(NT)], ["dbg_nT"], "d_dbg_nT")

        dma(Wc1[0:64, :, :], wbf["c1"][0:2048, :].rearrange("(l d) h -> d l h", d=64), wkeys("c1"), ["Wc1a"], "d_Wc1a")
        dma(Wc1[64:128, :, :], wbf["c1"][2048:4096, :].rearrange("(l d) h -> d l h", d=64), wkeys("c1"), ["Wc1b"], "d_Wc1b")
        cload(Wc2[:], wc2_d, "Wc2", "pool")
        cload(peT[:], peT_d, "peT", "pool")
        cload(validT[:], cd["validT"], "validT", "pool")
        cload(keep[:], cd["keep"], "keep")
        cload(forced[:], cd["forced"], "forced")
        cload(ovz[:], cd["ovz"], "ovz", "pool")
        cload(sel[:], cd["sel"], "sel", "pool")
        for s in range(2 if b == 0 else 0):
            bk = bank("aux")
            rows = slice(64 * s, 64 * s + 64)
            for l in range(32):
                mm(ps[bk][:, 0:1], Wc1[rows, l, :], peT[rows, l:l + 1], l == 0, l == 31,
                   ["Wc1a", "Wc1b", "peT"], [psk[bk]])
            cp("dve", hb[:, s:s + 1], ps[bk][:, 0:1], [psk[bk]], ["hb%d" % s])
            ts("dve", nhb[:, s:s + 1], ps[bk][:, 0:1], -1.0, ALU.mult, [psk[bk]], ["nhb%d" % s])

        sb.reset(AD_END)
        sm = sb.alloc("sm", [64, T], F32)
        cw = sm
        cpos = sb.alloc("cpos", [8, T], F32)
        r1 = sm
        hmlT = sb.alloc("hmlT", [72, T], BF16)
        Gh = sb.alloc("Gh", [64, T], BF16)
        (wsm,), (wsmk,) = wload("in", 0, 8, 0, 64)
        for c in range(4):
            bk = bank("mm")
            for kc in range(8):
                mm(ps[bk][0:64, :], wsm[:, kc, 0:64], nT[:, kc, 512 * c:512 * c + 512], kc == 0, kc == 7,
                   [wsmk] + nTk(c), [psk[bk]])
            act(sm[:, 512 * c:512 * c + 512], ps[bk][0:64, :], AF.Copy, [psk[bk]], ["sm%d" % c])
        smk = ["sm%d" % c for c in range(4)]
        act(cw[0:8, :], sm[0:8, :], AF.Exp, smk + ["negb"], ["cw_f"], bias=negb[:, 0:1], scale=-1.0)
        act(cw[0:8, :], cw[0:8, :], AF.Ln, ["cw_f"], ["cw_f"], bias=1.0)
        P.op("dve", lambda e: e.tensor_tensor_scan(out=cpos[:], data0=cw[0:8, :], data1=cw[0:8, :], initial=0.0,
                                                    op0=ALU.add, op1=ALU.max), ["cw_f"], ["cpos"])
        ts("dve", hmlT[0:8, :], cpos[:], -8.0, ALU.mult, ["cpos"], ["hml0"])
        stt("dve", r1[0:8, :], cpos[:], -8.0, hmlT[0:8, :], ALU.mult, ALU.subtract, ["cpos", "hml0", "cw_f"], ["r1"])
        cp("dve", Gh[0:8, :], r1[0:8, :], ["r1"], ["midtmp"])
        cp("dve", hmlT[32:40, :], Gh[0:8, :], ["midtmp"], ["hml1"])
        tt("dve", r1[0:8, :], r1[0:8, :], Gh[0:8, :], ALU.subtract, ["r1", "midtmp"], ["r1"])
        cp("dve", hmlT[64:72, :], r1[0:8, :], ["r1"], ["hml2"])
        def emit_cposTok():
            bk = bank("aux")
            pvw = ps[bk][:, 0:128].rearrange("p (t h) -> p t h", h=8)
            for t_ in range(NT):
                def f(e, o=pvw[:, t_, :], i=cpos[0:8, 128 * t_:128 * t_ + 128]):
                    return e.transpose(out=o, in_=i, identity=ident_f[0:8, 0:8])
                P.op("pe", f, ["cpos", "ident_f"], [psk[bk]])
            cp("dve", cposTok[:], pvw, [psk[bk]], ["cposTok"])
        act(cw[32:56, :], sm[32:56, :], AF.Exp, smk, ["cw_g"], scale=-1.0)
        act(cw[32:56, :], cw[32:56, :], AF.Ln, ["cw_g"], ["cw_g"], bias=1.0)
        memset("pool", GT[:], 0.0, ["GT"])
        cp("dve", Gh[32:56, :], cw[32:56, :], ["cw_g"], ["Gh"])
        tt("dve", GT[32:56, :], cw[32:56, :], Gh[32:56, :], ALU.subtract, ["cw_g", "Gh", "GT"], ["GT"])
        cp("dve", GT[0:24, :], Gh[32:56, :], ["Gh", "GT"], ["GT"])
        if dbg and b == 0:
            dma(dbg_d["cpos"], cpos[:], ["cpos"], ["dbg_cpos"], "d_dbg_cpos")

        QTh = [sb.alloc("QTh%d" % i, [96, T], BF16) for i in range(2)]
        KTh = [sb.alloc("KTh%d" % i, [96, T], BF16) for i in range(2)]
        for i in range(2):
            memset("pool", KTh[i][64:96, :], 1.0, ["KTh_ones%d" % i])
            memset("pool", QTh[i][64:96, :], -1.0, ["QTh_neg%d" % i])
        Vaug = sb.alloc("Vaug", [128, NT, 2, 128], BF16)
        PT = [sb.alloc("PT%d" % i, [128, 512], BF16) for i in range(4)]
        rz = [sb.alloc("rz%d" % i, [64, 512], F32) for i in range(2)]
        memset("pool", Vaug[:, :, :, 64:128], 1.0, ["Vaug_ones"])
        ptr = [0]
        rzr = [0]
        for hp in range(4 if upto >= "C" else 0):
            (wc,), (wck,) = wload("in", 0, 8, 64 + 384 * hp, 384)
            for c in range(4):
                for which, dst, dk in ((0, QTh, "QT"), (1, KTh, "KT")):
                    bk = bank("mm")
                    for kc in range(8):
                        mm(ps[bk][:], wc[:, kc, 128 * which:128 * which + 128], nT[:, kc, 512 * c:512 * c + 512],
                           kc == 0, kc == 7, [wck] + nTk(c), [psk[bk]])
                    act(dst[0][0:64, 512 * c:512 * c + 512], ps[bk][0:64, :], AF.Copy, [psk[bk]], ["%s%d_0" % (dk, c)])
                    act(dst[1][0:64, 512 * c:512 * c + 512], ps[bk][64:128, :], AF.Copy, [psk[bk]], ["%s%d_1" % (dk, c)])
                bk = bank("mm")
                pv4 = ps[bk][:].rearrange("p (a e d) -> p a e d", a=4, e=2)
                for a in range(4):
                    t_ = 4 * c + a
                    for kc in range(8):
                        mm(ps[bk][:, 128 * a:128 * a + 128], nT[:, kc, 128 * t_:128 * t_ + 128], wc[:, kc, 256:384],
                           kc == 0, kc == 7, [wck, "nT%d" % t_], [psk[bk]])
                cp("dve", Vaug[:, 4 * c:4 * c + 4, :, 0:64], pv4, [psk[bk]], ["Vaug%d" % c])
            for e_ in range(2):
                h = 2 * hp + e_
                for r in range(3):
                    dma(QTh[e_][64 + r:65 + r, :], hmlT[32 * r + h:32 * r + h + 1, :], ["hml%d" % r, "QTh_neg%d" % e_], ["C3_%d" % e_], "d_C3_%d" % e_)
                    dma(KTh[e_][67 + r:68 + r, :], hmlT[32 * r + h:32 * r + h + 1, :], ["hml%d" % r, "KTh_ones%d" % e_], ["C3k_%d" % e_], "d_C3k_%d" % e_)
            steps = []
            units = {}
            for I in range(4):
                njs = 4 * I + 4
                for j in range(njs):
                    for e_ in range(2):
                        unit = units.setdefault((e_, I), {})
                        def front(st={}, e_=e_, I=I, j=j, hp=hp):
                            h = 2 * hp + e_
                            pb = 64 * e_
                            c0 = max(0, 128 * (j - 4 * I))
                            sbk = bank("mm")
                            q0 = 512 * I + c0
                            mm(ps[sbk][:, c0:512], KTh[e_][0:70, 128 * j:128 * j + 128], QTh[e_][0:70, q0:512 * I + 512],
                               True, True, ["KT%d_%d" % (j // 4, e_), "QT%d_%d" % (I, e_), "KTh_ones%d" % e_, "QTh_neg%d" % e_,
                                            "C3_%d" % e_, "C3k_%d" % e_], [psk[sbk]])
                            pt = PT[ptr[0] % 4]
                            ptk = "PT%d" % (ptr[0] % 4)
                            ptr[0] += 1
                            act(pt[:, c0:512], ps[sbk][:, c0:512], AF.Exp, [psk[sbk]], [ptk], scale=0.125)
                            if j >= 4 * I:
                                tt("pool", pt[:, c0:c0 + 128], pt[:, c0:c0 + 128], tri[:], ALU.mult, [ptk, "tri"], [ptk])
                            st["pt"], st["ptk"], st["c0"] = pt, ptk, c0

                        def back(st=front.__defaults__[0], unit=unit, e_=e_, I=I, j=j, njs=njs, hp=hp):
                            pb = 64 * e_
                            if "ab" not in unit:
                                unit["ab"] = bank("acc")
                            ab = unit["ab"]
                            pt, ptk, c0 = st["pt"], st["ptk"], st["c0"]
                            mm(ps[ab][:, c0:512], Vaug[:, j, e_, :], pt[:, c0:512], j == 0, j == njs - 1,
                               ["Vaug%d" % (j // 4), "Vaug_ones", ptk], [psk[ab]])
                            if j == njs - 1:
                                rzz = rz[rzr[0] % 2]
                                rzk = "rz%d" % (rzr[0] % 2)
                                rzr[0] += 1
                                act(rzz[0:64, :], ps[ab][64:128, :], AF.Ln, [psk[ab]], [rzk])
                                act(rzz[0:64, :], rzz[0:64, :], AF.Exp, [rzk], [rzk], scale=-1.0)
                                tt("dve", mixT[pb:pb + 64, hp, 512 * I:512 * I + 512], ps[ab][0:64, :], rzz[0:64, :], ALU.mult,
                                   [psk[ab], rzk], ["mixT%d_%d_%d" % (hp, I, e_)])
                        steps.append((front, back))
            pipeline(steps)

        if upto >= "D":
            bar()
            sb.reset(AD_END)
            Qp0 = sb.alloc("Qp", [64, 4, 512], BF16)
            QrA0 = sb.alloc("QrA", [96, 4, 512], BF16)
            if "Qp1" not in sb.cache:
                sb.cache["Qp1"] = nc.alloc_sbuf_tensor_at("Qp1", [64, 4, 512], BF16, offset=xbuf_off[0])
                sb.cache["QrA1"] = nc.alloc_sbuf_tensor_at("QrA1", [96, 4, 512], BF16, offset=xbuf_off[1])
            Qp1, QrA1 = sb.cache["Qp1"], sb.cache["QrA1"]
            Qpb = [Qp0, Qp1]
            QrAb = [QrA0, QrA1]
            mb4 = sb.alloc("mb4", [128, 4, 32], BF16)
            KVc = sb.alloc("KVc", [128, T], BF16)
            KsAug = sb.alloc("KsAug", [96, T], BF16)
            KwT = sb.alloc("KwT", [64, T], BF16)
            VsAug = sb.alloc("VsAug", [128, NT, 128], BF16)
            VwAug = sb.alloc("VwAug", [128, NT, 128], BF16)
            hid = sb.alloc("hid", [128, 2, 128], BF16)
            kccT = sb.alloc("kccT", [64, 128], BF16)
            VcAug = sb.alloc("VcAug", [128, 128], BF16)
            U = [sb.alloc("U%d" % i, [128, 512], BF16) for i in range(4)]
            PT = [sb.alloc("PTd%d" % i, [128, 512], BF16) for i in range(4)]
            ropec = [[sb.alloc("rope%d_%d" % (i, k), [128, 512], F32) for k in range(2)] for i in range(2)]
            t12 = [[sb.alloc("t12_%d_%d" % (i, k), [128, 512], F32) for k in range(2)] for i in range(1)]
            accS = sb.alloc("accS", [64, 4, 512], F32)
            rz = [sb.alloc("rzd%d" % i, [64, 512], F32) for i in range(2)]
            fg = [sb.alloc("fg%d" % i, [64, 512], F32) for i in range(2)]
            tmpc = [sb.alloc("tmpc%d" % i, [64, 512], F32) for i in range(2)]
            silu_x = sb.alloc("silu_x", [128, 128], F32)
            silu_e = sb.alloc("silu_e", [128, 128], F32)
            zt = [sb.alloc("zt%d" % i, [128, 4], F32) for i in range(2)]
            impacc = [sb.alloc("impacc%d" % i, [128, 32], F32) for i in range(2)]
            imp2 = [sb.alloc("imp2_%d" % i, [128, 32], F32) for i in range(2)]
            imp3 = [sb.alloc("imp3_%d" % i, [128, 32], F32) for i in range(2)]
            top = [sb.alloc("top%d" % i, [128, 16], F32) for i in range(2)]
            mb = [sb.alloc("mb%d" % i, [128, 32], BF16) for i in range(2)]
            cload(KsAug[64:96, :], cd["blkind"], "KsAug_ind", "pool")
            memset("pool", VsAug[:, :, 64:128], 1.0, ["VsAug_ones"])
            memset("pool", VwAug[:, :, 64:128], 1.0, ["VwAug_ones"])
            memset("pool", VcAug[:, 0:64], 0.0, ["VcAug_zero"])
            memset("pool", VcAug[:, 64:128], 1.0, ["VcAug_ones"])
            ptr = [0]
            cnt2 = [0]
            ropei = [0]
            t12i = [0]

            def load_rope(c):
                i = ropei[0] % 2
                ropei[0] += 1
                dma(ropec[i][0][:], cd["cossin"][0, :, 512 * c:512 * c + 512], [], ["ropeC%d" % i], "d_ropeC%d" % i)
                dma(ropec[i][1][:], cd["cossin"][1, :, 512 * c:512 * c + 512], [], ["ropeS%d" % i], "d_ropeS%d" % i)
                return ropec[i][0], ropec[i][1], "ropeC%d" % i, "ropeS%d" % i

            def proj_fm(w, wk, c0, c, nrows=128):
                bk = bank("mm")
                for kc in range(8):
                    mm(ps[bk][0:nrows, :], w[:, kc, c0:c0 + nrows], nT[:, kc, 512 * c:512 * c + 512], kc == 0, kc == 7,
                       [wk] + nTk(c), [psk[bk]])
                return bk

            def rope_pair(bA, bB, cosT, sinT, ck, sk_):
                i = 0
                t1, t2 = t12[i]
                tt("dve", t1[:], ps[bA][:], cosT[:], ALU.mult, [psk[bA], ck], ["t1_%d" % i])
                tt("dve", t2[:], ps[bB][:], sinT[:], ALU.mult, [psk[bB], sk_], ["t2_%d" % i])
                return t1, t2, ["t1_%d" % i, "t2_%d" % i]

            for g in range(2):
                base_g = 64 + 1536 + 1024 * g
                (wq,), (wqk,) = wload("in", 0, 8, base_g, 512)
                (wkv,), (wkvk,) = wload("in", 0, 8, base_g + 512, 512)
                for c in range(4):
                    cosT, sinT, ck, sk_ = load_rope(c)
                    bk = proj_fm(wkv, wkvk, 0, c)
                    act(KVc[:, 512 * c:512 * c + 512], ps[bk][:], AF.Copy, [psk[bk]], ["KVc%d" % c])
                    bA = proj_fm(wkv, wkvk, 128, c)
                    bB = proj_fm(wkv, wkvk, 256, c)
                    t1, t2, tk = rope_pair(bA, bB, cosT, sinT, ck, sk_)
                    tt("pool", KsAug[0:64, 512 * c:512 * c + 512], t1[0:64, :], t2[0:64, :], ALU.add, tk, ["KsAug%d" % c])
                    tt("pool", KwT[0:64, 512 * c:512 * c + 512], t1[64:128, :], t2[64:128, :], ALU.add, tk, ["KwT%d" % c])
                    bk = bank("mm")
                    pv4 = ps[bk][:].rearrange("p (a d) -> p a d", a=4)
                    for a in range(4):
                        t_ = 4 * c + a
                        for kc in range(8):
                            mm(ps[bk][:, 128 * a:128 * a + 128], nT[:, kc, 128 * t_:128 * t_ + 128], wkv[:, kc, 384:512],
                               kc == 0, kc == 7, [wkvk, "nT%d" % t_], [psk[bk]])
                    cp("dve", VsAug[:, 4 * c:4 * c + 4, 0:64], pv4[:, :, 0:64], [psk[bk]], ["VsAug%d" % c])
                    cp("dve", VwAug[:, 4 * c:4 * c + 4, 0:64], pv4[:, :, 64:128], [psk[bk]], ["VwAug%d" % c])
                KVck = ["KVc%d" % c for c in range(4)]
                if DSTOP <= 1:
                    continue
                for s in range(2):
                    rows = slice(64 * s, 64 * s + 64)
                    bk = bank("mm")
                    for l in range(32):
                        mm(ps[bk][:, 0:127], Wc1[rows, l, :], KVc[rows, l:l + 16 * 126 + 1:16], l == 0, l == 31,
                           ["Wc1a", "Wc1b"] + KVck, [psk[bk]])
                    act(silu_e[:, 0:127], ps[bk][:, 0:127], AF.Exp, [psk[bk], "nhb%d" % s], ["silu_e"],
                        bias=nhb[:, s:s + 1], scale=-1.0)
                    ts("dve", silu_x[:, 0:127], ps[bk][:, 0:127], hb[:, s:s + 1], ALU.add, [psk[bk], "hb%d" % s], ["silu_x"])
                    ts("dve", silu_e[:, 0:127], silu_e[:, 0:127], 1.0, ALU.add, ["silu_e"], ["silu_e"])
                    P.op("dve", lambda e: e.reciprocal(out=silu_e[:, 0:127], in_=silu_e[:, 0:127]), ["silu_e"], ["silu_e"])
                    tt("dve", hid[:, s, 0:127], silu_x[:, 0:127], silu_e[:, 0:127], ALU.mult, ["silu_x", "silu_e"], ["hid%d" % s])
                bk = bank("mm")
                mm(ps[bk][0:64, 0:127], Wc2[:, 0:64], hid[:, 0, 0:127], True, True, ["Wc2", "hid0"], [psk[bk]])
                act(kccT[0:64, 0:127], ps[bk][0:64, 0:127], AF.Copy, [psk[bk]], ["kccT"])
                bk = bank("mm")
                mm(ps[bk][0:127, 0:64], hid[:, 1, 0:127], Wc2[:, 64:128], True, True, ["Wc2", "hid1"], [psk[bk]])
                cp("dve", VcAug[0:127, 0:64], ps[bk][0:127, 0:64], [psk[bk], "VcAug_zero"], ["VcAug"])

                if DSTOP <= 2:
                    continue
                def qproj(I):
                    par = I % 2
                    Qp, QrA = Qpb[par], QrAb[par]
                    cosT, sinT, ck, sk_ = load_rope(I)
                    for mt in range(2):
                        bA = proj_fm(wq, wqk, 128 * mt, I)
                        bB = proj_fm(wq, wqk, 256 + 128 * mt, I)
                        act(Qp[0:64, 2 * mt, :], ps[bA][0:64, :], AF.Copy, [psk[bA]], ["Qp%d_%d" % (2 * mt, par)])
                        act(Qp[0:64, 2 * mt + 1, :], ps[bA][64:128, :], AF.Copy, [psk[bA]], ["Qp%d_%d" % (2 * mt + 1, par)])
                        t1, t2, tk = rope_pair(bA, bB, cosT, sinT, ck, sk_)
                        tt("dve", QrA[0:64, 2 * mt, :], t1[0:64, :], t2[0:64, :], ALU.add, tk, ["Qr%d_%d" % (2 * mt, par)])
                        tt("dve", QrA[0:64, 2 * mt + 1, :], t1[64:128, :], t2[64:128, :], ALU.add, tk, ["Qr%d_%d" % (2 * mt + 1, par)])

                if DSTOP <= 2:
                    continue
                qproj(0)
                for I in range(4):
                    par = I % 2
                    Qp, QrA = Qpb[par], QrAb[par]

                    def combine(ab, r, br, first, last, I=I, g=g):
                        h = 4 * g + r
                        i = cnt2[0] % 2
                        cnt2[0] += 1
                        if only_br is not None and br != only_br:
                            return
                        act(rz[i][:], ps[ab][64:128, :], AF.Ln, [psk[ab], "tiny_t"], ["rzd%d" % i], bias=tiny_t[0:64, 0:1])
                        dst = mixT[64 * (h % 2):64 * (h % 2) + 64, 4 + h // 2, 512 * I:512 * I + 512]
                        dk = "mixT%d_%d_%d" % (4 + h // 2, I, h % 2)
                        if only_br is not None:
                            act(rz[i][:], rz[i][:], AF.Exp, ["rzd%d" % i], ["rzd%d" % i], scale=-1.0)
                            tt("dve", dst, ps[ab][0:64, :], rz[i][:], ALU.mult, [psk[ab], "rzd%d" % i], [dk])
                            return
                        mm(ps[6][0:64, :], sel[0:56, 3 * h + br, :], GT[0:56, 512 * I:512 * I + 512], True, True,
                           ["sel", "GT"], [psk[6]])
                        tt("dve", rz[i][:], ps[6][0:64, :], rz[i][:], ALU.add, [psk[6], "rzd%d" % i], ["rzd%d" % i])
                        act(fg[i][:], rz[i][:], AF.Exp, ["rzd%d" % i], ["fg%d" % i], scale=-1.0)
                        fgi, fgk = fg[i], "fg%d" % i
                        if first:
                            tt("dve", accS[:, r, :], ps[ab][0:64, :], fgi[:], ALU.mult, [psk[ab], fgk], ["accS%d" % r])
                        elif not last:
                            tt("dve", tmpc[i][:], ps[ab][0:64, :], fgi[:], ALU.mult, [psk[ab], fgk], ["tmpc%d" % i])
                            tt("pool", accS[:, r, :], accS[:, r, :], tmpc[i][:], ALU.add, ["accS%d" % r, "tmpc%d" % i], ["accS%d" % r])
                        else:
                            tt("dve", tmpc[i][:], ps[ab][0:64, :], fgi[:], ALU.mult, [psk[ab], fgk], ["tmpc%d" % i])
                            tt("dve", dst, accS[:, r, :], tmpc[i][:], ALU.add, ["accS%d" % r, "tmpc%d" % i], [dk])

                    if DSTOP <= 3:
                        continue
                    impb = {}

                    def emit_imp(half, I=I):
                        ib = 7
                        impb[half] = ib
                        iv4 = ps[ib][:].rearrange("p (a r c) -> p a r c", a=2, r=4)
                        for r in range(4):
                            for a2 in range(2):
                                a = 2 * half + a2
                                mm(iv4[:, a2, r, 0:33], U[r][0:127, 128 * a:128 * a + 128], ovz[0:127, 0:33], True, True,
                                   ["U%d" % r, "ovz"], [psk[ib]])

                    def emit_topk(a, I=I):
                        half, a2 = a // 2, a % 2
                        ib = impb[half]
                        iv = ps[ib][:].rearrange("p (a r c) -> p a r c", a=2, r=4)[:, a2, :, :]
                        t_ = 4 * I + a
                        i = a % 2
                        ts("dve", zt[i][:], iv[:, :, 32], 1e-30, ALU.max, [psk[ib]], ["zt%d" % i])
                        P.op("dve", lambda e, o=zt[i][:]: e.reciprocal(out=o, in_=o), ["zt%d" % i], ["zt%d" % i])
                        ts("dve", impacc[i][:], iv[:, 0, 0:32], zt[i][:, 0:1], ALU.mult, [psk[ib], "zt%d" % i], ["impacc%d" % i])
                        for r in range(1, 4):
                            stt("dve", impacc[i][:], iv[:, r, 0:32], zt[i][:, r:r + 1], impacc[i][:], ALU.mult, ALU.add,
                                [psk[ib], "zt%d" % i, "impacc%d" % i], ["impacc%d" % i])
                        tt("pool", imp2[i][:], impacc[i][:], keep[:, t_, :], ALU.mult, ["impacc%d" % i, "keep"], ["imp2_%d" % i])
                        tt("pool", imp2[i][:], imp2[i][:], forced[:, t_, :], ALU.add, ["imp2_%d" % i, "forced"], ["imp2_%d" % i])
                        P.op("dve", lambda e, o=top[i][:, 0:8], x_=imp2[i][:]: e.max(out=o, in_=x_), ["imp2_%d" % i], ["topa%d" % i])
                        P.op("dve", lambda e, o=imp3[i][:], a_=top[i][:, 0:8], x_=imp2[i][:]:
                             e.match_replace(out=o, in_to_replace=a_, in_values=x_, imm_value=-1e30),
                             ["imp2_%d" % i, "topa%d" % i], ["imp3_%d" % i])
                        P.op("dve", lambda e, o=top[i][:, 8:16], x_=imp3[i][:]: e.max(out=o, in_=x_), ["imp3_%d" % i], ["topb%d" % i])
                        ts("dve", mb4[:, a, :], imp2[i][:], top[i][:, 15:16], ALU.is_lt, ["imp2_%d" % i, "topb%d" % i], ["mb4_%d" % a],
                           s2=NEGB, op1=ALU.mult)

                    steps = []
                    for r in range(4):
                        def front(st={}, r=r, I=I, par=par, Qp=Qp):
                            sbk = bank("mm")
                            mm(ps[sbk][0:127, :], kccT[0:64, 0:127], Qp[0:64, r, :], True, True, ["kccT", "Qp%d_%d" % (r, par)], [psk[sbk]])
                            act(U[r][0:127, :], ps[sbk][0:127, :], AF.Exp, [psk[sbk]], ["U%d" % r], scale=0.125)
                            tt("pool", U[r][0:127, :], U[r][0:127, :], validT[0:127, 512 * I:512 * I + 512], ALU.mult,
                               ["U%d" % r, "validT"], ["U%d" % r])

                        def back(r=r, I=I):
                            ab = bank("acc")
                            mm(ps[ab][:, :], VcAug[0:127, :], U[r][0:127, :], True, True, ["VcAug", "VcAug_ones", "U%d" % r], [psk[ab]])
                            combine(ab, r, 0, True, False)
                            if r == 3 and I >= 2 and DSTOP > 4:
                                emit_imp(0)
                        steps.append((front, back))
                    if DSTOP > 6:
                      for r in range(4):
                        unit = {}
                        jlo = max(0, 4 * I - 4)
                        first_j = 4 * I - 1 if I > 0 else 0
                        js = [first_j] + [j for j in range(jlo, 4 * I + 4) if j != first_j]
                        for n_, j in enumerate(js):
                            def front(st={}, r=r, I=I, j=j, par=par, QrA=QrA):
                                qlo = max(j, 4 * I)
                                qhi = min(j + 4, 4 * I + 3)
                                ca = 128 * (qlo - 4 * I)
                                cb = 128 * (qhi - 4 * I + 1)
                                sbk = bank("mm")
                                mm(ps[sbk][:, ca:cb], KwT[0:64, 128 * j:128 * j + 128], QrA[0:64, r, ca:cb], True, True,
                                   ["KwT%d" % (j // 4), "Qr%d_%d" % (r, par)], [psk[sbk]])
                                pt = PT[ptr[0] % 4]
                                ptk = "PTd%d" % (ptr[0] % 4)
                                ptr[0] += 1
                                act(pt[:, ca:cb], ps[sbk][:, ca:cb], AF.Exp, [psk[sbk]], [ptk], scale=0.125)
                                if qlo == j:
                                    tt("pool", pt[:, ca:ca + 128], pt[:, ca:ca + 128], tri[:], ALU.mult, [ptk, "tri"], [ptk])
                                if qhi == j + 4:
                                    tt("pool", pt[:, cb - 128:cb], pt[:, cb - 128:cb], atri[:], ALU.mult, [ptk, "atri"], [ptk])
                                st["pt"], st["ptk"], st["ca"], st["cb"] = pt, ptk, ca, cb

                            def back(st=front.__defaults__[0], unit=unit, r=r, j=j, n_=n_, nj=len(js), I=I):
                                if "ab" not in unit:
                                    unit["ab"] = bank("acc")
                                ab = unit["ab"]
                                pt, ptk, ca, cb = st["pt"], st["ptk"], st["ca"], st["cb"]
                                mm(ps[ab][:, ca:cb], VwAug[:, j, :], pt[:, ca:cb], n_ == 0, n_ == nj - 1,
                                   ["VwAug%d" % (j // 4), "VwAug_ones", ptk], [psk[ab]])
                                if n_ == nj - 1:
                                    combine(ab, r, 2, False, False)
                                    if I >= 2 and DSTOP > 4:
                                        if r == 2:
                                            emit_imp(1)
                                        for a in {0: (0,), 1: (1,), 2: (2, 3), 3: ()}[r]:
                                            emit_topk(a)
                            steps.append((front, back))
                    pipeline(steps)
                    if I < 3:
                        qproj(I + 1)
                    if I >= 2:
                        mbv = ps[6][:].bitcast(BF16)
                        for a in range(4):
                            P.op("pe", lambda e, o=mbv[0:32, 128 * a:128 * a + 128], x_=mb4[:, a, :]:
                                 e.transpose(out=o, in_=x_, identity=ident_bf[:]), ["mb4_%d" % a, "ident_bf"], [psk[6]])
                        for r in range(4):
                            cp("dve", QrA[64:96, r, :], mbv[0:32, 0:512], [psk[6]], ["QrM%d_%d" % (r, par)])
                        if dbg and b == 0:
                            dma(dbg_d["mb"][g, I], QrA[64:96, 0, :], ["QrM0_%d" % par], ["dbg_mb%d%d" % (g, I)], "d_dbg_mb")
                    else:
                        for r in range(4):
                            memset("pool", QrA[64:96, r, :], 0.0, ["QrM%d_%d" % (r, par)])
                    if DSTOP <= 5:
                        continue
                    steps = []
                    for r in range(4):
                        unit = {}
                        njs = 4 * I + 4
                        for j in range(njs):
                            def front(st={}, r=r, I=I, j=j, par=par, QrA=QrA):
                                c0 = max(0, 128 * (j - 4 * I))
                                sbk = bank("mm")
                                mm(ps[sbk][:, c0:512], KsAug[0:96, 128 * j:128 * j + 128], QrA[0:96, r, c0:512], True, True,
                                   ["KsAug%d" % (j // 4), "KsAug_ind", "Qr%d_%d" % (r, par), "QrM%d_%d" % (r, par)], [psk[sbk]])
                                pt = PT[ptr[0] % 4]
                                ptk = "PTd%d" % (ptr[0] % 4)
                                ptr[0] += 1
                                act(pt[:, c0:512], ps[sbk][:, c0:512], AF.Exp, [psk[sbk]], [ptk], scale=0.125)
                                if j >= 4 * I:
                                    tt("pool", pt[:, c0:c0 + 128], pt[:, c0:c0 + 128], tri[:], ALU.mult, [ptk, "tri"], [ptk])
                                st["pt"], st["ptk"], st["c0"] = pt, ptk, c0

                            def back(st=front.__defaults__[0], unit=unit, r=r, j=j, njs=njs):
                                if "ab" not in unit:
                                    unit["ab"] = bank("acc")
                                ab = unit["ab"]
                                pt, ptk, c0 = st["pt"], st["ptk"], st["c0"]
                                mm(ps[ab][:, c0:512], VsAug[:, j, :], pt[:, c0:512], j == 0, j == njs - 1,
                                   ["VsAug%d" % (j // 4), "VsAug_ones", ptk], [psk[ab]])
                                if j == njs - 1:
                                    combine(ab, r, 1, False, True)
                            steps.append((front, back))
                    pipeline(steps)

        if upto >= "E":
            bar()
            sb.reset(P_END)
            KmT = sb.alloc("KmT", [128, 8, 256], BF16)
            Vm = sb.alloc("Vm", [128, 2, D], BF16)
            gbc = sb.alloc("gbc", [128, 3, D], F32)
            mT = sb.alloc("mT", [128, 8, 256], BF16)
            hbuf = sb.alloc("hbuf", [128, 4, D], F32)
            nT2 = sb.alloc("nT2", [128, 8, 512], BF16)
            hT = sb.alloc("hT", [128, 32, 512], BF16)
            PTe = [sb.alloc("PTe%d" % i, [128, 512], BF16) for i in range(4)]
            rze = [sb.alloc("rze%d" % i, [128, 512], F32) for i in range(2)]
            tmpe = [sb.alloc("tmpe%d" % i, [128, D], F32) for i in range(2)]
            relu_t = [sb.alloc("relu%d" % i, [128, 512], F32) for i in range(2)]
            ssq = [sb.alloc("ssq%d" % i, [128, 4], F32) for i in range(2)]
            gpre = sb.alloc("gpre", [128, 2, D], F32)
            npi = [0]
            nti = [0]

            def norm_T2(src, srck, gi, dst, dstk):
                i = nti[0] % 2
                nti[0] += 1
                gf = None
                norm_T(src, srck, gi, dst, dstk, xn[i][:], "xn%d" % i, ssb[i], "ss%d" % i, junk[:], "junk", gfree=gf)

            def norm_post(b0, b1, gi, resid, residk, out, outk, srcs=None):
                i = npi[0] % 2
                npi[0] += 1
                sq = ssq[i]
                sk = "ssq%d" % i
                if srcs is None:
                    s0, s0k, s1, s1k = ps[b0][:], psk[b0], ps[b1][:], psk[b1]
                else:
                    s0, s0k, s1, s1k = srcs
                act(junk[:, 0:512], s0, AF.Square, [s0k], [sk + "a"], accum=sq[:, 0:1])
                act(junk[:, 512:1024], s1, AF.Square, [s1k], [sk + "b"], accum=sq[:, 1:2])
                tt("dve", sq[:, 2:3], sq[:, 0:1], sq[:, 1:2], ALU.add, [sk + "a", sk + "b"], [sk + "c"])
                act(sq[:, 3:4], sq[:, 2:3], AF.Ln, [sk + "c", "eps_t"], [sk + "r"], bias=eps_t[:, 0:1], scale=1.0 / D)
                act(sq[:, 3:4], sq[:, 3:4], AF.Exp, [sk + "r"], [sk + "r"], scale=-0.5)
                tk = "tmpe%d" % i
                stt("dve", tmpe[i][:, 0:512], s0, sq[:, 3:4], gbc[:, gi, 0:512], ALU.mult, ALU.mult,
                    [s0k, sk + "r", "gbc%d" % gi], [tk + "a"])
                stt("dve", tmpe[i][:, 512:1024], s1, sq[:, 3:4], gbc[:, gi, 512:1024], ALU.mult, ALU.mult,
                    [s1k, sk + "r", "gbc%d" % gi], [tk + "b"])
                tt("dve", out, tmpe[i][:], resid, ALU.add, [tk + "a", tk + "b", residk], [outk])

            for i in range(3):
                dma(gbc[:, i, :], gbc_d[i], [], ["gbc%d" % i], "d_gbc%d" % i)
            for i in range(2):
                dma(gpre[:, i, :], gpre_d[i], [], ["gpre%d" % i], "d_gpre%d" % i)
            for m_ in range(2):
                xb = xbuf[m_ % 2]
                xk = "xbuf%d" % (m_ % 2)
                dma(xb[:], mem_d[b, 128 * m_:128 * m_ + 128, :], [], [xk], "d_" + xk)
                norm_T2(xb[:], xk, 2, mT[:, :, 128 * m_:128 * m_ + 128], "mT%d" % m_)
            mTk = ["mT0", "mT1"]
            for half in range(2):
                (w,), (wk,) = wload("xkv", 0, 8, 512 * half, 512)
                for m4 in range(4):
                    bk = bank("mm")
                    for kc in range(8):
                        mm(ps[bk][:, 0:256], w[:, kc, 128 * m4:128 * m4 + 128], mT[:, kc, :], kc == 0, kc == 7, [wk] + mTk, [psk[bk]])
                    act(KmT[:, 4 * half + m4, :], ps[bk][:, 0:256], AF.Copy, [psk[bk]], ["KmT%d" % (4 * half + m4)])
            for half in range(2):
                (w,), (wk,) = wload("xkv", 0, 8, 1024 + 512 * half, 512)
                for kb in range(2):
                    bk = bank("mm")
                    for kc in range(8):
                        mm(ps[bk][:], mT[:, kc, 128 * kb:128 * kb + 128], w[:, kc, :], kc == 0, kc == 7, [wk, "mT%d" % kb], [psk[bk]])
                    act(Vm[:, kb, 512 * half:512 * half + 512], ps[bk][:], AF.Copy, [psk[bk]], ["Vm%d_%d" % (kb, half)])
            Vmk = ["Vm%d_%d" % (kb, half) for kb in range(2) for half in range(2)]

            pte = [0]
            for ci in range(4):
                mixk = [k for k in P.allkeys if isinstance(k, str) and k.startswith("mixT") and k.split("_")[1] == str(ci)]
                (w0,), (w0k,) = wload("out", 0, 8, 0, 512)
                (w1,), (w1k,) = wload("out", 0, 8, 512, 512)
                for a in range(4):
                    t_ = 4 * ci + a
                    xb = xbuf[a % 2]
                    xk = "xbuf%d" % (a % 2)
                    dma(xb[:], x_d[b, 128 * t_:128 * t_ + 128, :], [], [xk], "d_" + xk)
                    bks = []
                    for w, wk in ((w0, w0k), (w1, w1k)):
                        bk = bank("mm4")
                        for kc in range(8):
                            mm(ps[bk][:], mixT[:, kc, 128 * t_:128 * t_ + 128], w[:, kc, :], kc == 0, kc == 7,
                               [wk] + [k for k in mixk if k.startswith("mixT%d_" % kc)], [psk[bk]])
                        bks.append(bk)
                    norm_post(bks[0], bks[1], 0, xb[:], xk, hbuf[:, a, :], "h%d" % a)
                for a in range(4):
                    norm_T2(hbuf[:, a, :], "h%d" % a, 1, nT2[:, :, 128 * a:128 * a + 128], "nT2_%d" % a)
                nT2k = ["nT2_%d" % a for a in range(4)]
                for half in range(2):
                    (w,), (wk,) = wload("xq", 0, 8, 512 * half, 512)
                    for m4 in range(4):
                        mt = 4 * half + m4
                        bk = bank("mm")
                        for kc in range(8):
                            mm(ps[bk][:], w[:, kc, 128 * m4:128 * m4 + 128], nT2[:, kc, :], kc == 0, kc == 7, [wk] + nT2k, [psk[bk]])
                        act(hT[:, mt, :], ps[bk][:], AF.Copy, [psk[bk]], ["hT%d" % mt])
                ptsd = {}

                def e4_front(hh):
                    pts = []
                    for kb in range(2):
                        sbk = bank("mm")
                        for dc in range(2):
                            mm(ps[sbk][:], KmT[:, 2 * hh + dc, 128 * kb:128 * kb + 128], hT[:, 2 * hh + dc, :], dc == 0, dc == 1,
                               ["KmT%d" % (2 * hh + dc), "hT%d" % (2 * hh + dc)], [psk[sbk]])
                        pt = PTe[pte[0] % 4]
                        ptk = "PTe%d" % (pte[0] % 4)
                        pte[0] += 1
                        act(pt[:], ps[sbk][:], AF.Exp, [psk[sbk]], [ptk], scale=1.0 / 16.0)
                        pts.append((pt, ptk))
                    ptsd[hh] = pts

                def e4_back(hh):
                    pts = ptsd[hh]
                    obs = [bank("acc"), bank("acc")]
                    zb = 6 + (hh % 2)
                    for mo in range(2):
                        for kb in range(2):
                            mm(ps[obs[mo]][:], Vm[:, kb, 256 * hh + 128 * mo:256 * hh + 128 * mo + 128], pts[kb][0][:], kb == 0, kb == 1,
                               Vmk + [pts[kb][1]], [psk[obs[mo]]])
                    for kb in range(2):
                        mm(ps[zb][:], ones_bf[:, :], pts[kb][0][:], kb == 0, kb == 1, ["ones_bf", pts[kb][1]], [psk[zb]])
                    rzi = rze[hh % 2]
                    rzk = "rze%d" % (hh % 2)
                    act(rzi[:], ps[zb][:], AF.Ln, [psk[zb]], [rzk])
                    act(rzi[:], rzi[:], AF.Exp, [rzk], [rzk], scale=-1.0)
                    for mo in range(2):
                        tt("dve", hT[:, 8 + 2 * hh + mo, :], ps[obs[mo]][:], rzi[:], ALU.mult, [psk[obs[mo]], rzk], ["hT%d" % (8 + 2 * hh + mo)])

                e4_front(0)
                for hh in range(4):
                    if hh + 1 < 4:
                        e4_front(hh + 1)
                    e4_back(hh)
                (w0,), (w0k,) = wload("xo", 0, 8, 0, 512)
                (w1,), (w1k,) = wload("xo", 0, 8, 512, 512)
                for a in range(4):
                    bks = []
                    for w, wk in ((w0, w0k), (w1, w1k)):
                        bk = bank("mm4")
                        for kc in range(8):
                            mm(ps[bk][:], hT[:, 8 + kc, 128 * a:128 * a + 128], w[:, kc, :], kc == 0, kc == 7,
                               [wk, "hT%d" % (8 + kc)], [psk[bk]])
                        bks.append(bk)
                    norm_post(bks[0], bks[1], 1, hbuf[:, a, :], "h%d" % a, hbuf[:, a, :], "h%d" % a)
                for a in range(4):
                    norm_T2(hbuf[:, a, :], "h%d" % a, 3, nT2[:, :, 128 * a:128 * a + 128], "nT2_%d" % a)
                for hc in range(8):
                    (w,), (wk,) = wload("up", 0, 8, 512 * hc, 512)
                    for m4 in range(4):
                        m_ = 4 * hc + m4
                        bk = bank("mm")
                        for kc in range(8):
                            mm(ps[bk][:], w[:, kc, 128 * m4:128 * m4 + 128], nT2[:, kc, :], kc == 0, kc == 7, [wk] + nT2k, [psk[bk]])
                        ri = m_ % 2
                        act(relu_t[ri][:], ps[bk][:], AF.Relu, [psk[bk]], ["relu%d" % ri])
                        tt("pool", hT[:, m_, :], relu_t[ri][:], relu_t[ri][:], ALU.mult, ["relu%d" % ri], ["hT%d" % m_])
                for hkg in range(4):
                    (w0, w1), (w0k, w1k) = wload("down", 8 * hkg, 8, 0, 1024)
                    for a in range(4):
                        for half, (w, wk) in enumerate(((w0, w0k), (w1, w1k))):
                            bk = 2 * a + half
                            for k8 in range(8):
                                hk = 8 * hkg + k8
                                mm(ps[bk][:], hT[:, hk, 128 * a:128 * a + 128], w[:, k8, :], hkg == 0 and k8 == 0, hkg == 3 and k8 == 7,
                                   [wk, "hT%d" % hk], [psk[bk]])
                for a in range(4):
                    t_ = 4 * ci + a
                    norm_post(2 * a, 2 * a + 1, 2, hbuf[:, a, :], "h%d" % a, hbuf[:, a, :], "h%d" % a)
                    yk = "y%d_%d" % (b, t_)
                    dma(y_d[b, 128 * t_:128 * t_ + 128, :], hbuf[:, a, :], ["h%d" % a], [yk], "d_y%d" % a)
                    ykeys.append(yk)

    P.op("sp", None, reads=ykeys)
    if dbg:
        mk = [k for k in P.allkeys if str(k).startswith("mixT")]
        dma(dbg_d["mixT"], mixT[:], mk, ["dbg_mixT"], "d_dbg_mixT")
        P.op("sp", None, reads=["dbg_mixT", "dbg_nT", "dbg_cpos"])
    P.finish()
    return nc, P


def prep_shared(inputs):
    f = lambda k: np.ascontiguousarray(np.asarray(inputs[k], np.float32)[0])
    w_in = f("w_in")
    idx = w_in_index()
    w_ext = np.zeros((D, NCOLS), np.float32)
    m = idx >= 0
    w_ext[:, m] = w_in[:, idx[m]]
    sh = {
        "w_in_ext": w_ext,
        "w_out": f("w_mix_out"), "w_xq": f("w_xq"), "w_xkv": f("w_xkv"), "w_xo": f("w_xo"),
        "w_up": f("w_up"), "w_down": f("w_down"),
        "w_c1": np.ascontiguousarray(np.concatenate([f("w_ck1"), f("w_cv1")], 0)),
        "w_c2": np.ascontiguousarray(np.concatenate([f("w_ck2"), f("w_cv2")], 1)),
        "peT": np.ascontiguousarray(np.concatenate([f("pe_k").T, f("pe_v").T], 0)),
        "b_forget": np.ascontiguousarray(f("b_forget").reshape(8, 1)),
    }
    gc = np.zeros((128, 32), np.float32)
    for i, k in enumerate(("g_mix_pre", "g_x_pre", "g_mem", "g_mlp_pre")):
        gc[:, 8 * i:8 * i + 8] = f(k).reshape(8, 128).T
    sh["gcols"] = gc
    sh["gbc"] = np.ascontiguousarray(np.stack([np.broadcast_to(f(k)[None, :], (128, D))
                                               for k in ("g_mix_post", "g_x_post", "g_mlp_post")], 0))
    sh["gpre"] = np.ascontiguousarray(np.stack([np.broadcast_to(f(k)[None, :], (128, D))
                                                for k in ("g_x_pre", "g_mlp_pre")], 0))
    for k, v in host_consts().items():
        sh["c_" + k] = np.ascontiguousarray(v)
    return sh


def kernel(**inputs):
    ncores = 8
    x = np.asarray(inputs["x"], np.float32)
    mem = np.asarray(inputs["mem"], np.float32)
    nseq = x.shape[0] // ncores
    sh = prep_shared(inputs)
    nc, P = build(nseq)
    in_maps = []
    for c in range(ncores):
        m = dict(sh)
        m["x"] = np.ascontiguousarray(x[c * nseq:(c + 1) * nseq])
        m["mem"] = np.ascontiguousarray(mem[c * nseq:(c + 1) * nseq])
        in_maps.append(m)
    res = run_bass_kernel_spmd(nc, in_maps, core_ids=list(range(ncores)))
    return np.concatenate([np.asarray(r["y"], np.float32) for r in res.results], axis=0)
```

```python
import contextlib
import numpy as np
import concourse.bass as bass
import concourse.mybir as mybir
from concourse.bass_utils import run_bass_kernel_spmd

F32 = mybir.dt.float32
BF16 = mybir.dt.bfloat16
AF = mybir.ActivationFunctionType
ALU = mybir.AluOpType

ENGS = ("pe", "act", "dve", "pool", "sp")
T = 2048
D = 1024
NT = 16
EPS = 1e-6
NEGB = -240000.0
NCOLS = 64 + 1536 + 2048
import os as _os
DSTOP = int(_os.environ.get("DSTOP", "99"))
FINAL_ENG = _os.environ.get("FINAL_ENG", "dve")


class Op:
    __slots__ = ("eng", "fn", "reads", "writes", "dsem", "waits", "inc", "epoch")

    def __init__(self, eng, fn, reads, writes, dsem, epoch):
        self.eng = eng
        self.fn = fn
        self.reads = tuple(reads)
        self.writes = tuple(writes)
        self.dsem = dsem
        self.waits = {}
        self.inc = None
        self.epoch = epoch


class Prog:
    def __init__(self, nc):
        self.nc = nc
        self.ops = []
        self.epoch = 0
        self.allkeys = set()
        self.bar_keys = ()

    def op(self, eng, fn, reads=(), writes=(), dsem=None):
        assert eng in ENGS
        reads = tuple(reads) + self.bar_keys
        self.allkeys.update(reads)
        self.allkeys.update(writes)
        self.ops.append(Op(eng, fn, reads, writes, dsem, self.epoch))

    def barrier(self, fns):
        keys = tuple(sorted(self.allkeys, key=str))
        n = len([o for o in self.ops if o.fn is not None])
        newbar = tuple("BAR_%s_%d" % (e, n) for e in ("pe", "act", "dve", "pool"))
        for e, k in zip(("pe", "act", "dve", "pool"), newbar):
            self.ops.append(Op(e, fns[e], keys, (k,), None, self.epoch))
        self.allkeys = set(newbar)
        self.bar_keys = newbar

    def finish(self):
        nc = self.nc
        ops = self.ops
        last_w = {}
        readers = {}
        deps_of = []
        needed = set()
        for i, op in enumerate(ops):
            deps = set()
            is_dma = op.dsem is not None
            for k in op.reads:
                j = last_w.get(k)
                if j is not None:
                    deps.add(j)
                if isinstance(k, str) and k.startswith("ps"):
                    for j in readers.get(k, ()):
                        if ops[j].eng != op.eng:
                            deps.add(j)
            for k in op.writes:
                j = last_w.get(k)
                if j is not None:
                    oj = ops[j]
                    if is_dma or oj.dsem is not None or oj.eng != op.eng or op.eng != "pe":
                        deps.add(j)
                for j in readers.get(k, ()):
                    oj = ops[j]
                    if is_dma or oj.dsem is not None or oj.eng != op.eng or op.eng != "pe":
                        deps.add(j)
            deps.discard(i)
            for k in op.reads:
                readers.setdefault(k, []).append(i)
            for k in op.writes:
                last_w[k] = i
                readers[k] = []
            deps_of.append(deps)
            needed |= deps
        cnt = {}
        token = {}
        sem_names = set()
        for i, op in enumerate(ops):
            if op.dsem is not None:
                cnt[op.dsem] = cnt.get(op.dsem, 0) + 16
                token[i] = (op.dsem, cnt[op.dsem])
                op.inc = (op.dsem, 16)
                sem_names.add(op.dsem)
            elif i in needed:
                assert op.fn is not None
                s = "c_%s_%d" % (op.eng, op.epoch)
                cnt[s] = cnt.get(s, 0) + 1
                token[i] = (s, cnt[s])
                op.inc = (s, 1)
                sem_names.add(s)
        waited = {e: {} for e in ENGS}
        nwaits = 0
        for i, op in enumerate(ops):
            w = {}
            for j in deps_of[i]:
                s, v = token[j]
                if v > w.get(s, 0):
                    w[s] = v
            wd = waited[op.eng]
            for s, v in list(w.items()):
                if wd.get(s, 0) >= v:
                    del w[s]
                else:
                    wd[s] = v
            op.waits = w
            nwaits += len(w)
        self.stats = dict(n_ops=len(ops), n_waits=nwaits, n_sems=len(sem_names),
                          maxcnt=max(cnt.values()) if cnt else 0)
        sems = {}
        with contextlib.ExitStack() as st:
            for s in sorted(sem_names):
                sems[s] = st.enter_context(nc.semaphore(s))
            block = st.enter_context(nc.Block())
            per = {e: [o for o in ops if o.eng == e] for e in ENGS}

            def run(eng, lst):
                for o in lst:
                    for s, v in o.waits.items():
                        eng.wait_ge(sems[s], v)
                    if o.fn is None:
                        continue
                    ins = o.fn(eng)
                    if o.inc is not None:
                        ins.then_inc(sems[o.inc[0]], o.inc[1])

            @block.tensor
            def _(e):
                run(e, per["pe"])

            @block.scalar
            def _(e):
                run(e, per["act"])

            @block.vector
            def _(e):
                run(e, per["dve"])

            @block.gpsimd
            def _(e):
                run(e, per["pool"])

            @block.sync
            def _(e):
                run(e, per["sp"])


def _bf(a):
    import ml_dtypes
    return np.asarray(a, np.float32).astype(ml_dtypes.bfloat16)


def host_consts():
    c = {}
    p = np.arange(128)
    c["ident_f"] = np.eye(128, dtype=np.float32)
    c["tri"] = (p[:, None] <= p[None, :]).astype(np.float32)
    c["atri"] = (p[:, None] > p[None, :]).astype(np.float32)
    half = 32
    inv = (10000.0 ** (-np.arange(half, dtype=np.float32) / half)).astype(np.float32)
    pos = np.arange(T, dtype=np.float32)
    ang = (pos[:, None] * inv[None, :]).astype(np.float32)
    cos = np.cos(ang).astype(np.float32).T
    sin = np.sin(ang).astype(np.float32).T
    cs = np.zeros((2, 128, T), np.float32)
    for base in (0, 64):
        cs[0, base:base + 32] = cos
        cs[0, base + 32:base + 64] = cos
        cs[1, base:base + 32] = -sin
        cs[1, base + 32:base + 64] = sin
    c["cossin"] = cs
    c["blkind"] = (np.arange(T)[None, :] // 64 == np.arange(32)[:, None]).astype(np.float32)
    n = np.arange(128)
    valid = ((16 * n[:, None] + 31) <= np.arange(T)[None, :]) & (n[:, None] < 127)
    c["validT"] = valid.astype(np.float32)
    starts = np.arange(127) * 16
    sel_lo = np.arange(32) * 64
    ov = ((starts[:, None] < sel_lo[None, :] + 64) & (starts[:, None] + 32 > sel_lo[None, :])).astype(np.float32)
    ovz = np.zeros((128, 40), np.float32)
    ovz[:127, :32] = ov
    ovz[:127, 32] = 1.0
    c["ovz"] = ovz
    t = np.arange(T)
    cur = t // 64
    j = np.arange(32)
    keep = ((j[None, :] < cur[:, None] - 1) & (j[None, :] != 0)).astype(np.float32)
    forced = np.zeros((T, 32), np.float32)
    forced[(j[None, :] == 0) | (j[None, :] == cur[:, None] - 1)] = 1e4
    forced[j[None, :] == cur[:, None]] = 2e4
    forced[j[None, :] > cur[:, None]] = -1.0
    c["keep"] = keep.reshape(NT, 128, 32).transpose(1, 0, 2).copy()
    c["forced"] = forced.reshape(NT, 128, 32).transpose(1, 0, 2).copy()
    sel = np.zeros((128, 24, 64), np.float32)
    for r in range(24):
        sel[r, r, :] = 1.0
        sel[32 + r, r, :] = 1.0
    c["sel"] = sel
    return c


def w_in_index():
    idx = []
    idx += list(range(1536, 1544)) + [-1] * 24 + list(range(2824, 2848)) + [-1] * 8
    for hp in range(4):
        idx += list(range(128 * hp, 128 * hp + 128))
        idx += list(range(512 + 128 * hp, 512 + 128 * hp + 128))
        idx += list(range(1024 + 128 * hp, 1024 + 128 * hp + 128))

    def sw(l):
        return l[32:] + l[:32]

    for g in range(2):
        qa, qb = [], []
        for r in range(4):
            h = 4 * g + r
            cols = list(range(1544 + 64 * h, 1544 + 64 * h + 64))
            qa += cols
            qb += sw(cols)
        idx += qa + qb
        kc = list(range(2056 + 64 * g, 2056 + 64 * g + 64))
        vc = list(range(2184 + 64 * g, 2184 + 64 * g + 64))
        ks = list(range(2312 + 64 * g, 2312 + 64 * g + 64))
        vs = list(range(2440 + 64 * g, 2440 + 64 * g + 64))
        kw = list(range(2568 + 64 * g, 2568 + 64 * g + 64))
        vw = list(range(2696 + 64 * g, 2696 + 64 * g + 64))
        idx += kc + vc + ks + kw + sw(ks) + sw(kw) + vs + vw
    assert len(idx) == NCOLS
    return np.array(idx)


class SB:
    def __init__(self, nc, limit):
        self.nc = nc
        self.limit = limit
        self.top = 16512
        self.n = 0
        self.cache = {}

    def alloc(self, name, shape, dt):
        nbytes = int(np.prod(shape[1:])) * (4 if dt == F32 else 2)
        nbytes = (nbytes + 31) // 32 * 32
        off = self.top
        self.top += nbytes
        assert self.top <= self.limit, (name, self.top, self.limit)
        key = (name, off, tuple(shape), str(dt))
        if key not in self.cache:
            self.n += 1
            self.cache[key] = self.nc.alloc_sbuf_tensor_at("%s_%d" % (name, self.n), list(shape), dt, offset=off)
        return self.cache[key]

    def mark(self):
        return self.top

    def reset(self, m):
        self.top = m


def build(nseq, upto="E", dbg=False, only_br=None):
    nc = bass.Bass("TRN2", target_bir_lowering=False)
    P = Prog(nc)

    def din(name, shape, dt=F32):
        return nc.dram_tensor(name, list(shape), dt, kind="ExternalInput").ap()

    x_d = din("x", [nseq, T, D])
    mem_d = din("mem", [nseq, 256, D])
    w_d = {
        "in": din("w_in_ext", [D, NCOLS]),
        "out": din("w_out", [D, D]),
        "xq": din("w_xq", [D, D]),
        "xkv": din("w_xkv", [D, 2 * D]),
        "xo": din("w_xo", [D, D]),
        "up": din("w_up", [D, 4 * D]),
        "down": din("w_down", [4 * D, D]),
        "c1": din("w_c1", [4096, 128]),
    }
    wc2_d = din("w_c2", [128, 128])
    peT_d = din("peT", [128, 32])
    gcols_d = din("gcols", [128, 32])
    gbc_d = din("gbc", [3, 128, D])
    gpre_d = din("gpre", [2, 128, D])
    bf_d = din("b_forget", [8, 1])
    cst = host_consts()
    cd = {k: din("c_" + k, v.shape) for k, v in cst.items()}
    y_d = nc.dram_tensor("y", [nseq, T, D], F32, kind="ExternalOutput").ap()
    dbg_d = {}
    if dbg:
        dbg_d["mixT"] = nc.dram_tensor("dbg_mixT", [128, 8, T], BF16, kind="ExternalOutput").ap()
        dbg_d["nT"] = nc.dram_tensor("dbg_nT", [128, 8, T], BF16, kind="ExternalOutput").ap()
        dbg_d["cpos"] = nc.dram_tensor("dbg_cpos", [8, T], F32, kind="ExternalOutput").ap()
        dbg_d["mb"] = nc.dram_tensor("dbg_mb", [2, 4, 32, 512], BF16, kind="ExternalOutput").ap()
    wbf = {k: nc.dram_tensor("wbf_" + k, list(v.shape), BF16, kind="Internal").ap() for k, v in w_d.items()}

    ps = [nc.alloc_psum_tensor("ps%d" % i, [128, 512], F32) for i in range(8)]
    psk = ["ps%d" % i for i in range(8)]
    rr = {}
    acc_pool = [(4, 5)]

    def bank(pool):
        lst = {"mm": (0, 1, 2, 3), "acc": acc_pool[0], "aux": (6, 7), "mm4": (0, 1, 2, 3, 4, 5, 6, 7)}[pool]
        i = rr.get(pool, 0)
        rr[pool] = i + 1
        return lst[i % len(lst)]

    sb = SB(nc, 229376)
    ident_bf = sb.alloc("ident_bf", [128, 128], BF16)
    ident_f = sb.alloc("ident_f", [128, 128], F32)
    tri = sb.alloc("tri", [128, 128], BF16)
    atri = sb.alloc("atri", [128, 128], BF16)
    ones_bf = sb.alloc("ones_bf", [128, 128], BF16)
    gcols = sb.alloc("gcols", [128, 32], F32)
    negb = sb.alloc("negb", [8, 1], F32)
    bfs = sb.alloc("bfs", [8, 1], F32)
    hb = sb.alloc("hb", [128, 2], F32)
    nhb = sb.alloc("nhb", [128, 2], F32)
    scr = sb.alloc("scr", [128, 8], F32)
    eps_t = sb.alloc("eps_t", [128, 1], F32)
    tiny_t = sb.alloc("tiny_t", [128, 1], F32)
    wslots = [sb.alloc("wslot%d" % i, [128, 8, 512], BF16) for i in range(4)]
    wsk = ["wslot%d" % i for i in range(4)]
    wrr = [0]
    mixT = sb.alloc("mixT", [128, 8, T], BF16)
    xbuf_off = []
    xbuf = []
    for i in range(2):
        xbuf_off.append(sb.top)
        xbuf.append(sb.alloc("xbuf%d" % i, [128, D], F32))
    xn = [sb.alloc("xn%d" % i, [128, D], BF16) for i in range(2)]
    junk = sb.alloc("junk", [128, D], BF16)
    ssb = [sb.alloc("ss%d" % i, [128, 2], F32) for i in range(2)]
    P_END = sb.mark()

    def cload(dst_ap, src_ap, key, eng="sp"):
        P.op(eng, lambda e: e.dma_start(out=dst_ap, in_=src_ap), writes=[key], dsem="d_" + key)

    def bar():
        P.barrier({
            "pe": lambda e: e.matmul(ps[7][0:1, 0:1], lhsT=ones_bf[0:1, 0:1], rhs=ones_bf[0:1, 0:1], start=True, stop=True),
            "act": lambda e: e.activation(out=scr[0:1, 0:1], in_=scr[0:1, 1:2], func=AF.Copy),
            "dve": lambda e: e.memset(scr[0:1, 2:3], 0.0),
            "pool": lambda e: e.memset(scr[0:1, 3:4], 0.0),
        })

    P.op("dve", lambda e: e.memset(scr[:], 0.0), writes=["scr"])
    P.op("pool", lambda e: e.memset(ones_bf[:], 1.0), writes=["ones_bf"])
    P.op("pool", lambda e: e.memset(eps_t[:], EPS), writes=["eps_t"])
    P.op("pool", lambda e: e.memset(tiny_t[:], 1e-18), writes=["tiny_t"])
    cload(ident_f[:], cd["ident_f"], "ident_f")
    cload(ident_bf[:], cd["ident_f"], "ident_bf", "pool")
    cload(tri[:], cd["tri"], "tri", "pool")
    cload(atri[:], cd["atri"], "atri", "pool")
    cload(gcols[:], gcols_d, "gcols")
    cload(bfs[:], bf_d, "bfs")
    P.op("dve", lambda e: e.tensor_scalar_mul(out=negb[:], in0=bfs[:], scalar1=-1.0), reads=["bfs"], writes=["negb"])
    for k in ("c1", "in", "out", "xq", "xkv", "xo", "up", "down"):
        src = w_d[k]
        dst = wbf[k]
        rows, cols = src.shape
        if cols > 1024:
            nsp = (cols + 1023) // 1024
            step = cols // nsp
            assert step * nsp == cols
            for i in range(nsp):
                def f(e, s=src[:, i * step:(i + 1) * step], d=dst[:, i * step:(i + 1) * step]):
                    return e.dma_start(out=d, in_=s)
                P.op("pool", f, writes=["wbf_%s_%d" % (k, i)], dsem="d_wbf_%s_%d" % (k, i))
        else:
            def f(e, s=src, d=dst):
                return e.dma_start(out=d, in_=s)
            P.op("pool", f, writes=["wbf_%s_0" % k], dsem="d_wbf_%s_0" % k)

    def wkeys(k, c0=None, ncol=None):
        cols = w_d[k].shape[1]
        n = (cols + 1023) // 1024 if cols > 1024 else 1
        if c0 is None or n == 1:
            return ["wbf_%s_%d" % (k, i) for i in range(n)]
        step = cols // n
        return ["wbf_%s_%d" % (k, i) for i in range(n) if i * step < c0 + ncol and (i + 1) * step > c0]

    def wview(k):
        return wbf[k].rearrange("(kc p) c -> p kc c", p=128)

    def wload(k, kc0, nkc, c0, ncol, nslots=1):
        assert nkc == 8 and ncol in (512, 1024, 384, 64)
        ns = 2 if ncol == 1024 else 1
        i0 = wrr[0] % 4
        if ns == 2 and i0 % 2 == 1:
            wrr[0] += 1
            i0 = wrr[0] % 4
        wrr[0] += ns
        keys = [wsk[i0 + i] for i in range(ns)]
        src = wview(k)[:, kc0:kc0 + nkc, c0:c0 + ncol]
        aps = []
        for i in range(ns):
            w = min(512, ncol)
            dstt = wslots[i0 + i][:, :, 0:w]
            srci = src[:, :, i * 512:i * 512 + w]

            def f(e, d=dstt, s=srci):
                return e.dma_start(out=d, in_=s)
            P.op("sp", f, reads=wkeys(k, c0 + i * 512, w), writes=[keys[i]], dsem="d_" + keys[i])
            aps.append(dstt)
        return aps, keys

    LA = 3

    def pipeline(steps):
        n = len(steps)
        for i in range(n + LA):
            if i < n:
                steps[i][0]()
            if i >= LA:
                steps[i - LA][1]()

    def mm(out, lhsT, rhs, start, stop, reads, writes):
        P.op("pe", lambda e: e.matmul(out, lhsT=lhsT, rhs=rhs, start=start, stop=stop), reads, writes)

    def act(out, in_, func, reads, writes, bias=None, scale=None, accum=None):
        kw = {}
        if bias is not None:
            kw["bias"] = bias
        if scale is not None:
            kw["scale"] = scale
        if accum is not None:
            kw["accum_out"] = accum
        P.op("act", lambda e: e.activation(out=out, in_=in_, func=func, **kw), reads, writes)

    def tt(eng, out, in0, in1, op, reads, writes):
        P.op(eng, lambda e: e.tensor_tensor(out=out, in0=in0, in1=in1, op=op), reads, writes)

    def ts(eng, out, in0, s1, op0, reads, writes, s2=None, op1=None):
        if op1 is None:
            P.op(eng, lambda e: e.tensor_scalar(out=out, in0=in0, scalar1=s1, scalar2=None, op0=op0), reads, writes)
        else:
            P.op(eng, lambda e: e.tensor_scalar(out=out, in0=in0, scalar1=s1, scalar2=s2, op0=op0, op1=op1), reads, writes)

    def stt(eng, out, in0, scalar, in1, op0, op1, reads, writes, accum=None):
        if accum is None:
            P.op(eng, lambda e: e.scalar_tensor_tensor(out=out, in0=in0, scalar=scalar, in1=in1, op0=op0, op1=op1), reads, writes)
        else:
            P.op(eng, lambda e: e.scalar_tensor_tensor(out=out, in0=in0, scalar=scalar, in1=in1, op0=op0, op1=op1, accum_out=accum), reads, writes)

    def cp(eng, out, in_, reads, writes):
        P.op(eng, lambda e: e.tensor_copy(out=out, in_=in_), reads, writes)

    def memset(eng, ap, val, writes):
        P.op(eng, lambda e: e.memset(ap, val), writes=writes)

    def dma(out, in_, reads, writes, dsem, eng="sp"):
        P.op(eng, lambda e: e.dma_start(out=out, in_=in_), reads, writes, dsem=dsem)

    def norm_T(src, srck, gi, dst, dstk, xn, xnk, ssb, ssk, junk, junkk, gfree=None):
        act(junk, src, AF.Square, [srck], [ssk], accum=ssb[:, 0:1])
        act(ssb[:, 1:2], ssb[:, 0:1], AF.Ln, [ssk, "eps_t"], [ssk + "r"], bias=eps_t[:, 0:1], scale=1.0 / D)
        act(ssb[:, 1:2], ssb[:, 1:2], AF.Exp, [ssk + "r"], [ssk + "r"], scale=-0.5)
        if gfree is None:
            ts("dve", xn, src, ssb[:, 1:2], ALU.mult, [srck, ssk + "r"], [xnk])
        else:
            stt("dve", xn, src, ssb[:, 1:2], gfree[0], ALU.mult, ALU.mult, [srck, ssk + "r", gfree[1]], [xnk])
        b = bank("aux")
        pv = ps[b][:].bitcast(BF16).rearrange("p (k t) -> p k t", k=8)
        for kc in range(8):
            def f(e, o=pv[:, kc, :], i=xn[:, 128 * kc:128 * kc + 128]):
                return e.transpose(out=o, in_=i, identity=ident_bf[:])
            P.op("pe", f, [xnk, "ident_bf"], [psk[b]])
        if gfree is None:
            g_bc = gcols[:, 8 * gi:8 * gi + 8].unsqueeze(2).to_broadcast([128, 8, 128])
            tt("dve", dst, pv, g_bc, ALU.mult, [psk[b], "gcols"], [dstk])
        else:
            act(dst, pv, AF.Copy, [psk[b]], [dstk])

    ykeys = []
    for b in range(nseq):
        P.epoch = b
        sb.reset(P_END)
        if b > 0:
            bar()
        nT = sb.alloc("nT", [128, 8, T], BF16)
        GT = sb.alloc("GT", [56, T], BF16)
        cposTok = sb.alloc("cposTok", [128, NT, 8], F32)
        Wc1 = sb.alloc("Wc1", [128, 32, 128], BF16)
        Wc2 = sb.alloc("Wc2", [128, 128], BF16)
        peT = sb.alloc("peT", [128, 32], BF16)
        validT = sb.alloc("validT", [128, T], BF16)
        keep = sb.alloc("keep", [128, NT, 32], F32)
        forced = sb.alloc("forced", [128, NT, 32], F32)
        ovz = sb.alloc("ovz", [128, 40], BF16)
        sel = sb.alloc("sel", [128, 24, 64], BF16)
        AD_END = sb.mark()

        for t_ in range(NT):
            xb = xbuf[t_ % 2]
            xk = "xbuf%d" % (t_ % 2)
            dma(xb[:], x_d[b, 128 * t_:128 * t_ + 128, :], [], [xk], "d_" + xk)
            norm_T(xb[:], xk, 0, nT[:, :, 128 * t_:128 * t_ + 128], "nT%d" % t_,
                   xn[t_ % 2][:], "xn%d" % (t_ % 2), ssb[t_ % 2], "ss%d" % (t_ % 2), junk[:], "junk")
        nTk = lambda c: ["nT%d" % (4 * c + i) for i in range(4)]
        if dbg and b == 0:
            dma(dbg_d["nT"], nT[:], ["nT%d" % i for i in range(NT)], ["dbg_nT"], "d_dbg_nT")

        dma(Wc1[0:64, :, :], wbf["c1"][0:2048, :].rearrange("(l d) h -> d l h", d=64), wkeys("c1"), ["Wc1a"], "d_Wc1a")
        dma(Wc1[64:128, :, :], wbf["c1"][2048:4096, :].rearrange("(l d) h -> d l h", d=64), wkeys("c1"), ["Wc1b"], "d_Wc1b")
        cload(Wc2[:], wc2_d, "Wc2", "pool")
        cload(peT[:], peT_d, "peT", "pool")
        cload(validT[:], cd["validT"], "validT", "pool")
        cload(keep[:], cd["keep"], "keep")
        cload(forced[:], cd["forced"], "forced")
        cload(ovz[:], cd["ovz"], "ovz", "pool")
        cload(sel[:], cd["sel"], "sel", "pool")
        for s in range(2 if b == 0 else 0):
            bk = bank("aux")
            rows = slice(64 * s, 64 * s + 64)
            for l in range(32):
                mm(ps[bk][:, 0:1], Wc1[rows, l, :], peT[rows, l:l + 1], l == 0, l == 31,
                   ["Wc1a", "Wc1b", "peT"], [psk[bk]])
            cp("dve", hb[:, s:s + 1], ps[bk][:, 0:1], [psk[bk]], ["hb%d" % s])
            ts("dve", nhb[:, s:s + 1], ps[bk][:, 0:1], -1.0, ALU.mult, [psk[bk]], ["nhb%d" % s])

        sb.reset(AD_END)
        sm = sb.alloc("sm", [64, T], F32)
        cw = sm
        cpos = sb.alloc("cpos", [8, T], F32)
        r1 = sm
        hmlT = sb.alloc("hmlT", [72, T], BF16)
        Gh = sb.alloc("Gh", [64, T], BF16)
        (wsm,), (wsmk,) = wload("in", 0, 8, 0, 64)
        for c in range(4):
            bk = bank("mm")
            for kc in range(8):
                mm(ps[bk][0:64, :], wsm[:, kc, 0:64], nT[:, kc, 512 * c:512 * c + 512], kc == 0, kc == 7,
                   [wsmk] + nTk(c), [psk[bk]])
            act(sm[:, 512 * c:512 * c + 512], ps[bk][0:64, :], AF.Copy, [psk[bk]], ["sm%d" % c])
        smk = ["sm%d" % c for c in range(4)]
        act(cw[0:8, :], sm[0:8, :], AF.Exp, smk + ["negb"], ["cw_f"], bias=negb[:, 0:1], scale=-1.0)
        act(cw[0:8, :], cw[0:8, :], AF.Ln, ["cw_f"], ["cw_f"], bias=1.0)
        P.op("dve", lambda e: e.tensor_tensor_scan(out=cpos[:], data0=cw[0:8, :], data1=cw[0:8, :], initial=0.0,
                                                    op0=ALU.add, op1=ALU.max), ["cw_f"], ["cpos"])
        ts("dve", hmlT[0:8, :], cpos[:], -8.0, ALU.mult, ["cpos"], ["hml0"])
        stt("dve", r1[0:8, :], cpos[:], -8.0, hmlT[0:8, :], ALU.mult, ALU.subtract, ["cpos", "hml0", "cw_f"], ["r1"])
        cp("dve", Gh[0:8, :], r1[0:8, :], ["r1"], ["midtmp"])
        cp("dve", hmlT[32:40, :], Gh[0:8, :], ["midtmp"], ["hml1"])
        tt("dve", r1[0:8, :], r1[0:8, :], Gh[0:8, :], ALU.subtract, ["r1", "midtmp"], ["r1"])
        cp("dve", hmlT[64:72, :], r1[0:8, :], ["r1"], ["hml2"])
        def emit_cposTok():
            bk = bank("aux")
            pvw = ps[bk][:, 0:128].rearrange("p (t h) -> p t h", h=8)
            for t_ in range(NT):
                def f(e, o=pvw[:, t_, :], i=cpos[0:8, 128 * t_:128 * t_ + 128]):
                    return e.transpose(out=o, in_=i, identity=ident_f[0:8, 0:8])
                P.op("pe", f, ["cpos", "ident_f"], [psk[bk]])
            cp("dve", cposTok[:], pvw, [psk[bk]], ["cposTok"])
        act(cw[32:56, :], sm[32:56, :], AF.Exp, smk, ["cw_g"], scale=-1.0)
        act(cw[32:56, :], cw[32:56, :], AF.Ln, ["cw_g"], ["cw_g"], bias=1.0)
        memset("pool", GT[:], 0.0, ["GT"])
        cp("dve", Gh[32:56, :], cw[32:56, :], ["cw_g"], ["Gh"])
        tt("dve", GT[32:56, :], cw[32:56, :], Gh[32:56, :], ALU.subtract, ["cw_g", "Gh", "GT"], ["GT"])
        cp("dve", GT[0:24, :], Gh[32:56, :], ["Gh", "GT"], ["GT"])
        if dbg and b == 0:
            dma(dbg_d["cpos"], cpos[:], ["cpos"], ["dbg_cpos"], "d_dbg_cpos")

        QTh = [sb.alloc("QTh%d" % i, [96, T], BF16) for i in range(2)]
        KTh = [sb.alloc("KTh%d" % i, [96, T], BF16) for i in range(2)]
        for i in range(2):
            memset("pool", KTh[i][64:96, :], 1.0, ["KTh_ones%d" % i])
            memset("pool", QTh[i][64:96, :], -1.0, ["QTh_neg%d" % i])
        Vaug = sb.alloc("Vaug", [128, NT, 2, 128], BF16)
        PT = [sb.alloc("PT%d" % i, [128, 512], BF16) for i in range(4)]
        rz = [sb.alloc("rz%d" % i, [64, 512], F32) for i in range(2)]
        memset("pool", Vaug[:, :, :, 64:128], 1.0, ["Vaug_ones"])
        ptr = [0]
        rzr = [0]
        acc_pool[0] = (4, 5, 6)
        for hp in range(4 if upto >= "C" else 0):
            (wc,), (wck,) = wload("in", 0, 8, 64 + 384 * hp, 384)
            for c in range(4):
                for which, dst, dk in ((0, QTh, "QT"), (1, KTh, "KT")):
                    bk = bank("mm")
                    for kc in range(8):
                        mm(ps[bk][:], wc[:, kc, 128 * which:128 * which + 128], nT[:, kc, 512 * c:512 * c + 512],
                           kc == 0, kc == 7, [wck] + nTk(c), [psk[bk]])
                    act(dst[0][0:64, 512 * c:512 * c + 512], ps[bk][0:64, :], AF.Copy, [psk[bk]], ["%s%d_0" % (dk, c)])
                    act(dst[1][0:64, 512 * c:512 * c + 512], ps[bk][64:128, :], AF.Copy, [psk[bk]], ["%s%d_1" % (dk, c)])
                bk = bank("mm")
                pv4 = ps[bk][:].rearrange("p (a e d) -> p a e d", a=4, e=2)
                for a in range(4):
                    t_ = 4 * c + a
                    for kc in range(8):
                        mm(ps[bk][:, 128 * a:128 * a + 128], nT[:, kc, 128 * t_:128 * t_ + 128], wc[:, kc, 256:384],
                           kc == 0, kc == 7, [wck, "nT%d" % t_], [psk[bk]])
                cp("dve", Vaug[:, 4 * c:4 * c + 4, :, 0:64], pv4, [psk[bk]], ["Vaug%d" % c])
            for e_ in range(2):
                h = 2 * hp + e_
                for r in range(3):
                    dma(QTh[e_][64 + r:65 + r, :], hmlT[32 * r + h:32 * r + h + 1, :], ["hml%d" % r, "QTh_neg%d" % e_], ["C3_%d" % e_], "d_C3_%d" % e_)
                    dma(KTh[e_][67 + r:68 + r, :], hmlT[32 * r + h:32 * r + h + 1, :], ["hml%d" % r, "KTh_ones%d" % e_], ["C3k_%d" % e_], "d_C3k_%d" % e_)
            steps = []
            units = {}
            for I in range(4):
                njs = 4 * I + 4
                for j in range(njs):
                    for e_ in range(2):
                        unit = units.setdefault((e_, I), {})
                        def front(st={}, e_=e_, I=I, j=j, hp=hp):
                            h = 2 * hp + e_
                            pb = 64 * e_
                            c0 = max(0, 128 * (j - 4 * I))
                            sbk = bank("mm")
                            q0 = 512 * I + c0
                            mm(ps[sbk][:, c0:512], KTh[e_][0:70, 128 * j:128 * j + 128], QTh[e_][0:70, q0:512 * I + 512],
                               True, True, ["KT%d_%d" % (j // 4, e_), "QT%d_%d" % (I, e_), "KTh_ones%d" % e_, "QTh_neg%d" % e_,
                                            "C3_%d" % e_, "C3k_%d" % e_], [psk[sbk]])
                            pt = PT[ptr[0] % 4]
                            ptk = "PT%d" % (ptr[0] % 4)
                            ptr[0] += 1
                            act(pt[:, c0:512], ps[sbk][:, c0:512], AF.Exp, [psk[sbk]], [ptk], scale=0.125)
                            if j >= 4 * I:
                                tt("pool", pt[:, c0:c0 + 128], pt[:, c0:c0 + 128], tri[:], ALU.mult, [ptk, "tri"], [ptk])
                            st["pt"], st["ptk"], st["c0"] = pt, ptk, c0

                        def back(st=front.__defaults__[0], unit=unit, e_=e_, I=I, j=j, njs=njs, hp=hp):
                            pb = 64 * e_
                            if "ab" not in unit:
                                unit["ab"] = bank("acc")
                            ab = unit["ab"]
                            pt, ptk, c0 = st["pt"], st["ptk"], st["c0"]
                            mm(ps[ab][:, c0:512], Vaug[:, j, e_, :], pt[:, c0:512], j == 0, j == njs - 1,
                               ["Vaug%d" % (j // 4), "Vaug_ones", ptk], [psk[ab]])
                            if j == njs - 1:
                                rzz = rz[rzr[0] % 2]
                                rzk = "rz%d" % (rzr[0] % 2)
                                rzr[0] += 1
                                act(rzz[0:64, :], ps[ab][64:128, :], AF.Ln, [psk[ab]], [rzk])
                                act(rzz[0:64, :], rzz[0:64, :], AF.Exp, [rzk], [rzk], scale=-1.0)
                                tt("dve", mixT[pb:pb + 64, hp, 512 * I:512 * I + 512], ps[ab][0:64, :], rzz[0:64, :], ALU.mult,
                                   [psk[ab], rzk], ["mixT%d_%d_%d" % (hp, I, e_)])
                        steps.append((front, back))
            pipeline(steps)

        acc_pool[0] = (4, 5)
        if upto >= "D":
            bar()
            sb.reset(AD_END)
            Qp0 = sb.alloc("Qp", [64, 4, 512], BF16)
            QrA0 = sb.alloc("QrA", [96, 4, 512], BF16)
            if "Qp1" not in sb.cache:
                sb.cache["Qp1"] = nc.alloc_sbuf_tensor_at("Qp1", [64, 4, 512], BF16, offset=xbuf_off[0])
                sb.cache["QrA1"] = nc.alloc_sbuf_tensor_at("QrA1", [96, 4, 512], BF16, offset=xbuf_off[1])
            Qp1, QrA1 = sb.cache["Qp1"], sb.cache["QrA1"]
            Qpb = [Qp0, Qp1]
            QrAb = [QrA0, QrA1]
            mb4 = sb.alloc("mb4", [128, 4, 32], BF16)
            KVc = sb.alloc("KVc", [128, T], BF16)
            KsAug = sb.alloc("KsAug", [96, T], BF16)
            KwT = sb.alloc("KwT", [64, T], BF16)
            VsAug = sb.alloc("VsAug", [128, NT, 128], BF16)
            VwAug = sb.alloc("VwAug", [128, NT, 128], BF16)
            hid = sb.alloc("hid", [128, 2, 128], BF16)
            kccT = sb.alloc("kccT", [64, 128], BF16)
            VcAug = sb.alloc("VcAug", [128, 128], BF16)
            U = [sb.alloc("U%d" % i, [128, 512], BF16) for i in range(4)]
            PT = [sb.alloc("PTd%d" % i, [128, 512], BF16) for i in range(4)]
            ropec = [[sb.alloc("rope%d_%d" % (i, k), [128, 512], F32) for k in range(2)] for i in range(2)]
            t12 = [[sb.alloc("t12_%d_%d" % (i, k), [128, 512], F32) for k in range(2)] for i in range(1)]
            accS = sb.alloc("accS", [64, 4, 512], F32)
            rz = [sb.alloc("rzd%d" % i, [64, 512], F32) for i in range(2)]
            fg = [sb.alloc("fg%d" % i, [64, 512], F32) for i in range(2)]
            tmpc = [sb.alloc("tmpc%d" % i, [64, 512], F32) for i in range(2)]
            silu_x = sb.alloc("silu_x", [128, 128], F32)
            silu_e = sb.alloc("silu_e", [128, 128], F32)
            zt = [sb.alloc("zt%d" % i, [128, 4], F32) for i in range(2)]
            impacc = [sb.alloc("impacc%d" % i, [128, 32], F32) for i in range(2)]
            imp2 = [sb.alloc("imp2_%d" % i, [128, 32], F32) for i in range(2)]
            imp3 = [sb.alloc("imp3_%d" % i, [128, 32], F32) for i in range(2)]
            top = [sb.alloc("top%d" % i, [128, 16], F32) for i in range(2)]
            mb = [sb.alloc("mb%d" % i, [128, 32], BF16) for i in range(2)]
            cload(KsAug[64:96, :], cd["blkind"], "KsAug_ind", "pool")
            memset("pool", VsAug[:, :, 64:128], 1.0, ["VsAug_ones"])
            memset("pool", VwAug[:, :, 64:128], 1.0, ["VwAug_ones"])
            memset("pool", VcAug[:, 0:64], 0.0, ["VcAug_zero"])
            memset("pool", VcAug[:, 64:128], 1.0, ["VcAug_ones"])
            ptr = [0]
            cnt2 = [0]
            ropei = [0]
            t12i = [0]

            def load_rope(c):
                i = ropei[0] % 2
                ropei[0] += 1
                dma(ropec[i][0][:], cd["cossin"][0, :, 512 * c:512 * c + 512], [], ["ropeC%d" % i], "d_ropeC%d" % i)
                dma(ropec[i][1][:], cd["cossin"][1, :, 512 * c:512 * c + 512], [], ["ropeS%d" % i], "d_ropeS%d" % i)
                return ropec[i][0], ropec[i][1], "ropeC%d" % i, "ropeS%d" % i

            def proj_fm(w, wk, c0, c, nrows=128):
                bk = bank("mm")
                for kc in range(8):
                    mm(ps[bk][0:nrows, :], w[:, kc, c0:c0 + nrows], nT[:, kc, 512 * c:512 * c + 512], kc == 0, kc == 7,
                       [wk] + nTk(c), [psk[bk]])
                return bk

            def rope_pair(bA, bB, cosT, sinT, ck, sk_):
                i = 0
                t1, t2 = t12[i]
                tt("dve", t1[:], ps[bA][:], cosT[:], ALU.mult, [psk[bA], ck], ["t1_%d" % i])
                tt("dve", t2[:], ps[bB][:], sinT[:], ALU.mult, [psk[bB], sk_], ["t2_%d" % i])
                return t1, t2, ["t1_%d" % i, "t2_%d" % i]

            for g in range(2):
                base_g = 64 + 1536 + 1024 * g
                (wq,), (wqk,) = wload("in", 0, 8, base_g, 512)
                (wkv,), (wkvk,) = wload("in", 0, 8, base_g + 512, 512)
                for c in range(4):
                    cosT, sinT, ck, sk_ = load_rope(c)
                    bk = proj_fm(wkv, wkvk, 0, c)
                    act(KVc[:, 512 * c:512 * c + 512], ps[bk][:], AF.Copy, [psk[bk]], ["KVc%d" % c])
                    bA = proj_fm(wkv, wkvk, 128, c)
                    bB = proj_fm(wkv, wkvk, 256, c)
                    t1, t2, tk = rope_pair(bA, bB, cosT, sinT, ck, sk_)
                    tt("pool", KsAug[0:64, 512 * c:512 * c + 512], t1[0:64, :], t2[0:64, :], ALU.add, tk, ["KsAug%d" % c])
                    tt("pool", KwT[0:64, 512 * c:512 * c + 512], t1[64:128, :], t2[64:128, :], ALU.add, tk, ["KwT%d" % c])
                    bk = bank("mm")
                    pv4 = ps[bk][:].rearrange("p (a d) -> p a d", a=4)
                    for a in range(4):
                        t_ = 4 * c + a
                        for kc in range(8):
                            mm(ps[bk][:, 128 * a:128 * a + 128], nT[:, kc, 128 * t_:128 * t_ + 128], wkv[:, kc, 384:512],
                               kc == 0, kc == 7, [wkvk, "nT%d" % t_], [psk[bk]])
                    cp("dve", VsAug[:, 4 * c:4 * c + 4, 0:64], pv4[:, :, 0:64], [psk[bk]], ["VsAug%d" % c])
                    cp("dve", VwAug[:, 4 * c:4 * c + 4, 0:64], pv4[:, :, 64:128], [psk[bk]], ["VwAug%d" % c])
                KVck = ["KVc%d" % c for c in range(4)]
                if DSTOP <= 1:
                    continue
                for s in range(2):
                    rows = slice(64 * s, 64 * s + 64)
                    bk = bank("mm")
                    for l in range(32):
                        mm(ps[bk][:, 0:127], Wc1[rows, l, :], KVc[rows, l:l + 16 * 126 + 1:16], l == 0, l == 31,
                           ["Wc1a", "Wc1b"] + KVck, [psk[bk]])
                    act(silu_e[:, 0:127], ps[bk][:, 0:127], AF.Exp, [psk[bk], "nhb%d" % s], ["silu_e"],
                        bias=nhb[:, s:s + 1], scale=-1.0)
                    ts("dve", silu_x[:, 0:127], ps[bk][:, 0:127], hb[:, s:s + 1], ALU.add, [psk[bk], "hb%d" % s], ["silu_x"])
                    ts("dve", silu_e[:, 0:127], silu_e[:, 0:127], 1.0, ALU.add, ["silu_e"], ["silu_e"])
                    P.op("dve", lambda e: e.reciprocal(out=silu_e[:, 0:127], in_=silu_e[:, 0:127]), ["silu_e"], ["silu_e"])
                    tt("dve", hid[:, s, 0:127], silu_x[:, 0:127], silu_e[:, 0:127], ALU.mult, ["silu_x", "silu_e"], ["hid%d" % s])
                bk = bank("mm")
                mm(ps[bk][0:64, 0:127], Wc2[:, 0:64], hid[:, 0, 0:127], True, True, ["Wc2", "hid0"], [psk[bk]])
                act(kccT[0:64, 0:127], ps[bk][0:64, 0:127], AF.Copy, [psk[bk]], ["kccT"])
                bk = bank("mm")
                mm(ps[bk][0:127, 0:64], hid[:, 1, 0:127], Wc2[:, 64:128], True, True, ["Wc2", "hid1"], [psk[bk]])
                cp("dve", VcAug[0:127, 0:64], ps[bk][0:127, 0:64], [psk[bk], "VcAug_zero"], ["VcAug"])

                if DSTOP <= 2:
                    continue
                def qproj(I):
                    par = I % 2
                    Qp, QrA = Qpb[par], QrAb[par]
                    cosT, sinT, ck, sk_ = load_rope(I)
                    for mt in range(2):
                        bA = proj_fm(wq, wqk, 128 * mt, I)
                        bB = proj_fm(wq, wqk, 256 + 128 * mt, I)
                        act(Qp[0:64, 2 * mt, :], ps[bA][0:64, :], AF.Copy, [psk[bA]], ["Qp%d_%d" % (2 * mt, par)])
                        act(Qp[0:64, 2 * mt + 1, :], ps[bA][64:128, :], AF.Copy, [psk[bA]], ["Qp%d_%d" % (2 * mt + 1, par)])
                        t1, t2, tk = rope_pair(bA, bB, cosT, sinT, ck, sk_)
                        tt("dve", QrA[0:64, 2 * mt, :], t1[0:64, :], t2[0:64, :], ALU.add, tk, ["Qr%d_%d" % (2 * mt, par)])
                        tt("dve", QrA[0:64, 2 * mt + 1, :], t1[64:128, :], t2[64:128, :], ALU.add, tk, ["Qr%d_%d" % (2 * mt + 1, par)])

                if DSTOP <= 2:
                    continue
                qproj(0)
                for I in range(4):
                    par = I % 2
                    Qp, QrA = Qpb[par], QrAb[par]

                    def combine(ab, r, br, first, last, I=I, g=g):
                        h = 4 * g + r
                        i = cnt2[0] % 2
                        cnt2[0] += 1
                        if only_br is not None and br != only_br:
                            return
                        act(rz[i][:], ps[ab][64:128, :], AF.Ln, [psk[ab], "tiny_t"], ["rzd%d" % i], bias=tiny_t[0:64, 0:1])
                        dst = mixT[64 * (h % 2):64 * (h % 2) + 64, 4 + h // 2, 512 * I:512 * I + 512]
                        dk = "mixT%d_%d_%d" % (4 + h // 2, I, h % 2)
                        if only_br is not None:
                            act(rz[i][:], rz[i][:], AF.Exp, ["rzd%d" % i], ["rzd%d" % i], scale=-1.0)
                            tt("dve", dst, ps[ab][0:64, :], rz[i][:], ALU.mult, [psk[ab], "rzd%d" % i], [dk])
                            return
                        mm(ps[6][0:64, :], sel[0:56, 3 * h + br, :], GT[0:56, 512 * I:512 * I + 512], True, True,
                           ["sel", "GT"], [psk[6]])
                        tt("dve", rz[i][:], ps[6][0:64, :], rz[i][:], ALU.add, [psk[6], "rzd%d" % i], ["rzd%d" % i])
                        act(fg[i][:], rz[i][:], AF.Exp, ["rzd%d" % i], ["fg%d" % i], scale=-1.0)
                        fgi, fgk = fg[i], "fg%d" % i
                        if first:
                            tt("dve", accS[:, r, :], ps[ab][0:64, :], fgi[:], ALU.mult, [psk[ab], fgk], ["accS%d" % r])
                        elif not last:
                            tt("dve", tmpc[i][:], ps[ab][0:64, :], fgi[:], ALU.mult, [psk[ab], fgk], ["tmpc%d" % i])
                            tt("pool", accS[:, r, :], accS[:, r, :], tmpc[i][:], ALU.add, ["accS%d" % r, "tmpc%d" % i], ["accS%d" % r])
                        else:
                            tt("dve", tmpc[i][:], ps[ab][0:64, :], fgi[:], ALU.mult, [psk[ab], fgk], ["tmpc%d" % i])
                            tt("dve", dst, accS[:, r, :], tmpc[i][:], ALU.add, ["accS%d" % r, "tmpc%d" % i], [dk])

                    if DSTOP <= 3:
                        continue
                    impb = {}

                    def emit_imp(half, I=I):
                        ib = 7
                        impb[half] = ib
                        iv4 = ps[ib][:].rearrange("p (a r c) -> p a r c", a=2, r=4)
                        for r in range(4):
                            for a2 in range(2):
                                a = 2 * half + a2
                                mm(iv4[:, a2, r, 0:33], U[r][0:127, 128 * a:128 * a + 128], ovz[0:127, 0:33], True, True,
                                   ["U%d" % r, "ovz"], [psk[ib]])

                    def emit_topk(a, I=I):
                        half, a2 = a // 2, a % 2
                        ib = impb[half]
                        iv = ps[ib][:].rearrange("p (a r c) -> p a r c", a=2, r=4)[:, a2, :, :]
                        t_ = 4 * I + a
                        i = a % 2
                        ts("dve", zt[i][:], iv[:, :, 32], 1e-30, ALU.max, [psk[ib]], ["zt%d" % i])
                        P.op("dve", lambda e, o=zt[i][:]: e.reciprocal(out=o, in_=o), ["zt%d" % i], ["zt%d" % i])
                        ts("dve", impacc[i][:], iv[:, 0, 0:32], zt[i][:, 0:1], ALU.mult, [psk[ib], "zt%d" % i], ["impacc%d" % i])
                        for r in range(1, 4):
                            stt("dve", impacc[i][:], iv[:, r, 0:32], zt[i][:, r:r + 1], impacc[i][:], ALU.mult, ALU.add,
                                [psk[ib], "zt%d" % i, "impacc%d" % i], ["impacc%d" % i])
                        tt("pool", imp2[i][:], impacc[i][:], keep[:, t_, :], ALU.mult, ["impacc%d" % i, "keep"], ["imp2_%d" % i])
                        tt("pool", imp2[i][:], imp2[i][:], forced[:, t_, :], ALU.add, ["imp2_%d" % i, "forced"], ["imp2_%d" % i])
                        P.op("dve", lambda e, o=top[i][:, 0:8], x_=imp2[i][:]: e.max(out=o, in_=x_), ["imp2_%d" % i], ["topa%d" % i])
                        P.op("dve", lambda e, o=imp3[i][:], a_=top[i][:, 0:8], x_=imp2[i][:]:
                             e.match_replace(out=o, in_to_replace=a_, in_values=x_, imm_value=-1e30),
                             ["imp2_%d" % i, "topa%d" % i], ["imp3_%d" % i])
                        P.op("dve", lambda e, o=top[i][:, 8:16], x_=imp3[i][:]: e.max(out=o, in_=x_), ["imp3_%d" % i], ["topb%d" % i])
                        ts("dve", mb4[:, a, :], imp2[i][:], top[i][:, 15:16], ALU.is_lt, ["imp2_%d" % i, "topb%d" % i], ["mb4_%d" % a],
                           s2=NEGB, op1=ALU.mult)

                    steps = []
                    for r in range(4):
                        def front(st={}, r=r, I=I, par=par, Qp=Qp):
                            sbk = bank("mm")
                            mm(ps[sbk][0:127, :], kccT[0:64, 0:127], Qp[0:64, r, :], True, True, ["kccT", "Qp%d_%d" % (r, par)], [psk[sbk]])
                            act(U[r][0:127, :], ps[sbk][0:127, :], AF.Exp, [psk[sbk]], ["U%d" % r], scale=0.125)
                            tt("pool", U[r][0:127, :], U[r][0:127, :], validT[0:127, 512 * I:512 * I + 512], ALU.mult,
                               ["U%d" % r, "validT"], ["U%d" % r])

                        def back(r=r, I=I):
                            ab = bank("acc")
                            mm(ps[ab][:, :], VcAug[0:127, :], U[r][0:127, :], True, True, ["VcAug", "VcAug_ones", "U%d" % r], [psk[ab]])
                            combine(ab, r, 0, True, False)
                            if r == 3 and I >= 2 and DSTOP > 4:
                                emit_imp(0)
                        steps.append((front, back))
                    if DSTOP > 6:
                      for r in range(4):
                        unit = {}
                        jlo = max(0, 4 * I - 4)
                        first_j = 4 * I - 1 if I > 0 else 0
                        js = [first_j] + [j for j in range(jlo, 4 * I + 4) if j != first_j]
                        for n_, j in enumerate(js):
                            def front(st={}, r=r, I=I, j=j, par=par, QrA=QrA):
                                qlo = max(j, 4 * I)
                                qhi = min(j + 4, 4 * I + 3)
                                ca = 128 * (qlo - 4 * I)
                                cb = 128 * (qhi - 4 * I + 1)
                                sbk = bank("mm")
                                mm(ps[sbk][:, ca:cb], KwT[0:64, 128 * j:128 * j + 128], QrA[0:64, r, ca:cb], True, True,
                                   ["KwT%d" % (j // 4), "Qr%d_%d" % (r, par)], [psk[sbk]])
                                pt = PT[ptr[0] % 4]
                                ptk = "PTd%d" % (ptr[0] % 4)
                                ptr[0] += 1
                                act(pt[:, ca:cb], ps[sbk][:, ca:cb], AF.Exp, [psk[sbk]], [ptk], scale=0.125)
                                if qlo == j:
                                    tt("pool", pt[:, ca:ca + 128], pt[:, ca:ca + 128], tri[:], ALU.mult, [ptk, "tri"], [ptk])
                                if qhi == j + 4:
                                    tt("pool", pt[:, cb - 128:cb], pt[:, cb - 128:cb], atri[:], ALU.mult, [ptk, "atri"], [ptk])
                                st["pt"], st["ptk"], st["ca"], st["cb"] = pt, ptk, ca, cb

                            def back(st=front.__defaults__[0], unit=unit, r=r, j=j, n_=n_, nj=len(js), I=I):
                                if "ab" not in unit:
                                    unit["ab"] = bank("acc")
                                ab = unit["ab"]
                                pt, ptk, ca, cb = st["pt"], st["ptk"], st["ca"], st["cb"]
                                mm(ps[ab][:, ca:cb], VwAug[:, j, :], pt[:, ca:cb], n_ == 0, n_ == nj - 1,
                                   ["VwAug%d" % (j // 4), "VwAug_ones", ptk], [psk[ab]])
                                if n_ == nj - 1:
                                    combine(ab, r, 2, False, False)
                                    if I >= 2 and DSTOP > 4:
                                        if r == 2:
                                            emit_imp(1)
                                        for a in {0: (0,), 1: (1,), 2: (2, 3), 3: ()}[r]:
                                            emit_topk(a)
                            steps.append((front, back))
                    pipeline(steps)
                    if I < 3:
                        qproj(I + 1)
                    if I >= 2:
                        mbv = ps[6][:].bitcast(BF16)
                        for a in range(4):
                            P.op("pe", lambda e, o=mbv[0:32, 128 * a:128 * a + 128], x_=mb4[:, a, :]:
                                 e.transpose(out=o, in_=x_, identity=ident_bf[:]), ["mb4_%d" % a, "ident_bf"], [psk[6]])
                        for r in range(4):
                            cp("dve", QrA[64:96, r, :], mbv[0:32, 0:512], [psk[6]], ["QrM%d_%d" % (r, par)])
                        if dbg and b == 0:
                            dma(dbg_d["mb"][g, I], QrA[64:96, 0, :], ["QrM0_%d" % par], ["dbg_mb%d%d" % (g, I)], "d_dbg_mb")
                    else:
                        for r in range(4):
                            memset("pool", QrA[64:96, r, :], 0.0, ["QrM%d_%d" % (r, par)])
                    if DSTOP <= 5:
                        continue
                    steps = []
                    for r in range(4):
                        unit = {}
                        njs = 4 * I + 4
                        for j in range(njs):
                            def front(st={}, r=r, I=I, j=j, par=par, QrA=QrA):
                                c0 = max(0, 128 * (j - 4 * I))
                                sbk = bank("mm")
                                mm(ps[sbk][:, c0:512], KsAug[0:96, 128 * j:128 * j + 128], QrA[0:96, r, c0:512], True, True,
                                   ["KsAug%d" % (j // 4), "KsAug_ind", "Qr%d_%d" % (r, par), "QrM%d_%d" % (r, par)], [psk[sbk]])
                                pt = PT[ptr[0] % 4]
                                ptk = "PTd%d" % (ptr[0] % 4)
                                ptr[0] += 1
                                act(pt[:, c0:512], ps[sbk][:, c0:512], AF.Exp, [psk[sbk]], [ptk], scale=0.125)
                                if j >= 4 * I:
                                    tt("pool", pt[:, c0:c0 + 128], pt[:, c0:c0 + 128], tri[:], ALU.mult, [ptk, "tri"], [ptk])
                                st["pt"], st["ptk"], st["c0"] = pt, ptk, c0

                            def back(st=front.__defaults__[0], unit=unit, r=r, j=j, njs=njs):
                                if "ab" not in unit:
                                    unit["ab"] = bank("acc")
                                ab = unit["ab"]
                                pt, ptk, c0 = st["pt"], st["ptk"], st["c0"]
                                mm(ps[ab][:, c0:512], VsAug[:, j, :], pt[:, c0:512], j == 0, j == njs - 1,
                                   ["VsAug%d" % (j // 4), "VsAug_ones", ptk], [psk[ab]])
                                if j == njs - 1:
                                    combine(ab, r, 1, False, True)
                            steps.append((front, back))
                    pipeline(steps)

        if upto >= "E":
            bar()
            sb.reset(P_END)
            KmT = sb.alloc("KmT", [128, 8, 256], BF16)
            Vm = sb.alloc("Vm", [128, 2, D], BF16)
            gbc = sb.alloc("gbc", [128, 3, D], F32)
            mT = sb.alloc("mT", [128, 8, 256], BF16)
            hbuf = sb.alloc("hbuf", [128, 4, D], F32)
            nT2 = sb.alloc("nT2", [128, 8, 512], BF16)
            hT = sb.alloc("hT", [128, 32, 512], BF16)
            PTe = [sb.alloc("PTe%d" % i, [128, 512], BF16) for i in range(4)]
            rze = [sb.alloc("rze%d" % i, [128, 512], F32) for i in range(2)]
            tmpe = [sb.alloc("tmpe%d" % i, [128, D], F32) for i in range(2)]
            relu_t = [sb.alloc("relu%d" % i, [128, 512], F32) for i in range(2)]
            ssq = [sb.alloc("ssq%d" % i, [128, 4], F32) for i in range(2)]
            gpre = sb.alloc("gpre", [128, 2, D], F32)
            npi = [0]
            nti = [0]

            def norm_T2(src, srck, gi, dst, dstk):
                i = nti[0] % 2
                nti[0] += 1
                gf = None
                norm_T(src, srck, gi, dst, dstk, xn[i][:], "xn%d" % i, ssb[i], "ss%d" % i, junk[:], "junk", gfree=gf)

            def norm_post(b0, b1, gi, resid, residk, out, outk, srcs=None):
                i = npi[0] % 2
                npi[0] += 1
                sq = ssq[i]
                sk = "ssq%d" % i
                if srcs is None:
                    s0, s0k, s1, s1k = ps[b0][:], psk[b0], ps[b1][:], psk[b1]
                else:
                    s0, s0k, s1, s1k = srcs
                act(junk[:, 0:512], s0, AF.Square, [s0k], [sk + "a"], accum=sq[:, 0:1])
                act(junk[:, 512:1024], s1, AF.Square, [s1k], [sk + "b"], accum=sq[:, 1:2])
                tt("dve", sq[:, 2:3], sq[:, 0:1], sq[:, 1:2], ALU.add, [sk + "a", sk + "b"], [sk + "c"])
                act(sq[:, 3:4], sq[:, 2:3], AF.Ln, [sk + "c", "eps_t"], [sk + "r"], bias=eps_t[:, 0:1], scale=1.0 / D)
                act(sq[:, 3:4], sq[:, 3:4], AF.Exp, [sk + "r"], [sk + "r"], scale=-0.5)
                tk = "tmpe%d" % i
                stt("dve", tmpe[i][:, 0:512], s0, sq[:, 3:4], gbc[:, gi, 0:512], ALU.mult, ALU.mult,
                    [s0k, sk + "r", "gbc%d" % gi], [tk + "a"])
                stt("dve", tmpe[i][:, 512:1024], s1, sq[:, 3:4], gbc[:, gi, 512:1024], ALU.mult, ALU.mult,
                    [s1k, sk + "r", "gbc%d" % gi], [tk + "b"])
                tt("dve", out, tmpe[i][:], resid, ALU.add, [tk + "a", tk + "b", residk], [outk])

            for i in range(3):
                dma(gbc[:, i, :], gbc_d[i], [], ["gbc%d" % i], "d_gbc%d" % i)
            for i in range(2):
                dma(gpre[:, i, :], gpre_d[i], [], ["gpre%d" % i], "d_gpre%d" % i)
            for m_ in range(2):
                xb = xbuf[m_ % 2]
                xk = "xbuf%d" % (m_ % 2)
                dma(xb[:], mem_d[b, 128 * m_:128 * m_ + 128, :], [], [xk], "d_" + xk)
                norm_T2(xb[:], xk, 2, mT[:, :, 128 * m_:128 * m_ + 128], "mT%d" % m_)
            mTk = ["mT0", "mT1"]
            for half in range(2):
                (w,), (wk,) = wload("xkv", 0, 8, 512 * half, 512)
                for m4 in range(4):
                    bk = bank("mm")
                    for kc in range(8):
                        mm(ps[bk][:, 0:256], w[:, kc, 128 * m4:128 * m4 + 128], mT[:, kc, :], kc == 0, kc == 7, [wk] + mTk, [psk[bk]])
                    act(KmT[:, 4 * half + m4, :], ps[bk][:, 0:256], AF.Copy, [psk[bk]], ["KmT%d" % (4 * half + m4)])
            for half in range(2):
                (w,), (wk,) = wload("xkv", 0, 8, 1024 + 512 * half, 512)
                for kb in range(2):
                    bk = bank("mm")
                    for kc in range(8):
                        mm(ps[bk][:], mT[:, kc, 128 * kb:128 * kb + 128], w[:, kc, :], kc == 0, kc == 7, [wk, "mT%d" % kb], [psk[bk]])
                    act(Vm[:, kb, 512 * half:512 * half + 512], ps[bk][:], AF.Copy, [psk[bk]], ["Vm%d_%d" % (kb, half)])
            Vmk = ["Vm%d_%d" % (kb, half) for kb in range(2) for half in range(2)]

            pte = [0]
            for ci in range(4):
                mixk = [k for k in P.allkeys if isinstance(k, str) and k.startswith("mixT") and k.split("_")[1] == str(ci)]
                (w0,), (w0k,) = wload("out", 0, 8, 0, 512)
                (w1,), (w1k,) = wload("out", 0, 8, 512, 512)
                for a in range(4):
                    t_ = 4 * ci + a
                    xb = xbuf[a % 2]
                    xk = "xbuf%d" % (a % 2)
                    dma(xb[:], x_d[b, 128 * t_:128 * t_ + 128, :], [], [xk], "d_" + xk)
                    bks = []
                    for w, wk in ((w0, w0k), (w1, w1k)):
                        bk = bank("mm4")
                        for kc in range(8):
                            mm(ps[bk][:], mixT[:, kc, 128 * t_:128 * t_ + 128], w[:, kc, :], kc == 0, kc == 7,
                               [wk] + [k for k in mixk if k.startswith("mixT%d_" % kc)], [psk[bk]])
                        bks.append(bk)
                    norm_post(bks[0], bks[1], 0, xb[:], xk, hbuf[:, a, :], "h%d" % a)
                for a in range(4):
                    norm_T2(hbuf[:, a, :], "h%d" % a, 1, nT2[:, :, 128 * a:128 * a + 128], "nT2_%d" % a)
                nT2k = ["nT2_%d" % a for a in range(4)]
                for half in range(2):
                    (w,), (wk,) = wload("xq", 0, 8, 512 * half, 512)
                    for m4 in range(4):
                        mt = 4 * half + m4
                        bk = bank("mm")
                        for kc in range(8):
                            mm(ps[bk][:], w[:, kc, 128 * m4:128 * m4 + 128], nT2[:, kc, :], kc == 0, kc == 7, [wk] + nT2k, [psk[bk]])
                        act(hT[:, mt, :], ps[bk][:], AF.Copy, [psk[bk]], ["hT%d" % mt])
                for hh in range(4):
                    pts = []
                    for kb in range(2):
                        sbk = bank("mm")
                        for dc in range(2):
                            mm(ps[sbk][:], KmT[:, 2 * hh + dc, 128 * kb:128 * kb + 128], hT[:, 2 * hh + dc, :], dc == 0, dc == 1,
                               ["KmT%d" % (2 * hh + dc), "hT%d" % (2 * hh + dc)], [psk[sbk]])
                        pt = PTe[pte[0] % 4]
                        ptk = "PTe%d" % (pte[0] % 4)
                        pte[0] += 1
                        act(pt[:], ps[sbk][:], AF.Exp, [psk[sbk]], [ptk], scale=1.0 / 16.0)
                        pts.append((pt, ptk))
                    obs = [bank("acc"), bank("acc")]
                    zb = 6 + (hh % 2)
                    for mo in range(2):
                        for kb in range(2):
                            mm(ps[obs[mo]][:], Vm[:, kb, 256 * hh + 128 * mo:256 * hh + 128 * mo + 128], pts[kb][0][:], kb == 0, kb == 1,
                               Vmk + [pts[kb][1]], [psk[obs[mo]]])
                    for kb in range(2):
                        mm(ps[zb][:], ones_bf[:, :], pts[kb][0][:], kb == 0, kb == 1, ["ones_bf", pts[kb][1]], [psk[zb]])
                    rzi = rze[hh % 2]
                    rzk = "rze%d" % (hh % 2)
                    act(rzi[:], ps[zb][:], AF.Ln, [psk[zb]], [rzk])
                    act(rzi[:], rzi[:], AF.Exp, [rzk], [rzk], scale=-1.0)
                    for mo in range(2):
                        tt("dve", hT[:, 8 + 2 * hh + mo, :], ps[obs[mo]][:], rzi[:], ALU.mult, [psk[obs[mo]], rzk], ["hT%d" % (8 + 2 * hh + mo)])
                (w0,), (w0k,) = wload("xo", 0, 8, 0, 512)
                (w1,), (w1k,) = wload("xo", 0, 8, 512, 512)
                for a in range(4):
                    bks = []
                    for w, wk in ((w0, w0k), (w1, w1k)):
                        bk = bank("mm4")
                        for kc in range(8):
                            mm(ps[bk][:], hT[:, 8 + kc, 128 * a:128 * a + 128], w[:, kc, :], kc == 0, kc == 7,
                               [wk, "hT%d" % (8 + kc)], [psk[bk]])
                        bks.append(bk)
                    norm_post(bks[0], bks[1], 1, hbuf[:, a, :], "h%d" % a, hbuf[:, a, :], "h%d" % a)
                for a in range(4):
                    norm_T2(hbuf[:, a, :], "h%d" % a, 3, nT2[:, :, 128 * a:128 * a + 128], "nT2_%d" % a)
                for hc in range(8):
                    (w,), (wk,) = wload("up", 0, 8, 512 * hc, 512)
                    for m4 in range(4):
                        m_ = 4 * hc + m4
                        bk = bank("mm")
                        for kc in range(8):
                            mm(ps[bk][:], w[:, kc, 128 * m4:128 * m4 + 128], nT2[:, kc, :], kc == 0, kc == 7, [wk] + nT2k, [psk[bk]])
                        ri = m_ % 2
                        act(relu_t[ri][:], ps[bk][:], AF.Relu, [psk[bk]], ["relu%d" % ri])
                        tt("pool", hT[:, m_, :], relu_t[ri][:], relu_t[ri][:], ALU.mult, ["relu%d" % ri], ["hT%d" % m_])
                for hkg in range(4):
                    (w0, w1), (w0k, w1k) = wload("down", 8 * hkg, 8, 0, 1024)
                    for a in range(4):
                        for half, (w, wk) in enumerate(((w0, w0k), (w1, w1k))):
                            bk = 2 * a + half
                            for k8 in range(8):
                                hk = 8 * hkg + k8
                                mm(ps[bk][:], hT[:, hk, 128 * a:128 * a + 128], w[:, k8, :], hkg == 0 and k8 == 0, hkg == 3 and k8 == 7,
                                   [wk, "hT%d" % hk], [psk[bk]])
                for a in range(4):
                    t_ = 4 * ci + a
                    norm_post(2 * a, 2 * a + 1, 2, hbuf[:, a, :], "h%d" % a, hbuf[:, a, :], "h%d" % a)
                    yk = "y%d_%d" % (b, t_)
                    dma(y_d[b, 128 * t_:128 * t_ + 128, :], hbuf[:, a, :], ["h%d" % a], [yk], "d_y%d" % a)
                    ykeys.append(yk)

    P.op("sp", None, reads=ykeys)
    if dbg:
        mk = [k for k in P.allkeys if str(k).startswith("mixT")]
        dma(dbg_d["mixT"], mixT[:], mk, ["dbg_mixT"], "d_dbg_mixT")
        P.op("sp", None, reads=["dbg_mixT", "dbg_nT", "dbg_cpos"])
    P.finish()
    return nc, P


def prep_shared(inputs):
    f = lambda k: np.ascontiguousarray(np.asarray(inputs[k], np.float32)[0])
    w_in = f("w_in")
    idx = w_in_index()
    w_ext = np.zeros((D, NCOLS), np.float32)
    m = idx >= 0
    w_ext[:, m] = w_in[:, idx[m]]
    sh = {
        "w_in_ext": w_ext,
        "w_out": f("w_mix_out"), "w_xq": f("w_xq"), "w_xkv": f("w_xkv"), "w_xo": f("w_xo"),
        "w_up": f("w_up"), "w_down": f("w_down"),
        "w_c1": np.ascontiguousarray(np.concatenate([f("w_ck1"), f("w_cv1")], 0)),
        "w_c2": np.ascontiguousarray(np.concatenate([f("w_ck2"), f("w_cv2")], 1)),
        "peT": np.ascontiguousarray(np.concatenate([f("pe_k").T, f("pe_v").T], 0)),
        "b_forget": np.ascontiguousarray(f("b_forget").reshape(8, 1)),
    }
    gc = np.zeros((128, 32), np.float32)
    for i, k in enumerate(("g_mix_pre", "g_x_pre", "g_mem", "g_mlp_pre")):
        gc[:, 8 * i:8 * i + 8] = f(k).reshape(8, 128).T
    sh["gcols"] = gc
    sh["gbc"] = np.ascontiguousarray(np.stack([np.broadcast_to(f(k)[None, :], (128, D))
                                               for k in ("g_mix_post", "g_x_post", "g_mlp_post")], 0))
    sh["gpre"] = np.ascontiguousarray(np.stack([np.broadcast_to(f(k)[None, :], (128, D))
                                                for k in ("g_x_pre", "g_mlp_pre")], 0))
    for k, v in host_consts().items():
        sh["c_" + k] = np.ascontiguousarray(v)
    return sh


def kernel(**inputs):
    ncores = 8
    x = np.asarray(inputs["x"], np.float32)
    mem = np.asarray(inputs["mem"], np.float32)
    nseq = x.shape[0] // ncores
    sh = prep_shared(inputs)
    nc, P = build(nseq)
    in_maps = []
    for c in range(ncores):
        m = dict(sh)
        m["x"] = np.ascontiguousarray(x[c * nseq:(c + 1) * nseq])
        m["mem"] = np.ascontiguousarray(mem[c * nseq:(c + 1) * nseq])
        in_maps.append(m)
    res = run_bass_kernel_spmd(nc, in_maps, core_ids=list(range(ncores)))
    return np.concatenate([np.asarray(r["y"], np.float32) for r in res.results], axis=0)
```
